# Optimizing a Trainium2 kernel written in Bass

```python
import math
import jax
import jax.numpy as jnp
from jax import lax
import numpy as np

D_MODEL = 1024
BATCH = 2
SEQ = 16384
DEPTH = 4

HEAD_DIM = 64
SB_HEADS = 4
GLA_HEADS = 4
GLA_KEY_DIM = 32
GLA_VAL_DIM = 64
GLA_LOW_RANK = 16
GLA_GATE_NORM = 16.0
GLA_CHUNK = 16
DN_HEADS = 4
DN_CONV = 4
DN_CHUNK = 64
DIL_HEADS = 4
DIL_PATTERNS = ((128, 1), (512, 4), (2048, 16))
DIL_BLOCK = 128
SB_BLOCK = 128
N_BRANCH = 4
BRANCH_WIDTH = 4 * HEAD_DIM
ROPE_THETA = 500000.0
ROT_DIM = HEAD_DIM // 4
D_FF = 4 * D_MODEL
EPS = 1e-6

IN_SIZES = (
    SB_HEADS * HEAD_DIM, SB_HEADS * HEAD_DIM, SB_HEADS * HEAD_DIM,
    GLA_HEADS * GLA_KEY_DIM, GLA_HEADS * GLA_KEY_DIM, GLA_HEADS * GLA_VAL_DIM, GLA_LOW_RANK, GLA_HEADS * GLA_VAL_DIM,
    3 * DN_HEADS * HEAD_DIM, DN_HEADS, DN_HEADS, DN_HEADS * HEAD_DIM,
    DIL_HEADS * HEAD_DIM, DIL_HEADS * HEAD_DIM, DIL_HEADS * HEAD_DIM,
    N_BRANCH * D_MODEL,
)
N_IN = sum(IN_SIZES)
IN_OFFSETS = tuple(int(o) for o in np.cumsum(IN_SIZES)[:-1])

kernel_name = 'hybrid_sb_gla_gdn_dilated_trunk'


def rms_norm(x, g):
    xf = x.astype(jnp.float32)
    y = xf * lax.rsqrt(jnp.mean(xf * xf, axis=-1, keepdims=True) + EPS)
    return (y * g.astype(jnp.float32)).astype(x.dtype)


def l2norm(x):
    xf = x.astype(jnp.float32)
    return xf * lax.rsqrt(jnp.sum(xf * xf, axis=-1, keepdims=True) + EPS)


def rope_partial(x, pos):
    half = ROT_DIM // 2
    inv_freq = ROPE_THETA ** (-jnp.arange(half, dtype=jnp.float32) / half)
    ang = pos.astype(jnp.float32)[:, None] * inv_freq[None, :]
    cos = jnp.cos(ang)[None, :, None, :]
    sin = jnp.sin(ang)[None, :, None, :]
    x1 = x[..., :half].astype(jnp.float32)
    x2 = x[..., half:ROT_DIM].astype(jnp.float32)
    rot = jnp.concatenate([x1 * cos - x2 * sin, x2 * cos + x1 * sin], axis=-1).astype(x.dtype)
    return jnp.concatenate([rot, x[..., ROT_DIM:]], axis=-1)


def _chunk(t, c):
    B, S = t.shape[:2]
    t = t.reshape(B, S // c, c, *t.shape[2:]).astype(jnp.float32)
    return jnp.moveaxis(t, [1, 3], [0, 2])


def _unchunk(t):
    nc, B, H, c, d = t.shape
    return jnp.moveaxis(t, [0, 2], [1, 3]).reshape(B, nc * c, H, d)


def causal_conv(x, w):
    K = w.shape[-1]
    rhs = w.T[:, None, :].astype(x.dtype)
    return lax.conv_general_dilated(x, rhs, window_strides=(1,), padding=[(K - 1, 0)],
                                    dimension_numbers=('NWC', 'WIO', 'NWC'), feature_group_count=x.shape[-1])


def stick_breaking_attention(q, k, v):
    B, S, H, dh = q.shape
    nb = S // SB_BLOCK
    scale = dh ** -0.5

    def blocks(t):
        return t.reshape(B, nb, SB_BLOCK, H, dh).transpose(1, 0, 3, 2, 4)

    qb, kb, vb = blocks(q), blocks(k), blocks(v)
    pq_np, pk_np = np.tril_indices(nb)
    pq = jnp.asarray(pq_np, dtype=jnp.int32)
    pk = jnp.asarray(pk_np, dtype=jnp.int32)
    r_idx = jnp.arange(SB_BLOCK, dtype=jnp.int32)
    later = (r_idx[:, None] > r_idx[None, :]).astype(jnp.float32)

    def pair(args):
        i, j = args
        z = jnp.einsum('bhqc,bhkc->bhqk', qb[i], kb[j]).astype(jnp.float32) * scale
        causal = (j * SB_BLOCK + r_idx)[None, :] < (i * SB_BLOCK + r_idx)[:, None]
        z = jnp.where(causal, z, -jnp.inf)
        sp = jax.nn.softplus(z)
        between = -jnp.einsum('bhqk,kc->bhqc', sp, later)
        w = jnp.exp(z - sp + between)
        o = jnp.einsum('bhqk,bhkc->bhqc', w, vb[j].astype(jnp.float32))
        return o, -jnp.sum(sp, axis=-1)

    o_all, r_all = lax.map(pair, (pq, pk))
    R = jnp.zeros((nb, nb, B, H, SB_BLOCK), jnp.float32).at[pq, pk].set(r_all)
    later_blocks = lax.cumsum(R, axis=1, reverse=True) - R
    factor = jnp.exp(later_blocks[pq, pk])
    out = jax.ops.segment_sum(factor[..., None] * o_all, pq, num_segments=nb)
    return out.transpose(1, 0, 3, 2, 4).reshape(B, S, H, dh).astype(v.dtype)


def gla_chunked(q, k, v, log_a):
    B, S, H, dk = q.shape
    dv = v.shape[-1]
    c = GLA_CHUNK
    qc, kc, vc = _chunk(q, c), _chunk(k, c), _chunk(v, c)
    b = jnp.cumsum(_chunk(log_a, c), axis=-2)
    causal = jnp.tril(jnp.ones((c, c), dtype=bool))
    decay = jnp.exp(jnp.where(causal[:, :, None], b[..., :, None, :] - b[..., None, :, :], -jnp.inf))
    scores = jnp.einsum('nbhic,nbhjc,nbhijc->nbhij', qc, kc, decay)
    o_intra = jnp.einsum('nbhij,nbhjd->nbhid', scores, vc)
    q_in = qc * jnp.exp(b)
    k_up = kc * jnp.exp(b[..., -1:, :] - b)
    a_last = jnp.exp(b[..., -1, :])

    def step(state, inp):
        q_i, k_i, v_i, a_i = inp
        o_i = jnp.einsum('bhic,bhcd->bhid', q_i, state)
        state = a_i[..., None] * state + jnp.einsum('bhic,bhid->bhcd', k_i, v_i)
        return state, o_i

    _, o_inter = lax.scan(step, jnp.zeros((B, H, dk, dv), jnp.float32), (q_in, k_up, vc, a_last))
    return _unchunk(o_intra + o_inter).astype(v.dtype)


def gla_branch(q, k, v, lr, r, w_lr2, b_lr, norm_g):
    B, S, H, dk = q.shape
    log_a = jax.nn.log_sigmoid((lr @ w_lr2 + b_lr).astype(jnp.float32)) / GLA_GATE_NORM
    o = gla_chunked(q * dk ** -0.5, k, v, log_a.reshape(B, S, H, dk))
    return rms_norm(o, norm_g) * jax.nn.silu(r)


def gated_delta_chunked(q, k, v, g, beta):
    B, S, H, dk = q.shape
    dv = v.shape[-1]
    c = DN_CHUNK
    qc, kc, vc = _chunk(q, c), _chunk(k, c), _chunk(v, c)
    bc = _chunk(beta, c)
    G = jnp.cumsum(_chunk(g, c), axis=-1)
    incl = jnp.tril(jnp.ones((c, c), dtype=bool))
    strict = jnp.tril(jnp.ones((c, c), dtype=bool), -1)
    decay = jnp.exp(jnp.where(incl, G[..., :, None] - G[..., None, :], -jnp.inf))
    kb = kc * bc[..., None]
    lower = jnp.where(strict, jnp.einsum('nbhic,nbhjc->nbhij', kb, kc) * decay, 0.0)
    rhs = jnp.concatenate([vc * bc[..., None], kb * jnp.exp(G)[..., None]], axis=-1)
    sol = lax.linalg.triangular_solve(jnp.eye(c, dtype=jnp.float32) + lower, rhs,
                                      left_side=True, lower=True, unit_diagonal=True)
    u, w = sol[..., :dv], sol[..., dv:]
    scores = jnp.where(incl, jnp.einsum('nbhic,nbhjc->nbhij', qc, kc) * decay, 0.0)
    q_in = qc * jnp.exp(G)[..., None]
    k_up = kc * jnp.exp(G[..., -1:] - G)[..., None]
    a_last = jnp.exp(G[..., -1])

    def step(state, inp):
        q_i, k_i, u_i, w_i, s_i, a_i = inp
        v_new = u_i - jnp.einsum('bhic,bhcd->bhid', w_i, state)
        o_i = jnp.einsum('bhic,bhcd->bhid', q_i, state) + jnp.einsum('bhij,bhjd->bhid', s_i, v_new)
        state = a_i[..., None, None] * state + jnp.einsum('bhic,bhid->bhcd', k_i, v_new)
        return state, o_i

    _, o = lax.scan(step, jnp.zeros((B, H, dk, dv), jnp.float32), (q_in, k_up, u, w, scores, a_last))
    return _unchunk(o).astype(v.dtype)


def deltanet_branch(qkv, a, b, gate, conv_w, a_log, dt_bias, norm_g):
    B, S, _ = qkv.shape
    qkv = jax.nn.silu(causal_conv(qkv, conv_w))
    q, k, v = [t.reshape(B, S, DN_HEADS, HEAD_DIM) for t in jnp.split(qkv, 3, axis=-1)]
    q = l2norm(q) * HEAD_DIM ** -0.5
    k = l2norm(k)
    beta = jax.nn.sigmoid(b.astype(jnp.float32))
    g = -jnp.exp(a_log.astype(jnp.float32)) * jax.nn.softplus(a.astype(jnp.float32) + dt_bias.astype(jnp.float32))
    o = gated_delta_chunked(q, k, v, g, beta)
    return rms_norm(o, norm_g) * jax.nn.silu(gate)


def dilated_pattern(q, k, v, dil, n_keys):
    B, S, H, dh = q.shape
    L = S // dil
    nb = -(-L // DIL_BLOCK)
    Lp = nb * DIL_BLOCK

    def strided(t):
        t = t.reshape(B, L, dil, H, dh).transpose(0, 2, 3, 1, 4)
        t = jnp.pad(t, ((0, 0), (0, 0), (0, 0), (0, Lp - L), (0, 0)))
        return t.reshape(B, dil, H, nb, DIL_BLOCK, dh)

    def with_prev(t):
        prev = jnp.pad(t[:, :, :, :-1], ((0, 0), (0, 0), (0, 0), (1, 0), (0, 0), (0, 0)))
        return jnp.concatenate([prev, t], axis=-2)

    qs = strided(q)
    kk = with_prev(strided(k))
    vv = with_prev(strided(v)).astype(jnp.float32)
    s = jnp.einsum('brhnqc,brhnkc->brhnqk', qs, kk).astype(jnp.float32) * HEAD_DIM ** -0.5
    qi = jnp.arange(DIL_BLOCK)[:, None]
    kj = jnp.arange(2 * DIL_BLOCK)[None, :]
    m = qi + DIL_BLOCK - kj
    key_n = jnp.arange(nb)[:, None, None] * DIL_BLOCK - DIL_BLOCK + kj[None]
    valid = (m >= 0) & (m <= n_keys) & (key_n >= 0)
    s = jnp.where(valid, s, -jnp.inf)
    mx = jnp.max(s, axis=-1)
    p = jnp.exp(s - mx[..., None])
    den = jnp.sum(p, axis=-1)
    num = jnp.einsum('brhnqk,brhnkc->brhnqc', p, vv)

    def unstrided(t):
        t = t.reshape(B, dil, H, Lp, *t.shape[5:])[:, :, :, :L]
        t = jnp.moveaxis(t, 3, 1)
        return t.reshape(B, S, H, *t.shape[4:])

    return unstrided(mx), unstrided(den), unstrided(num)


def dilated_branch(q, k, v, q_g, k_g):
    S = q.shape[1]
    pos = jnp.arange(S, dtype=jnp.int32)
    q = rope_partial(rms_norm(q, q_g), pos)
    k = rope_partial(rms_norm(k, k_g), pos)
    stats = [dilated_pattern(q, k, v, dil, window // dil) for window, dil in DIL_PATTERNS]
    mx = jnp.stack([st[0] for st in stats])
    den = jnp.stack([st[1] for st in stats])
    num = jnp.stack([st[2] for st in stats])
    w = jnp.exp(mx - jnp.max(mx, axis=0, keepdims=True))
    out = jnp.sum(w[..., None] * num, axis=0) / jnp.sum(w * den, axis=0)[..., None]
    return out.astype(v.dtype)


def hybrid_layer(x, g_mix, g_mlp, w_in, gla_w_lr2, gla_b_lr, gla_norm_g, dn_conv_w, dn_a_log, dn_dt_bias,
                 dn_norm_g, dil_q_g, dil_k_g, w_branch, w_out, w_mlp_in, w_mlp_out):
    B, S, _ = x.shape
    h = rms_norm(x, g_mix)
    (sb_q, sb_k, sb_v, gla_q, gla_k, gla_v, gla_lr, gla_r, dn_qkv, dn_a, dn_b, dn_gate,
     dil_q, dil_k, dil_v, gate_logits) = jnp.split(h @ w_in, IN_OFFSETS, axis=-1)

    def heads(t, n):
        return t.reshape(B, S, n, -1)

    o_sb = stick_breaking_attention(heads(sb_q, SB_HEADS), heads(sb_k, SB_HEADS), heads(sb_v, SB_HEADS))
    o_gla = gla_branch(heads(gla_q, GLA_HEADS), heads(gla_k, GLA_HEADS), heads(gla_v, GLA_HEADS), gla_lr,
                       heads(gla_r, GLA_HEADS), gla_w_lr2, gla_b_lr, gla_norm_g)
    o_dn = deltanet_branch(dn_qkv, dn_a, dn_b, heads(dn_gate, DN_HEADS), dn_conv_w, dn_a_log, dn_dt_bias, dn_norm_g)
    o_dil = dilated_branch(heads(dil_q, DIL_HEADS), heads(dil_k, DIL_HEADS), heads(dil_v, DIL_HEADS), dil_q_g, dil_k_g)

    gates = jax.nn.sigmoid(gate_logits.reshape(B, S, N_BRANCH, D_MODEL))
    merged = jnp.zeros_like(x)
    for n, o in enumerate((o_sb, o_gla, o_dn, o_dil)):
        merged = merged + gates[:, :, n] * (o.reshape(B, S, BRANCH_WIDTH) @ w_branch[n])
    x = x + merged @ w_out
    h = rms_norm(x, g_mlp)
    return x + jnp.square(jax.nn.relu(h @ w_mlp_in)) @ w_mlp_out


def setup_inputs(seed: int = 0) -> dict:
    key = jax.random.key(seed)
    ks = jax.random.split(key, 18)
    f32 = jnp.float32

    def nrm(k, shape, scale):
        return scale * jax.random.normal(k, shape, f32)

    dt = jnp.exp(jax.random.uniform(ks[8], (DEPTH, DN_HEADS), f32, math.log(1e-3), math.log(1e-1)))
    return {
        'x': jax.random.normal(ks[0], (BATCH, SEQ, D_MODEL), f32),
        'norm_mix_g': 1.0 + nrm(ks[1], (DEPTH, D_MODEL), 0.02),
        'norm_mlp_g': 1.0 + nrm(ks[2], (DEPTH, D_MODEL), 0.02),
        'w_in': nrm(ks[3], (DEPTH, D_MODEL, N_IN), D_MODEL ** -0.5),
        'gla_w_lr2': nrm(ks[4], (DEPTH, GLA_LOW_RANK, GLA_HEADS * GLA_KEY_DIM), GLA_LOW_RANK ** -0.5),
        'gla_b_lr': nrm(ks[5], (DEPTH, GLA_HEADS * GLA_KEY_DIM), 0.1),
        'gla_norm_g': 1.0 + nrm(ks[6], (DEPTH, GLA_VAL_DIM), 0.02),
        'dn_conv_w': nrm(ks[7], (DEPTH, 3 * DN_HEADS * HEAD_DIM, DN_CONV), DN_CONV ** -0.5),
        'dn_a_log': jnp.log(jax.random.uniform(ks[9], (DEPTH, DN_HEADS), f32, 1.0, 16.0)),
        'dn_dt_bias': dt + jnp.log(-jnp.expm1(-dt)),
        'dn_norm_g': 1.0 + nrm(ks[10], (DEPTH, HEAD_DIM), 0.02),
        'dil_q_norm_g': 1.0 + nrm(ks[11], (DEPTH, HEAD_DIM), 0.02),
        'dil_k_norm_g': 1.0 + nrm(ks[12], (DEPTH, HEAD_DIM), 0.02),
        'w_branch': nrm(ks[13], (DEPTH, N_BRANCH, BRANCH_WIDTH, D_MODEL), BRANCH_WIDTH ** -0.5),
        'w_out': nrm(ks[14], (DEPTH, D_MODEL, D_MODEL), D_MODEL ** -0.5),
        'w_mlp_in': nrm(ks[15], (DEPTH, D_MODEL, D_FF), D_MODEL ** -0.5),
        'w_mlp_out': nrm(ks[16], (DEPTH, D_FF, D_MODEL), 0.5 * D_FF ** -0.5),
    }


def reference(x, norm_mix_g, norm_mlp_g, w_in, gla_w_lr2, gla_b_lr, gla_norm_g, dn_conv_w, dn_a_log, dn_dt_bias,
              dn_norm_g, dil_q_norm_g, dil_k_norm_g, w_branch, w_out, w_mlp_in, w_mlp_out):
    for l in range(DEPTH):
        x = hybrid_layer(x, norm_mix_g[l], norm_mlp_g[l], w_in[l], gla_w_lr2[l], gla_b_lr[l], gla_norm_g[l],
                         dn_conv_w[l], dn_a_log[l], dn_dt_bias[l], dn_norm_g[l], dil_q_norm_g[l], dil_k_norm_g[l],
                         w_branch[l], w_out[l], w_mlp_in[l], w_mlp_out[l])
    return x
```

```python
import numpy as np
from contextlib import ExitStack
import concourse.bass as bass
import concourse.mybir as mybir
from concourse.bass_utils import run_bass_kernel_spmd

F32 = mybir.dt.float32
BF16 = mybir.dt.bfloat16
AF = mybir.ActivationFunctionType
ALU = mybir.AluOpType

D_MODEL = 1024
EPS = 1e-6
NEG = -30000.0


class Prog:
    ENGS = ("pe", "act", "dve", "pool", "sp")

    NSETS = 3

    def __init__(self, nc):
        self.nc = nc
        self.ops = []
        self.last_of = {}
        self.epoch = 0

    def add(self, eng, fn, r=(), w=(), chan=None, extra=(), inc=16, epoch_end=False):
        i = len(self.ops)
        x = tuple(t for t in r if isinstance(t, tuple) and t[0] == "ps")
        r = tuple(t for t in r if not (isinstance(t, tuple) and t[0] == "ps"))
        self.ops.append(dict(eng=eng, fn=fn, r=tuple(r), w=tuple(w), x=x, chan=chan, extra=tuple(extra), inc=inc,
                             epoch_end=epoch_end))
        self.last_of[("c", chan) if chan is not None else ("e", eng)] = i
        return i

    def pe(self, fn, r=(), w=()):
        return self.add("pe", fn, r, w)

    def act(self, fn, r=(), w=()):
        return self.add("act", fn, r, w)

    def dve(self, fn, r=(), w=()):
        return self.add("dve", fn, r, w)

    def pool(self, fn, r=(), w=()):
        return self.add("pool", fn, r, w)

    def dma(self, q, chan, fn, r=(), w=()):
        return self.add(q, fn, r, w, chan=chan)

    def coll(self, chan, fn, r=(), w=()):
        return self.add("pool", fn, r, w, chan=chan, inc=1)

    def barrier(self, new_epoch=False):
        new_epoch = False
        if new_epoch and self.epoch >= self.NSETS - 1:
            self.add("sp", "clear")
        lasts = list(self.last_of.values())
        for k, e in enumerate(self.ENGS):
            self.add(e, None, extra=lasts, epoch_end=(new_epoch and k == len(self.ENGS) - 1))
        if new_epoch:
            self.last_of = {}
            self.epoch += 1

    def emit(self, es):
        nc, ops = self.nc, self.ops
        n = len(ops)
        last_w, readers, chan_last = {}, {}, {}
        deps = [set() for _ in range(n)]
        for i, op in enumerate(ops):
            for t in op["r"]:
                if t in last_w:
                    deps[i].add(last_w[t])
            for t in op["w"] + op["x"]:
                if t in last_w:
                    deps[i].add(last_w[t])
                for rd in readers.get(t, ()):
                    deps[i].add(rd)
            for j in op["extra"]:
                deps[i].add(j)
            if op["chan"] is not None:
                if op["chan"] in chan_last:
                    deps[i].add(chan_last[op["chan"]])
                chan_last[op["chan"]] = i
            for t in op["r"]:
                readers.setdefault(t, []).append(i)
            for t in op["w"] + op["x"]:
                last_w[t] = i
                readers[t] = []
            deps[i].discard(i)

        def needed(i, j):
            a, b = ops[i], ops[j]
            if b["fn"] is None:
                return False
            if a["chan"] is not None or b["chan"] is not None:
                return True
            if a["eng"] != b["eng"]:
                return True
            if j in a["extra"]:
                return True
            if a["eng"] == "pe":
                return False
            return bool((set(a["r"]) | set(a["w"]) | set(a["x"])) & set(b["w"]))

        epoch_of = []
        ep = 0
        for i, op in enumerate(ops):
            epoch_of.append(ep)
            if op["epoch_end"]:
                ep += 1
        fdeps = [[j for j in deps[i] if needed(i, j) and epoch_of[j] == epoch_of[i]] for i in range(n)]
        signaled = set(j for i in range(n) for j in fdeps[i])
        cnt = {}
        ev = {}
        for i, op in enumerate(ops):
            if op["fn"] is None:
                continue
            if op["chan"] is not None:
                key = (epoch_of[i], "c", op["chan"])
                cnt[key] = cnt.get(key, 0) + op["inc"]
                ev[i] = (key, cnt[key])
            elif i in signaled:
                key = (epoch_of[i], "e", op["eng"])
                cnt[key] = cnt.get(key, 0) + 1
                ev[i] = (key, cnt[key])
        sems = {}
        for key in cnt:
            k3 = (key[0] % self.NSETS, key[1], key[2])
            if k3 not in sems:
                sems[k3] = es.enter_context(nc.semaphore("s_%d_%s_%s" % k3))
        self.stats = dict(n_ops=n, n_sems=len(sems), max_cnt=max(cnt.values()) if cnt else 0)

        def sem_of(key):
            return sems[(key[0] % self.NSETS, key[1], key[2])]

        def make_clear(i):
            tgt = (epoch_of[i] + 1) % self.NSETS
            lst = [sm for k3, sm in sems.items() if k3[0] == tgt]

            def fn(e):
                ins = None
                for sm in lst:
                    ins = e.sem_clear(sm)
                return ins
            return fn

        for i, op in enumerate(ops):
            if op["fn"] == "clear":
                op["fn"] = make_clear(i)

        block = es.enter_context(nc.Block())

        def run_engine(ename):
            def body(e):
                known = {}
                for i, op in enumerate(ops):
                    if op["eng"] != ename:
                        continue
                    need = {}
                    for j in fdeps[i]:
                        k, v = ev[j]
                        if need.get(k, 0) < v:
                            need[k] = v
                    for k, v in need.items():
                        if known.get(k, 0) < v:
                            e.wait_ge(sem_of(k), v)
                            known[k] = v
                    if op["fn"] is None:
                        continue
                    ins = op["fn"](e)
                    if i in ev:
                        k, v = ev[i]
                        ins.then_inc(sem_of(k), op["inc"] if k[1] == "c" else 1)
            return body

        block.tensor(run_engine("pe"))
        block.scalar(run_engine("act"))
        block.vector(run_engine("dve"))
        block.gpsimd(run_engine("pool"))
        block.sync(run_engine("sp"))


def make_consts(S):
    c = {}
    k = np.arange(128)
    c["ident"] = np.eye(128, dtype=np.float32)
    c["ones"] = np.ones((128, 128), np.float32)
    c["negge"] = -(k[:, None] >= k[None, :]).astype(np.float32)
    q = np.arange(512)
    mb = np.zeros((128, 4, 512), np.float32)
    for d in range(4):
        mb[:, d, :] = np.where((128 * d + k)[:, None] >= q[None, :], NEG, 0.0)
    c["sbmask"] = mb.reshape(128, 2048)
    le = (k[:, None] <= k[None, :]).astype(np.float32)
    lt = (k[:, None] < k[None, :]).astype(np.float32)
    gt = (k[:, None] > k[None, :]).astype(np.float32)
    ge = (k[:, None] >= k[None, :]).astype(np.float32)
    c["tri"] = np.concatenate([le, lt, gt, ge], axis=1)
    c["dilmask"] = np.concatenate([np.tile(ge, (1, 4)), np.tile(le, (1, 4))], axis=1)
    half = 8
    inv = 500000.0 ** (-np.arange(half, dtype=np.float32) / half)
    ang = np.arange(S, dtype=np.float32)[None, :] * inv[:, None].astype(np.float32)
    c["ropec"] = np.concatenate([np.cos(ang), np.cos(ang)], axis=0).astype(np.float32)
    c["ropes"] = np.concatenate([np.sin(ang), np.sin(ang)], axis=0).astype(np.float32)
    rt = np.zeros((16, 16), np.float32)
    for m in range(8):
        rt[m + 8, m] = -1.0
        rt[m, m + 8] = 1.0
    c["ropert"] = rt
    return c


def const_shapes(S):
    return {"ident": [128, 128], "ones": [128, 128], "negge": [128, 128], "sbmask": [128, 2048],
            "tri": [128, 512], "dilmask": [128, 1024], "ropec": [16, S], "ropes": [16, S], "ropert": [16, 16]}


class Ctx:
    pass


def build_B(S, phases=("sb",)):
    nc = bass.Bass("TRN2", target_bir_lowering=False)
    NT = S // 512
    dr = {}
    dr["xT"] = nc.dram_tensor("xT", [D_MODEL, S], F32, kind="ExternalInput").ap()
    dr["gmix"] = nc.dram_tensor("gmix", [128, 8], F32, kind="ExternalInput").ap()
    dr["w_sb"] = nc.dram_tensor("w_sb", [D_MODEL, 192], F32, kind="ExternalInput").ap()
    dr["w_dil"] = nc.dram_tensor("w_dil", [D_MODEL, 192], F32, kind="ExternalInput").ap()
    dr["w_gla"] = nc.dram_tensor("w_gla", [D_MODEL, 208], F32, kind="ExternalInput").ap()
    dr["w_dn"] = nc.dram_tensor("w_dn", [D_MODEL, 258], F32, kind="ExternalInput").ap()
    dr["pvec"] = nc.dram_tensor("pvec", [128, 8], F32, kind="ExternalInput").ap()
    dr["wlr2"] = nc.dram_tensor("wlr2", [32, 32], F32, kind="ExternalInput").ap()
    dr["dnconv"] = nc.dram_tensor("dnconv", [64, 12], F32, kind="ExternalInput").ap()
    for name, shp in const_shapes(S).items():
        dr[name] = nc.dram_tensor("c_" + name, shp, F32, kind="ExternalInput").ap()
    dr["o_sb"] = nc.dram_tensor("o_sb", [64, S], F32, kind="ExternalOutput").ap()
    dr["o_dil"] = nc.dram_tensor("o_dil", [64, S], F32, kind="ExternalOutput").ap()
    dr["o_gla"] = nc.dram_tensor("o_gla", [64, S], F32, kind="ExternalOutput").ap()
    dr["o_dn"] = nc.dram_tensor("o_dn", [64, S], F32, kind="ExternalOutput").ap()

    with ExitStack() as es:
        p = Prog(nc)
        C = Ctx()
        C.nc, C.p, C.S, C.NT, C.dr = nc, p, S, NT, dr

        def sb(name, shape, dt):
            return es.enter_context(nc.sbuf_tensor("s_" + name, shape, dt))

        C.PS = [es.enter_context(nc.psum_tensor("ps%d" % k, [128, 512], F32)) for k in range(8)]
        C.ident = sb("ident", [128, 128], BF16)
        C.ones = sb("ones", [128, 128], BF16)
        C.gmix = sb("gmix", [128, 8], F32)
        p.dma("pool", "cst", lambda e: e.dma_start(out=C.ident[:], in_=dr["ident"][:, :]), w=["ident"])
        p.dma("pool", "cst", lambda e: e.dma_start(out=C.ones[:], in_=dr["ones"][:, :]), w=["ones"])
        p.dma("sp", "cst2", lambda e: e.dma_start(out=C.gmix[:], in_=dr["gmix"][:, :]), w=["gmix"])
        C.identf = sb("identf", [128, 128], F32)
        C.onesf = sb("onesf", [128, 128], F32)
        C.tri = sb("tri", [128, 512], F32)
        C.tri_bf = sb("tri_bf", [128, 512], BF16)
        p.dma("pool", "cst", lambda e: e.dma_start(out=C.tri_bf[:], in_=dr["tri"][:, :]), w=["tri_bf"])
        C.pvec = sb("pvec", [128, 8], F32)
        p.dma("sp", "cst2", lambda e: e.dma_start(out=C.identf[:], in_=dr["ident"][:, :]), w=["identf"])
        p.dma("sp", "cst2", lambda e: e.dma_start(out=C.onesf[:], in_=dr["ones"][:, :]), w=["onesf"])
        p.dma("sp", "cst2", lambda e: e.dma_start(out=C.tri[:], in_=dr["tri"][:, :]), w=["tri"])
        p.dma("sp", "cst2", lambda e: e.dma_start(out=C.pvec[:], in_=dr["pvec"][:, :]), w=["pvec"])
        C.xt = [sb("xt%d" % i, [128, 8, 512], F32) for i in range(2)]
        C.sq = sb("sq", [128, 8, 512], BF16)
        C.rt = sb("rt", [128, 512], F32)
        C.rstd = sb("rstd", [128, 512], F32)
        C.hT = [sb("hT%d" % i, [128, 8, 512], BF16) for i in range(2)]

        C.cr_sq = sb("cr_sq", [64, 512], F32)
        C.cr_l = sb("cr_l", [64, 512], F32)
        C.gn_of = sb("gn_of", [64, 512], F32)
        C.gn_rs = sb("gn_rs", [64, 512], F32)
        C.gn_er = sb("gn_er", [64, 512], F32)
        C.gn_sg = sb("gn_sg", [64, 512], F32)
        C.gn_t1 = sb("gn_t1", [64, 512], F32)
        C.gn_out = [sb("gn_out%d" % i, [64, 512], F32) for i in range(2)]
        for ph, fn in (("sb", phase_sb), ("dil", phase_dil), ("gla", phase_gla), ("dn", phase_dn)):
            if ph in phases:
                with ExitStack() as es2:
                    fn(C, es2)
                p.barrier()

        p.barrier()
        p.emit(es)
        print("B program ops:", p.stats)
    return nc


def o_slice(C, name, tok, width):
    if getattr(C, "fused", False):
        TC = C.S // 4
        r_, w_ = divmod(tok, TC)
        k_, c_ = divmod(w_, C.CW)
        return C.osrc[name][k_][:, r_ * C.CW + c_:r_ * C.CW + c_ + width]
    return C.dr[name][:, tok:tok + width]


def load_weights(C, es2, name, ncols):
    nc, p = C.nc, C.p
    wst = es2.enter_context(nc.sbuf_tensor(getattr(C, "tag", "") + "s_wst_" + name, [128, 8, ncols], F32))
    wb = es2.enter_context(nc.sbuf_tensor(getattr(C, "tag", "") + "s_wb_" + name, [128, 8, ncols], BF16))
    src = C.dr[name].rearrange("(c p) n -> p c n", p=128)
    p.dma("sp", "w", lambda e: e.dma_start(out=wst[:], in_=src), w=["wst"])
    for ch in range(8):
        p.dve(lambda e, ch=ch: e.tensor_scalar(out=wb[:, ch, :], in0=wst[:, ch, :], scalar1=C.gmix[:, ch:ch + 1],
                                               scalar2=None, op0=ALU.mult),
              r=["wst", "gmix"], w=["wb"])
    return wb


def front_end(C, T):
    p, nc = C.p, C.nc
    b = T % 2
    if getattr(C, "fused", False):
        hT = C.hT[b]
        r_t, w_ = divmod(T * 512, C.S // 4)
        k_, col = divmod(w_, C.HW)
        src = C.hall[k_].rearrange("(r c p) s -> r p c s", r=4, p=128)[r_t][:, :, col:col + 512]
        p.dma("sp", "x%d" % b, lambda e: e.dma_start(out=hT[:], in_=src), r=[("hall", k_)], w=[("hT", b)])
        return hT, ("hT", b)
    xt, hT = C.xt[b], C.hT[b]
    src = C.dr["xT"].rearrange("(c p) s -> p c s", p=128)[:, :, T * 512:(T + 1) * 512]
    p.dma("sp", "x%d" % b, lambda e: e.dma_start(out=xt[:], in_=src), w=[("xt", b)])
    p.act(lambda e: e.activation(out=C.sq[:], in_=xt[:], func=AF.Square), r=[("xt", b)], w=["sq"])
    ss = C.PS[7]
    for ch in range(8):
        p.pe(lambda e, ch=ch: e.matmul(ss[:, :], lhsT=C.ones[:, :], rhs=C.sq[:, ch, :], start=(ch == 0), stop=(ch == 7)),
             r=["sq", "ones"], w=[("ps", 7)])
    p.act(lambda e: e.activation(out=C.rt[:], in_=ss[:, :], func=AF.Ln, scale=1.0 / D_MODEL, bias=EPS),
          r=[("ps", 7)], w=["rt"])
    p.act(lambda e: e.activation(out=C.rstd[:], in_=C.rt[:], func=AF.Exp, scale=-0.5), r=["rt"], w=["rstd"])
    for ch in range(8):
        p.dve(lambda e, ch=ch: e.tensor_tensor(out=hT[:, ch, :], in0=xt[:, ch, :], in1=C.rstd[:], op=ALU.mult),
              r=[("xt", b), "rstd"], w=[("hT", b)])
    return hT, ("hT", b)


def project(C, wb, hT, hT_tok, c0, ncol, bank):
    p = C.p
    ps = C.PS[bank]
    for ch in range(8):
        p.pe(lambda e, ch=ch: e.matmul(ps[0:ncol, :], lhsT=wb[:, ch, c0:c0 + ncol], rhs=hT[:, ch, :],
                                       start=(ch == 0), stop=(ch == 7)),
             r=["wb", hT_tok], w=[("ps", bank)])
    return ps


def phase_sb(C, es2):
    nc, p, S, NT = C.nc, C.p, C.S, C.NT
    dr = C.dr

    def sb(name, shape, dt):
        return es2.enter_context(nc.sbuf_tensor(getattr(C, "tag", "") + "s_" + name, shape, dt))

    wb = load_weights(C, es2, "w_sb", 192)
    Q = sb("sbQ", [64, S], BF16)
    K = sb("sbK", [64, S], BF16)
    V = sb("sbV", [128, S // 128, 64], BF16)
    vT = sb("sbvT", [64, 512], BF16)
    negge = sb("negge", [128, 128], BF16)
    mbias = sb("sbmask", [128, 4, 512], BF16)
    onescol = sb("onescol", [128, 1], BF16)
    negrow = sb("negrow", [1, 128], BF16)
    p.dma("pool", "cst", lambda e: e.dma_start(out=negge[:], in_=dr["negge"][:, :]), w=["negge"])
    p.dma("pool", "cst", lambda e: e.dma_start(out=mbias[:], in_=dr["sbmask"].rearrange("p (d q) -> p d q", d=4)),
          w=["mbias"])
    p.dve(lambda e: e.memset(onescol[:], 1.0), w=["onescol"])
    p.dve(lambda e: e.memset(negrow[:], -1.0), w=["negrow"])

    for T in range(NT):
        hT, htok = front_end(C, T)
        ps = project(C, wb, hT, htok, 0, 64, 0)
        p.act(lambda e, ps=ps, T=T: e.mul(out=Q[:, T * 512:(T + 1) * 512], in_=ps[0:64, :], mul=0.125),
              r=[("ps", 0)], w=[("sbQ", T)])
        ps = project(C, wb, hT, htok, 64, 64, 1)
        p.act(lambda e, ps=ps, T=T: e.copy(out=K[:, T * 512:(T + 1) * 512], in_=ps[0:64, :]),
              r=[("ps", 1)], w=[("sbK", T)])
        ps = project(C, wb, hT, htok, 128, 64, 2)
        p.act(lambda e, ps=ps: e.copy(out=vT[:], in_=ps[0:64, :]), r=[("ps", 2)], w=["sbvT"])
        pt = C.PS[3]
        for blk in range(4):
            p.pe(lambda e, blk=blk: e.matmul(pt[:, blk * 64:(blk + 1) * 64], lhsT=vT[:, blk * 128:(blk + 1) * 128],
                                             rhs=C.ident[0:64, 0:64], start=True, stop=True),
                 r=["sbvT", "ident"], w=[("ps", 3)])
        p.dve(lambda e, T=T: e.tensor_copy(out=V[:, 4 * T:4 * T + 4, :],
                                           in_=pt[:, 0:256].rearrange("p (b c) -> p b c", b=4)),
              r=[("ps", 3)], w=[("sbV", T)])

    ebuf = [sb("sb_e%d" % i, [128, 512], F32) for i in range(2)]
    spb = [sb("sb_sp%d" % i, [128, 512], BF16) for i in range(4)]
    wbuf = [sb("sb_w%d" % i, [128, 512], BF16) for i in range(3)]
    acc_f = [sb("sb_af%d" % i, [64, 512], F32) for i in range(2)]
    acc_hl = [sb("sb_ahl%d" % i, [64, 512], BF16) for i in range(3)]
    onesc64 = sb("onesc64", [128, 64], BF16)
    negsel = sb("negsel", [64, 128], BF16)
    p.dve(lambda e: e.memset(onesc64[:], 1.0), w=["onesc64"])
    p.dve(lambda e: e.memset(negsel[:], 0.0), w=["negsel"])
    p.dve(lambda e: e.memset(negsel[0:1, :], -1.0), r=["negsel"], w=["negsel"])
    p.dve(lambda e: e.memset(negsel[32:33, :], -1.0), r=["negsel"], w=["negsel"])
    fused = getattr(C, "fused", False)
    oq = "sp" if fused else "pool"
    osb = [sb("sb_o%d" % i, [64, 512], BF16 if fused else F32) for i in range(2)]

    tiles = []
    for T in range(NT):
        for j in range(4 * T + 3, -1, -1):
            tiles.append((T, j))
    nt = len(tiles)

    def stageA(s):
        T, j = tiles[s]
        zb = s % 4
        Z = C.PS[zb]
        diag = j >= 4 * T
        p.pe(lambda e: e.matmul(Z[:, :], lhsT=K[:, j * 128:(j + 1) * 128], rhs=Q[:, T * 512:(T + 1) * 512],
                                start=True, stop=False),
             r=[("sbK", j // 4), ("sbQ", T)], w=[("ps", zb)])
        if diag:
            d = j - 4 * T
            p.pe(lambda e: e.matmul(Z[:, :], lhsT=C.ident[:, :], rhs=mbias[:, d, :], start=False, stop=False),
                 r=["ident", "mbias"], w=[("ps", zb)])
        eb = ebuf[s % 2]
        p.act(lambda e: e.activation(out=eb[:], in_=Z[:, :], func=AF.Exp), r=[("ps", zb)], w=[("sbe", s % 2)])
        p.act(lambda e: e.activation(out=spb[s % 4][:], in_=eb[:], func=AF.Ln, bias=1.0),
              r=[("sbe", s % 2)], w=[("sbsp", s % 4)])

    def stageA2(s):
        T, j = tiles[s]
        first = (j == 4 * T + 3)
        last = (j == 0)
        if last:
            return
        cb = 6 + (s % 2)
        cs = C.PS[cb]
        sp = spb[s % 4]
        p.pe(lambda e: e.matmul(cs[0:64, :], lhsT=onesc64[:, :], rhs=sp[:], start=True, stop=True),
             r=["onesc64", ("sbsp", s % 4)], w=[("ps", cb)])
        af, afp = acc_f[s % 2], acc_f[(s - 1) % 2]
        if first:
            p.dve(CP(af[:], cs[0:64, :]), r=[("ps", cb)], w=[("sbaf", s % 2)])
        else:
            p.dve(TT(af[:], afp[:], cs[0:64, :], ALU.add), r=[("ps", cb), ("sbaf", (s - 1) % 2)], w=[("sbaf", s % 2)])
        hl = acc_hl[s % 3]
        p.dve(CP(hl[:], af[:]), r=[("sbaf", s % 2)], w=[("sbahl", s % 3)])
        p.pool(TT(hl[32:64, :], af[32:64, :], hl[32:64, :], ALU.subtract), r=[("sbaf", s % 2), ("sbahl", s % 3)],
               w=[("sbahl", s % 3)])

    def stageB(s):
        T, j = tiles[s]
        zb = s % 4
        Z = C.PS[zb]
        first = (j == 4 * T + 3)
        sp = spb[s % 4]
        p.pe(lambda e: e.matmul(Z[:, :], lhsT=negge[:, :], rhs=sp[:], start=False, stop=first),
             r=["negge", ("sbsp", s % 4)], w=[("ps", zb)])
        if not first:
            pa = (s - 1) % 3
            p.pe(lambda e: e.matmul(Z[:, :], lhsT=negsel[:, :], rhs=acc_hl[pa][:], start=False, stop=True),
                 r=["negsel", ("sbahl", pa)], w=[("ps", zb)])
        p.act(lambda e: e.activation(out=wbuf[s % 3][:], in_=Z[:, :], func=AF.Exp),
              r=[("ps", zb)], w=[("sbw", s % 3)])

    def stageC(s):
        T, j = tiles[s]
        first = (j == 4 * T + 3)
        last = (j == 0)
        ob = 4 + (T % 2)
        O = C.PS[ob]
        p.pe(lambda e: e.matmul(O[0:64, :], lhsT=V[:, j, :], rhs=wbuf[s % 3][:], start=first, stop=last),
             r=[("sbV", j // 4), ("sbw", s % 3)], w=[("ps", ob)])
        if last:
            o = osb[T % 2]
            dst = o_slice(C, "o_sb", T * 512, 512)
            p.dve(lambda e: e.tensor_copy(out=o[:], in_=O[0:64, :]), r=[("ps", ob)], w=[("sbo", T % 2)])
            p.dma(oq, "o%d" % (T % 2),
                  lambda e: e.dma_start(out=dst, in_=o[:]),
                  r=[("sbo", T % 2)], w=["out_sb"])

    for _ in range(int(getattr(C, "warm", 48))):
        p.pe(lambda e: e.matmul(C.PS[7][:, :], lhsT=negge[:, :], rhs=mbias[:, 0, :], start=True, stop=True),
             r=["negge", "mbias"], w=[("ps", 7)])
    for s in range(nt + 3):
        if s < nt:
            stageA(s)
        if 0 <= s - 1 < nt:
            stageA2(s - 1)
        if 0 <= s - 2 < nt:
            stageB(s - 2)
        if 0 <= s - 3 < nt:
            stageC(s - 3)


def TT(out, in0, in1, op):
    return lambda e: e.tensor_tensor(out=out, in0=in0, in1=in1, op=op)


def TS(out, in0, s1, s2, op0, op1=None):
    if op1 is None:
        return lambda e: e.tensor_scalar(out=out, in0=in0, scalar1=s1, scalar2=None, op0=op0)
    return lambda e: e.tensor_scalar(out=out, in0=in0, scalar1=s1, scalar2=s2, op0=op0, op1=op1)


def STT(out, in0, scalar, in1, op0, op1):
    return lambda e: e.scalar_tensor_tensor(out=out, in0=in0, scalar=scalar, in1=in1, op0=op0, op1=op1)


def ACTF(out, in_, func, **kw):
    return lambda e: e.activation(out=out, in_=in_, func=func, **kw)


def MM(out, lhsT, rhs, start=True, stop=True):
    return lambda e: e.matmul(out, lhsT=lhsT, rhs=rhs, start=start, stop=stop)


def CP(out, in_):
    return lambda e: e.tensor_copy(out=out, in_=in_)


def RCP(out, in_):
    return lambda e: e.reciprocal(out=out, in_=in_)


def pe_warmup(C, bank, n=48):
    for _ in range(n):
        C.p.pe(MM(C.PS[bank][:, :], C.ident[:, :], C.tri_bf[:, :]), r=["ident", "tri_bf"], w=[("ps", bank)])


def col_rstd(C, src, src_toks, nrow, inv_n, bank, out_rs, out_tok):
    p = C.p
    ps = C.PS[bank]
    p.act(ACTF(C.cr_sq[0:nrow, :], src, AF.Square), r=src_toks, w=["cr_sq"])
    p.pe(MM(ps[0:nrow, :], C.onesf[0:nrow, 0:nrow], C.cr_sq[0:nrow, :]), r=["cr_sq", "onesf"], w=[("ps", bank)])
    p.act(ACTF(C.cr_l[0:nrow, :], ps[0:nrow, :], AF.Ln, scale=inv_n, bias=EPS), r=[("ps", bank)], w=["cr_l"])
    p.act(ACTF(out_rs, C.cr_l[0:nrow, :], AF.Exp, scale=-0.5), r=["cr_l"], w=[out_tok])


def gated_norm(C, o_ps, o_tok, rf, rf_tok, gcol, out_name, T, ssbank):
    p = C.p
    p.act(lambda e: e.copy(out=C.gn_of[:], in_=o_ps), r=[o_tok], w=["gn_of"])
    col_rstd(C, C.gn_of[:], ["gn_of"], 64, 1.0 / 64, ssbank, C.gn_rs[:], "gn_rs")
    p.act(ACTF(C.gn_er[:], rf, AF.Exp, scale=-1.0), r=[rf_tok], w=["gn_er"])
    p.dve(TS(C.gn_er[:], C.gn_er[:], 1.0, None, ALU.add), r=["gn_er"], w=["gn_er"])
    p.dve(RCP(C.gn_sg[:], C.gn_er[:]), r=["gn_er"], w=["gn_sg"])
    p.dve(TT(C.gn_t1[:], C.gn_of[:], C.gn_rs[:], ALU.mult), r=["gn_of", "gn_rs"], w=["gn_t1"])
    p.dve(STT(C.gn_t1[:], C.gn_t1[:], gcol, rf, ALU.mult, ALU.mult), r=["gn_t1", rf_tok, "pvec"], w=["gn_t1"])
    ob = C.gn_out[T % 2]
    p.dve(TT(ob[:], C.gn_t1[:], C.gn_sg[:], ALU.mult), r=["gn_t1", "gn_sg"], w=[("gn_out", T % 2)])
    dst = o_slice(C, out_name, T * 512, 512)
    p.dma("sp" if getattr(C, "fused", False) else "pool", "go%d" % (T % 2), lambda e: e.dma_start(out=dst, in_=ob[:]),
          r=[("gn_out", T % 2)], w=["gn_dram"])


def phase_dil(C, es2):
    nc, p, S, NT, dr = C.nc, C.p, C.S, C.NT, C.dr

    def sb(name, shape, dt):
        return es2.enter_context(nc.sbuf_tensor(getattr(C, "tag", "") + "s_" + name, shape, dt))

    wb = load_weights(C, es2, "w_dil", 192)
    QV = sb("dQV", [128, S], BF16)
    Q = QV[0:64, :]
    K = sb("dK", [64, S], BF16)
    dmask = sb("dmask", [128, 1024], BF16)
    rt16 = sb("rt16", [16, 16], F32)
    cosb = sb("dcos", [16, 512], F32)
    sinb = sb("dsin", [16, 512], F32)
    qn = sb("dqn", [64, 512], F32)
    rsb = sb("drs", [64, 512], F32)
    t1 = sb("dt1", [16, 512], F32)
    t2 = sb("dt2", [16, 512], F32)
    p.dma("pool", "cst", lambda e: e.dma_start(out=dmask[:], in_=dr["dilmask"][:, :]), w=["dmask"])
    p.dma("sp", "cst2", lambda e: e.dma_start(out=rt16[:], in_=dr["ropert"][:, :]), w=["rt16"])

    for T in range(NT):
        hT, htok = front_end(C, T)
        ts_ = slice(T * 512, (T + 1) * 512)
        p.dma("sp", "rope", lambda e, ts_=ts_: e.dma_start(out=cosb[:], in_=dr["ropec"][:, ts_]), w=["dcos"])
        p.dma("sp", "rope", lambda e, ts_=ts_: e.dma_start(out=sinb[:], in_=dr["ropes"][:, ts_]), w=["dsin"])
        for which, (c0, gc, dst, scl) in enumerate(((0, 0, QV, 0.125), (64, 1, K, 1.0))):
            ps = project(C, wb, hT, htok, c0, 64, which)
            col_rstd(C, ps[0:64, :], [("ps", which)], 64, 1.0 / 64, 3, rsb[:], "drs")
            p.dve(STT(qn[:], ps[0:64, :], C.pvec[0:64, gc:gc + 1], rsb[:], ALU.mult, ALU.mult),
                  r=[("ps", which), "pvec", "drs"], w=["dqn"])
            p.pe(MM(C.PS[4][0:16, :], rt16[:, :], qn[0:16, :]), r=["rt16", "dqn"], w=[("ps", 4)])
            p.dve(TT(t1[:], qn[0:16, :], cosb[:], ALU.mult), r=["dqn", "dcos"], w=["dt1"])
            p.dve(TT(t2[:], C.PS[4][0:16, :], sinb[:], ALU.mult), r=[("ps", 4), "dsin"], w=["dt2"])
            p.dve(TT(qn[0:16, :], t1[:], t2[:], ALU.add), r=["dt1", "dt2"], w=["dqn"])
            p.act(lambda e, dst=dst, scl=scl, ts_=ts_: e.mul(out=dst[0:64, ts_], in_=qn[:], mul=scl), r=["dqn"], w=["dQK"])
        ps = project(C, wb, hT, htok, 128, 64, 2)
        p.dve(CP(QV[64:128, ts_], ps[0:64, :]), r=[("ps", 2)], w=["dVT"])
    p.barrier()

    nsp = S // 2048
    accn = sb("daccn", [64, 2048], F32)
    accd = sb("daccd", [64, 2048], F32)
    Vl = {(pi, par): sb("dVl%d_%d" % (pi, par), [128, 16, 64], BF16) for pi in range(3) for par in range(2)}
    pA = sb("dpA", [128, 512], BF16)
    pB = sb("dpB", [128, 512], BF16)
    obf = sb("dobf", [64, 2048], BF16)
    pats = (1, 4, 16)

    def lanes_of(d, sidx):
        if d == 1:
            return [(0, 16 * sidx + i) for i in range(16)]
        if d == 4:
            return [(r, 4 * sidx + i) for i in range(4) for r in range(4)]
        return [(r, sidx) for r in range(16)]

    def lane_slice(d, r, nb, off=0):
        t0 = d * 128 * nb + r - off
        return slice(t0, t0 + 127 * d + 1, d)

    def lane_index(d, r, nb):
        if d == 1:
            return nb // 16, nb % 16
        if d == 4:
            return nb // 4, (nb % 4) * 4 + r
        return nb, r

    for sidx in range(nsp):
        par = sidx % 2
        p.dve(lambda e: e.memset(accn[:], 0.0), w=["daccn"])
        p.dve(lambda e: e.memset(accd[:], 0.0), w=["daccd"])
        for pi, d in enumerate(pats):
            lanes = lanes_of(d, sidx)
            vl = Vl[(pi, par)]
            for half in range(2):
                bank = 4 + half
                for l8 in range(8):
                    r_, nb_ = lanes[half * 8 + l8]
                    p.pe(MM(C.PS[bank][:, l8 * 64:(l8 + 1) * 64], QV[64:128, lane_slice(d, r_, nb_)], C.ident[64:128, 64:128]),
                         r=["ident"], w=[("ps", bank)])
                p.dve(CP(vl[:, half * 8:(half + 1) * 8, :], C.PS[bank][:, :].rearrange("p (l c) -> p l c", l=8)),
                      r=[("ps", bank)], w=[("dVl", pi, par)])
            for g in range(4):
                cur = lanes[4 * g:4 * g + 4]
                any_prev = any(nb_ > 0 for (_, nb_) in cur)
                for li, (r_, nb_) in enumerate(cur):
                    qs = Q[:, lane_slice(d, r_, nb_)]
                    cs = slice(li * 128, (li + 1) * 128)
                    if nb_ > 0:
                        p.pe(MM(C.PS[0][:, cs], K[:, lane_slice(d, r_, nb_ - 1)], qs), w=[("ps", 0)])
                    p.pe(MM(C.PS[1][:, cs], K[:, lane_slice(d, r_, nb_)], qs), w=[("ps", 1)])
                if any_prev:
                    p.act(ACTF(pA[:], C.PS[0][:, :], AF.Exp), r=[("ps", 0)], w=["dpA"])
                    p.dve(TT(pA[:], pA[:], dmask[:, 0:512], ALU.mult), r=["dpA", "dmask"], w=["dpA"])
                p.act(ACTF(pB[:], C.PS[1][:, :], AF.Exp), r=[("ps", 1)], w=["dpB"])
                p.dve(TT(pB[:], pB[:], dmask[:, 512:1024], ALU.mult), r=["dpB", "dmask"], w=["dpB"])
                for li, (r_, nb_) in enumerate(cur):
                    cs = slice(li * 128, (li + 1) * 128)
                    sp_c, idx_c = lane_index(d, r_, nb_)
                    vcur = Vl[(pi, sp_c % 2)][:, idx_c, :]
                    rtoks = ["dpA", "dpB", ("dVl", pi, 0), ("dVl", pi, 1), "ones"]
                    if nb_ > 0:
                        sp_p, idx_p = lane_index(d, r_, nb_ - 1)
                        vprev = Vl[(pi, sp_p % 2)][:, idx_p, :]
                        p.pe(MM(C.PS[2][0:64, cs], vprev, pA[:, cs], True, False), r=rtoks, w=[("ps", 2)])
                        p.pe(MM(C.PS[2][0:64, cs], vcur, pB[:, cs], False, True), r=rtoks, w=[("ps", 2)])
                        p.pe(MM(C.PS[3][0:64, cs], C.ones[:, 0:64], pA[:, cs], True, False), r=rtoks, w=[("ps", 3)])
                        p.pe(MM(C.PS[3][0:64, cs], C.ones[:, 0:64], pB[:, cs], False, True), r=rtoks, w=[("ps", 3)])
                    else:
                        p.pe(MM(C.PS[2][0:64, cs], vcur, pB[:, cs], True, True), r=rtoks, w=[("ps", 2)])
                        p.pe(MM(C.PS[3][0:64, cs], C.ones[:, 0:64], pB[:, cs], True, True), r=rtoks, w=[("ps", 3)])
                for li, (r_, nb_) in enumerate(cur):
                    cs = slice(li * 128, (li + 1) * 128)
                    sl = lane_slice(d, r_, nb_, off=2048 * sidx)
                    p.dve(TT(accn[:, sl], accn[:, sl], C.PS[2][0:64, cs], ALU.add), r=[("ps", 2), "daccn"], w=["daccn"])
                    p.dve(TT(accd[:, sl], accd[:, sl], C.PS[3][0:64, cs], ALU.add), r=[("ps", 3), "daccd"], w=["daccd"])
        p.dve(RCP(accd[:], accd[:]), r=["daccd"], w=["daccd"])
        dst = o_slice(C, "o_dil", sidx * 2048, 2048)
        if getattr(C, "fused", False):
            p.dve(TT(obf[:], accn[:], accd[:], ALU.mult), r=["daccn", "daccd"], w=["dobf"])
            p.dma("sp", "od", lambda e, dst=dst: e.dma_start(out=dst, in_=obf[:]), r=["dobf"], w=["od_dram"])
        else:
            p.dve(TT(accn[:], accn[:], accd[:], ALU.mult), r=["daccn", "daccd"], w=["daccn"])
            p.dma("pool", "od", lambda e, dst=dst: e.dma_start(out=dst, in_=accn[:]),
                  r=["daccn"], w=["od_dram"])


def phase_gla(C, es2):
    nc, p, S, NT, dr = C.nc, C.p, C.S, C.NT, C.dr

    def sb(name, shape, dt):
        return es2.enter_context(nc.sbuf_tensor(getattr(C, "tag", "") + "s_" + name, shape, dt))

    wb = load_weights(C, es2, "w_gla", 208)
    wlr2 = sb("gwlr2", [32, 32], F32)
    lrT = sb("glrT", [32, 512], F32)
    qf = sb("gqf", [32, 512], F32)
    kf = sb("gkf", [32, 512], F32)
    vf = sb("gvf", [64, 512], F32)
    rf = sb("grf", [64, 512], F32)
    state = sb("gstate", [32, 64], F32)
    ge1 = sb("ge1", [128, 32], F32)
    spg = sb("gspg", [128, 32], F32)
    E1 = sb("gE1", [32, 128], F32)
    E2 = sb("gE2", [32, 128], F32)
    ED = sb("gED", [128, 32], F32)
    qb = sb("gqb", [32, 128], F32)
    kb = sb("gkb", [32, 128], F32)
    kup = sb("gkup", [128, 32], F32)
    vtok = sb("gvtok", [128, 64], F32)
    scT = sb("gscT", [128, 128], F32)
    tri_le, tri_gt = C.tri[:, 0:128], C.tri[:, 256:384]
    wlr2_dram = dr["wlr2"]
    p.dma("sp", "cst2", lambda e: e.dma_start(out=wlr2[:], in_=wlr2_dram[:, :]), w=["gwlr2"])
    pe_warmup(C, 7)
    p.dve(lambda e: e.memset(lrT[:], 1.0), w=["glrT"])
    p.dve(lambda e: e.memset(state[:], 0.0), w=["gstate"])
    PS = C.PS

    for T in range(NT):
        hT, htok = front_end(C, T)
        for (c0, n, bank, dst, tok) in ((0, 32, 0, qf, "gqf"), (32, 32, 1, kf, "gkf"), (64, 64, 2, vf, "gvf"),
                                       (128, 64, 3, rf, "grf"), (192, 16, 4, lrT, "glrT")):
            ps = project(C, wb, hT, htok, c0, n, bank)
            p.act(lambda e, dst=dst, n=n, ps=ps: e.copy(out=dst[0:n, :], in_=ps[0:n, :]), r=[("ps", bank)], w=[tok])
        for ci in range(4):
            cs = slice(ci * 128, (ci + 1) * 128)
            p.pe(MM(PS[5][:, 0:32], lrT[:, cs], wlr2[:, :]), r=["glrT", "gwlr2"], w=[("ps", 5)])
            p.act(ACTF(ge1[:], PS[5][:, 0:32], AF.Exp, scale=-1.0), r=[("ps", 5)], w=["ge1"])
            p.act(ACTF(spg[:], ge1[:], AF.Ln, bias=1.0), r=["ge1"], w=["gspg"])
            p.pe(MM(PS[6][0:32, 0:128], spg[:, :], tri_le), r=["gspg", "tri"], w=[("ps", 6)])
            p.pe(MM(PS[5][:, 32:64], tri_gt, spg[:, :]), r=["gspg", "tri"], w=[("ps", 5)])
            p.act(ACTF(E1[:], PS[6][0:32, 0:128], AF.Exp, scale=-1.0 / 16), r=[("ps", 6)], w=["gE1"])
            p.act(ACTF(E2[:], PS[6][0:32, 0:128], AF.Exp, scale=1.0 / 16), r=[("ps", 6)], w=["gE2"])
            p.act(ACTF(ED[:], PS[5][:, 32:64], AF.Exp, scale=-1.0 / 16), r=[("ps", 5)], w=["gED"])
            p.dve(STT(qb[:], qf[:, cs], 32.0 ** -0.5, E1[:], ALU.mult, ALU.mult), r=["gqf", "gE1"], w=["gqb"])
            p.dve(TT(kb[:], kf[:, cs], E2[:], ALU.mult), r=["gkf", "gE2"], w=["gkb"])
            p.pe(MM(PS[5][:, 64:96], kf[:, cs], C.identf[0:32, 0:32]), r=["gkf", "identf"], w=[("ps", 5)])
            p.pe(MM(PS[5][:, 128:192], vf[:, cs], C.identf[0:64, 0:64]), r=["gvf", "identf"], w=[("ps", 5)])
            p.dve(TT(kup[:], PS[5][:, 64:96], ED[:], ALU.mult), r=[("ps", 5), "gED"], w=["gkup"])
            p.dve(CP(vtok[:], PS[5][:, 128:192]), r=[("ps", 5)], w=["gvtok"])
            p.pe(MM(PS[6][:, 256:384], kb[:, :], qb[:, :]), r=["gkb", "gqb"], w=[("ps", 6)])
            p.dve(TT(scT[:], PS[6][:, 256:384], tri_le, ALU.mult), r=[("ps", 6), "tri"], w=["gscT"])
            p.pe(MM(PS[4][0:64, cs], vtok[:, :], scT[:, :], True, False), r=["gvtok", "gscT"], w=[("ps", 4)])
            p.pe(MM(PS[4][0:64, cs], state[:, :], qb[:, :], False, True), r=["gstate", "gqb"], w=[("ps", 4)])
            p.pe(MM(PS[6][0:32, 128:192], kup[:, :], vtok[:, :]), r=["gkup", "gvtok"], w=[("ps", 6)])
            p.dve(STT(state[:], state[:], E1[:, 127:128], PS[6][0:32, 128:192], ALU.mult, ALU.add),
                  r=["gstate", "gE1", ("ps", 6)], w=["gstate"])
        gated_norm(C, PS[4][0:64, :], ("ps", 4), rf[:], "grf", C.pvec[0:64, 2:3], "o_gla", T, 0)


IN_SIZES = (256, 256, 256, 128, 128, 256, 16, 256, 768, 4, 4, 256, 256, 256, 256, 4096)
IN_OFF = [0] + list(np.cumsum(IN_SIZES)[:-1])


def _cols(i, h, n):
    return np.arange(IN_OFF[i] + h * n, IN_OFF[i] + (h + 1) * n)


def prep_B_weights(P, l, h):
    w_in = P["w_in"][l]
    m = {}
    m["gmix"] = np.ascontiguousarray(P["norm_mix_g"][l].reshape(8, 128).T)
    m["w_sb"] = np.ascontiguousarray(w_in[:, np.concatenate([_cols(0, h, 64), _cols(1, h, 64), _cols(2, h, 64)])])
    m["w_dil"] = np.ascontiguousarray(w_in[:, np.concatenate([_cols(12, h, 64), _cols(13, h, 64), _cols(14, h, 64)])])
    m["w_gla"] = np.ascontiguousarray(w_in[:, np.concatenate([_cols(3, h, 32), _cols(4, h, 32), _cols(5, h, 64),
                                                              _cols(7, h, 64), _cols(6, 0, 16)])])
    dn_q = IN_OFF[8] + h * 64 + np.arange(64)
    m["w_dn"] = np.ascontiguousarray(w_in[:, np.concatenate([dn_q, dn_q + 256, dn_q + 512, _cols(11, h, 64),
                                                             _cols(9, h, 1), _cols(10, h, 1)])])
    pv = np.zeros((128, 8), np.float32)
    pv[:64, 0] = P["dil_q_norm_g"][l]
    pv[:64, 1] = P["dil_k_norm_g"][l]
    pv[:64, 2] = P["gla_norm_g"][l]
    pv[:64, 3] = P["dn_norm_g"][l]
    pv[:, 4] = P["dn_a_log"][l, h]
    pv[:, 5] = P["dn_dt_bias"][l, h]
    m["pvec"] = pv
    wl = np.zeros((32, 32), np.float32)
    wl[:16] = P["gla_w_lr2"][l][:, h * 32:(h + 1) * 32]
    wl[16] = P["gla_b_lr"][l][h * 32:(h + 1) * 32]
    m["wlr2"] = wl
    cw = P["dn_conv_w"][l]
    m["dnconv"] = np.ascontiguousarray(
        np.concatenate([cw[s * 256 + h * 64:s * 256 + (h + 1) * 64, :] for s in range(3)], axis=1))
    return m


def phase_dn(C, es2):
    nc, p, S, NT, dr = C.nc, C.p, C.S, C.NT, C.dr
    PS = C.PS

    def sb(name, shape, dt):
        return es2.enter_context(nc.sbuf_tensor(getattr(C, "tag", "") + "s_" + name, shape, dt))

    wb = load_weights(C, es2, "w_dn", 258)
    cw = sb("ncw", [64, 12], F32)
    dnconv_dram = dr["dnconv"]
    p.dma("sp", "cst2", lambda e: e.dma_start(out=cw[:], in_=dnconv_dram[:, :]), w=["ncw"])
    xbuf = [sb("nxb%d" % s_, [64, 515], F32) for s_ in range(3)]
    for s_ in range(3):
        p.dve(lambda e, s_=s_: e.memset(xbuf[s_][:], 0.0), w=[("nxb", s_)])
    y = sb("ny", [64, 512], F32)
    ee = sb("nee", [64, 512], F32)
    ys = sb("nys", [64, 512], F32)
    rs = sb("nrs", [64, 512], F32)
    qT = sb("nqT", [64, 512], F32)
    kT = sb("nkT", [64, 512], F32)
    vT = sb("nvT", [64, 512], F32)
    rf = sb("nrf", [64, 512], F32)
    state = sb("nstate", [64, 64], F32)
    p.dve(lambda e: e.memset(state[:], 0.0), w=["nstate"])
    pe_warmup(C, 7)
    nea = sb("nnea", [128, 1], F32)
    p.act(ACTF(nea[:], C.pvec[:, 4:5], AF.Exp), r=["pvec"], w=["nnea"])
    p.dve(TS(nea[:], nea[:], -1.0, None, ALU.mult), r=["nnea"], w=["nnea"])
    NSET = 2
    SETS = []
    for st_ in range(NSET):
        d_ = dict(
            sc=sb("nsc%d" % st_, [128, 16], F32), gtri=sb("ngtri%d" % st_, [128, 128], F32),
            bdiag=sb("nbdiag%d" % st_, [128, 128], F32), exT=sb("nexT%d" % st_, [128, 128], F32),
            exN=sb("nexN%d" % st_, [128, 128], F32), DBT=sb("nDBT%d" % st_, [128, 128], F32),
            decI=sb("ndecI%d" % st_, [128, 128], F32), eGB=sb("neGB%d" % st_, [64, 128], F32),
            scT=sb("nscT%d" % st_, [128, 128], F32),
            Xb=[sb("nX%d_%d" % (st_, i), [128, 128], F32) for i in range(2)],
            Yb=[sb("nY%d_%d" % (st_, i), [128, 128], F32) for i in range(2)],
            Rb=[sb("nR%d_%d" % (st_, i), [128, 128], F32) for i in range(2)],
            kup=sb("nkup%d" % st_, [128, 64], F32), bS=4 + 2 * st_, bG=5 + 2 * st_, c0=128 + 8 * st_, id=st_)
        SETS.append(d_)
    qin = sb("nqin", [64, 128], F32)
    wT = sb("nwT", [64, 128], F32)
    vnew = sb("nvnew", [128, 64], F32)
    tri_le, tri_lt, tri_gt = C.tri[:, 0:128], C.tri[:, 128:256], C.tri[:, 256:384]
    A_, B_, BETA, G_, GC, NGC, EG, ED_, BG, NB_ = range(10)

    for T in range(NT):
        hT, htok = front_end(C, T)
        for s_, dst in ((0, qT), (1, kT), (2, vT)):
            ps = project(C, wb, hT, htok, 64 * s_, 64, s_)
            xb = xbuf[s_]
            p.act(lambda e, xb=xb, ps=ps: e.copy(out=xb[:, 3:515], in_=ps[0:64, :]), r=[("ps", s_)], w=[("nxb", s_)])
            p.dve(TS(y[:], xb[:, 3:515], cw[:, 4 * s_ + 3:4 * s_ + 4], None, ALU.mult), r=[("nxb", s_), "ncw"], w=["ny"])
            for tap in (2, 1, 0):
                p.dve(STT(y[:], xb[:, tap:tap + 512], cw[:, 4 * s_ + tap:4 * s_ + tap + 1], y[:], ALU.mult, ALU.add),
                      r=[("nxb", s_), "ncw", "ny"], w=["ny"])
            p.dve(CP(xb[:, 0:3], xb[:, 512:515]), r=[("nxb", s_)], w=[("nxb", s_)])
            p.act(ACTF(ee[:], y[:], AF.Exp, scale=-1.0), r=["ny"], w=["nee"])
            p.dve(TS(ee[:], ee[:], 1.0, None, ALU.add), r=["nee"], w=["nee"])
            p.dve(RCP(ee[:], ee[:]), r=["nee"], w=["nee"])
            if s_ == 2:
                p.dve(TT(vT[:], y[:], ee[:], ALU.mult), r=["ny", "nee"], w=["nvT"])
            else:
                p.dve(TT(ys[:], y[:], ee[:], ALU.mult), r=["ny", "nee"], w=["nys"])
                col_rstd(C, ys[:], ["nys"], 64, 1.0, s_, rs[:], "nrs")
                p.dve(STT(dst[:], ys[:], 0.125 if s_ == 0 else 1.0, rs[:], ALU.mult, ALU.mult),
                      r=["nys", "nrs"], w=["nqT" if s_ == 0 else "nkT"])
        ps = project(C, wb, hT, htok, 192, 64, 3)
        p.act(lambda e, ps=ps: e.copy(out=rf[:], in_=ps[0:64, :]), r=[("ps", 3)], w=["nrf"])

        def prep_ops(ci, S_):
            cs = slice(ci * 128, (ci + 1) * 128)
            i_ = S_["id"]
            sc = S_["sc"]
            COL = lambda k: sc[:, k:k + 1]
            tS, tG = ("ps", S_["bS"]), ("ps", S_["bG"])
            PSs, PSg = PS[S_["bS"]], PS[S_["bG"]]
            c0 = S_["c0"]
            nsc = ("nsc", i_)
            ops = []
            A = ops.append

            def ab():
                for ch in range(8):
                    p.pe(MM(PS[2][:, c0:c0 + 2], hT[:, ch, cs], wb[:, ch, 256:258], ch == 0, ch == 7), r=[htok, "wb"],
                         w=[("ps", 2)])
            A(ab)
            A(lambda: p.act(ACTF(COL(B_), PS[2][:, c0 + 1:c0 + 2], AF.Exp, scale=-1.0), r=[("ps", 2)], w=[nsc]))
            A(lambda: p.dve(TS(COL(B_), COL(B_), 1.0, None, ALU.add), r=[nsc], w=[nsc]))
            A(lambda: p.dve(RCP(COL(BETA), COL(B_)), r=[nsc], w=[nsc]))
            A(lambda: p.act(ACTF(COL(A_), PS[2][:, c0:c0 + 1], AF.Exp, bias=C.pvec[:, 5:6]), r=[("ps", 2), "pvec"], w=[nsc]))
            A(lambda: p.act(ACTF(COL(A_), COL(A_), AF.Ln, bias=1.0), r=[nsc], w=[nsc]))
            A(lambda: p.dve(TT(COL(G_), COL(A_), nea[:], ALU.mult), r=[nsc, "nnea"], w=[nsc]))
            A(lambda: p.dve(TS(COL(NB_), COL(BETA), -1.0, None, ALU.mult), r=[nsc], w=[nsc]))
            A(lambda: p.pe(MM(PS[2][:, c0 + 2:c0 + 3], tri_le, COL(G_)), r=["tri", nsc], w=[("ps", 2)]))
            A(lambda: p.pe(MM(PS[2][:, c0 + 3:c0 + 4], tri_gt, COL(G_)), r=["tri", nsc], w=[("ps", 2)]))
            A(lambda: p.dve(TS(S_["gtri"][:], tri_le, COL(G_), None, ALU.mult), r=["tri", nsc], w=[("ngtri", i_)]))
            A(lambda: p.dve(TS(S_["bdiag"][:], C.identf[:, :], COL(BETA), None, ALU.mult), r=["identf", nsc],
                            w=[("nbdiag", i_)]))
            A(lambda: p.pe(MM(PSg[:, 0:128], C.onesf[:, :], S_["gtri"][:, :]), r=["onesf", ("ngtri", i_)], w=[tG]))
            A(lambda: p.pe(MM(PSg[:, 128:256], C.onesf[:, :], S_["bdiag"][:, :]), r=["onesf", ("nbdiag", i_)], w=[tG]))
            A(lambda: p.dve(CP(COL(GC), PS[2][:, c0 + 2:c0 + 3]), r=[("ps", 2)], w=[nsc]))
            A(lambda: p.act(ACTF(COL(EG), PS[2][:, c0 + 2:c0 + 3], AF.Exp), r=[("ps", 2)], w=[nsc]))
            A(lambda: p.act(ACTF(COL(ED_), PS[2][:, c0 + 3:c0 + 4], AF.Exp), r=[("ps", 2)], w=[nsc]))
            A(lambda: p.dve(TT(COL(BG), COL(BETA), COL(EG), ALU.mult), r=[nsc], w=[nsc]))
            exT, exN, DBT, decI, eGB, scT = S_["exT"], S_["exN"], S_["DBT"], S_["decI"], S_["eGB"], S_["scT"]
            A(lambda: p.dve(TS(exT[:], PSg[:, 0:128], COL(GC), 0.0, ALU.subtract, ALU.min), r=[tG, nsc], w=[("nexT", i_)]))
            A(lambda: p.dve(TS(exN[:], PSg[:, 0:128], COL(GC), 0.0, ALU.subtract, ALU.max), r=[tG, nsc], w=[("nexN", i_)]))
            A(lambda: p.act(ACTF(exT[:], exT[:], AF.Exp), r=[("nexT", i_)], w=[("nexT", i_)]))
            A(lambda: p.act(ACTF(exN[:], exN[:], AF.Exp, scale=-1.0), r=[("nexN", i_)], w=[("nexN", i_)]))
            A(lambda: p.dve(TT(decI[:], exT[:], tri_le, ALU.mult), r=[("nexT", i_), "tri"], w=[("ndecI", i_)]))
            A(lambda: p.dve(TT(DBT[:], exT[:], tri_lt, ALU.mult), r=[("nexT", i_), "tri"], w=[("nDBT", i_)]))
            A(lambda: p.dve(TT(DBT[:], DBT[:], PSg[:, 128:256], ALU.mult), r=[("nDBT", i_), tG], w=[("nDBT", i_)]))
            A(lambda: p.dve(TT(exN[:], exN[:], tri_gt, ALU.mult), r=[("nexN", i_), "tri"], w=[("nexN", i_)]))
            A(lambda: p.act(ACTF(eGB[:], PSg[0:64, 0:128], AF.Exp), r=[tG], w=[("neGB", i_)]))
            A(lambda: p.pe(MM(PSg[:, 256:384], kT[:, cs], kT[:, cs]), r=["nkT"], w=[tG]))
            A(lambda: p.pe(MM(PSg[:, 384:512], kT[:, cs], qT[:, cs]), r=["nkT", "nqT"], w=[tG]))
            Xb, Yb, Rb, kup = S_["Xb"], S_["Yb"], S_["Rb"], S_["kup"]
            A(lambda: p.dve(STT(Yb[0][:], PSg[:, 256:384], -1.0, DBT[:], ALU.mult, ALU.mult), r=[tG, ("nDBT", i_)],
                            w=[("nY", i_, 0)]))
            A(lambda: p.dve(STT(Xb[0][:], PSg[:, 256:384], COL(NB_), exN[:], ALU.mult, ALU.mult),
                            r=[tG, nsc, ("nexN", i_)], w=[("nX", i_, 0)]))
            A(lambda: p.dve(TT(scT[:], PSg[:, 384:512], decI[:], ALU.mult), r=[tG, ("ndecI", i_)], w=[("nscT", i_)]))
            A(lambda: p.pe(MM(PSs[:, 384:448], kT[:, cs], C.identf[0:64, 0:64]), r=["nkT", "identf"], w=[tS]))
            A(lambda: p.pe(MM(PSs[:, 448:512], vT[:, cs], C.identf[0:64, 0:64]), r=["nvT", "identf"], w=[tS]))
            A(lambda: p.dve(TS(Rb[0][:, 0:64], PSs[:, 448:512], COL(BETA), None, ALU.mult), r=[tS, nsc], w=[("nR", i_, 0)]))
            A(lambda: p.dve(TS(Rb[0][:, 64:128], PSs[:, 384:448], COL(BG), None, ALU.mult), r=[tS, nsc], w=[("nR", i_, 0)]))
            A(lambda: p.dve(TS(kup[:], PSs[:, 384:448], COL(ED_), None, ALU.mult), r=[tS, nsc], w=[("nkup", i_)]))
            for lv in range(7):
                a, b = lv % 2, (lv + 1) % 2
                A(lambda a=a: p.pe(MM(PSs[:, 0:128], Yb[a][:, :], Rb[a][:, :]), r=[("nY", i_, a), ("nR", i_, a)], w=[tS]))
                if lv < 6:
                    A(lambda a=a: p.pe(MM(PSs[:, 128:256], Yb[a][:, :], Xb[a][:, :]), r=[("nY", i_, a), ("nX", i_, a)], w=[tS]))
                    A(lambda a=a: p.pe(MM(PSs[:, 256:384], Xb[a][:, :], Yb[a][:, :]), r=[("nY", i_, a), ("nX", i_, a)], w=[tS]))
                A(lambda a=a, b=b: p.dve(TT(Rb[b][:], Rb[a][:], PSs[:, 0:128], ALU.add), r=[("nR", i_, a), tS], w=[("nR", i_, b)]))
                if lv < 6:
                    A(lambda b=b: p.act(lambda e: e.copy(out=Xb[b][:], in_=PSs[:, 128:256]), r=[tS], w=[("nX", i_, b)]))
                    A(lambda b=b: p.act(lambda e: e.copy(out=Yb[b][:], in_=PSs[:, 256:384]), r=[tS], w=[("nY", i_, b)]))
            return ops

        def scan_ops(ci, S_):
            cs = slice(ci * 128, (ci + 1) * 128)
            i_ = S_["id"]
            Rf, eGB, scT, kup = S_["Rb"][1], S_["eGB"], S_["scT"], S_["kup"]
            p.pe(MM(PS[1][0:64, 128:256], Rf[:, 64:128], C.identf[:, :]), r=[("nR", i_, 1), "identf"], w=[("ps", 1)])
            p.act(lambda e: e.copy(out=wT[:], in_=PS[1][0:64, 128:256]), r=[("ps", 1)], w=["nwT"])
            p.pe(MM(PS[1][:, 64:128], wT[:, :], state[:, :]), r=["nwT", "nstate"], w=[("ps", 1)])
            p.dve(TT(vnew[:], Rf[:, 0:64], PS[1][:, 64:128], ALU.subtract), r=[("nR", i_, 1), ("ps", 1)], w=["nvnew"])
            p.dve(TT(qin[:], qT[:, cs], eGB[:], ALU.mult), r=["nqT", ("neGB", i_)], w=["nqin"])
            p.pe(MM(PS[0][0:64, cs], state[:, :], qin[:, :], True, False), r=["nstate", "nqin"], w=[("ps", 0)])
            p.pe(MM(PS[0][0:64, cs], vnew[:, :], scT[:, :], False, True), r=["nvnew", ("nscT", i_)], w=[("ps", 0)])
            p.pe(MM(PS[1][0:64, 0:64], kup[:, :], vnew[:, :]), r=[("nkup", i_), "nvnew"], w=[("ps", 1)])
            p.dve(STT(state[:], state[:], eGB[:, 127:128], PS[1][0:64, 0:64], ALU.mult, ALU.add),
                  r=["nstate", ("neGB", i_), ("ps", 1)], w=["nstate"])

        for pair in range(2):
            la = prep_ops(2 * pair, SETS[0])
            lb = prep_ops(2 * pair + 1, SETS[1])
            for k in range(max(len(la), len(lb))):
                if k < len(la):
                    la[k]()
                if k < len(lb):
                    lb[k]()
            scan_ops(2 * pair, SETS[0])
            scan_ops(2 * pair + 1, SETS[1])
        gated_norm(C, PS[0][0:64, :], ("ps", 0), rf[:], "nrf", C.pvec[0:64, 3:4], "o_dn", T, 3)


def build_C(TC):
    nc = bass.Bass("TRN2", target_bir_lowering=False)
    dr = {}
    dr["xT"] = nc.dram_tensor("xT", [D_MODEL, TC], F32, kind="ExternalInput").ap()
    dr["oT"] = nc.dram_tensor("oT", [1024, TC], F32, kind="ExternalInput").ap()
    dr["gmix"] = nc.dram_tensor("gmix", [128, 8], F32, kind="ExternalInput").ap()
    dr["gmlp"] = nc.dram_tensor("gmlp", [128, 8], F32, kind="ExternalInput").ap()
    dr["w_gate"] = nc.dram_tensor("w_gate", [1024, 4096], F32, kind="ExternalInput").ap()
    dr["w_branch"] = nc.dram_tensor("w_branch", [1024, 1024], F32, kind="ExternalInput").ap()
    dr["w_out"] = nc.dram_tensor("w_out", [1024, 1024], F32, kind="ExternalInput").ap()
    dr["w_mlp_in"] = nc.dram_tensor("w_mlp_in", [1024, 4096], F32, kind="ExternalInput").ap()
    dr["w_mlp_out"] = nc.dram_tensor("w_mlp_out", [4096, 1024], F32, kind="ExternalInput").ap()
    dr["ones"] = nc.dram_tensor("c_ones", [128, 128], F32, kind="ExternalInput").ap()
    dr["x1"] = nc.dram_tensor("x1_scratch", [D_MODEL, TC], F32, kind="Internal").ap()
    dr["xo"] = nc.dram_tensor("xoT", [D_MODEL, TC], F32, kind="ExternalOutput").ap()

    def fm(ap):
        return ap.rearrange("(c p) s -> p c s", p=128)

    with ExitStack() as es:
        p = Prog(nc)
        PS = [es.enter_context(nc.psum_tensor("ps%d" % k, [128, 512], F32)) for k in range(8)]

        def sb0(name, shape, dt):
            return es.enter_context(nc.sbuf_tensor("s_" + name, shape, dt))

        ones = sb0("ones", [128, 128], BF16)
        gmix = sb0("gmix", [128, 8], F32)
        gmlp = sb0("gmlp", [128, 8], F32)
        p.dma("pool", "cst", lambda e: e.dma_start(out=ones[:], in_=dr["ones"][:, :]), w=["ones"])
        p.dma("sp", "cst2", lambda e: e.dma_start(out=gmix[:], in_=dr["gmix"][:, :]), w=["gmix"])
        p.dma("sp", "cst2", lambda e: e.dma_start(out=gmlp[:], in_=dr["gmlp"][:, :]), w=["gmlp"])

        def norm_tile(xt, xtok, n, gcol, sq, rt, rstd, hT, htok):
            p.act(ACTF(sq[:], xt[:], AF.Square), r=[xtok], w=["sq"])
            for ch in range(8):
                p.pe(MM(PS[7][:, 0:n], ones[:, :], sq[:, ch, :], ch == 0, ch == 7), r=["sq", "ones"], w=[("ps", 7)])
            p.act(ACTF(rt[:], PS[7][:, 0:n], AF.Ln, scale=1.0 / D_MODEL, bias=EPS), r=[("ps", 7)], w=["rt"])
            p.act(ACTF(rstd[:], rt[:], AF.Exp, scale=-0.5), r=["rt"], w=["rstd"])
            for ch in range(8):
                p.dve(STT(hT[:, ch, :], xt[:, ch, :], gcol[:, ch:ch + 1], rstd[:], ALU.mult, ALU.mult),
                      r=[xtok, "rstd", "gmix", "gmlp"], w=[htok])

        with ExitStack() as es2:
            def sb(name, shape, dt):
                return es2.enter_context(nc.sbuf_tensor("s_" + name, shape, dt))
            Wg = sb("Wg", [128, 8, 4096], BF16)
            Wb = sb("Wb", [128, 8, 1024], BF16)
            Wo = sb("Wo", [128, 8, 1024], BF16)
            for kc in range(8):
                p.dma("pool", "wg", lambda e, kc=kc: e.dma_start(out=Wg[:, kc, :], in_=dr["w_gate"][kc * 128:(kc + 1) * 128, :]),
                      w=["Wg"])
            p.dma("pool", "wb", lambda e: e.dma_start(out=Wb[:], in_=fm(dr["w_branch"])), w=["Wb"])
            p.dma("pool", "wb", lambda e: e.dma_start(out=Wo[:], in_=fm(dr["w_out"])), w=["Wo"])
            xt = sb("xt", [128, 8, 512], F32)
            ob = sb("ob", [128, 8, 512], BF16)
            sq = sb("sq", [128, 8, 512], BF16)
            rt = sb("rt", [128, 512], F32)
            rstd = sb("rstd", [128, 512], F32)
            hT = sb("hT", [128, 8, 512], BF16)
            merged = sb("merged", [128, 8, 512], BF16)
            gs = [sb("gs%d" % i, [128, 512], F32) for i in range(2)]
            macc = sb("macc", [128, 512], F32)
            mtmp = sb("mtmp", [128, 512], F32)
            for t in range(TC // 512):
                ts_ = slice(t * 512, (t + 1) * 512)
                p.dma("sp", "cx", lambda e, ts_=ts_: e.dma_start(out=xt[:], in_=fm(dr["xT"])[:, :, ts_]), w=["xt"])
                p.dma("pool", "co", lambda e, ts_=ts_: e.dma_start(out=ob[:], in_=fm(dr["oT"])[:, :, ts_]), w=["ob"])
                norm_tile(xt, "xt", 512, gmix, sq, rt, rstd, hT, "hT")
                k = 0
                for og in range(8):
                    osl = slice(og * 128, (og + 1) * 128)
                    for n in range(4):
                        a, b = k % 3, 3 + (k % 3)
                        k += 1
                        for kc in range(8):
                            p.pe(MM(PS[a][:, :], Wg[:, kc, n * 1024 + og * 128:n * 1024 + (og + 1) * 128], hT[:, kc, :],
                                    kc == 0, kc == 7), r=["Wg", "hT"], w=[("ps", a)])
                        g_ = gs[n % 2]
                        p.act(ACTF(g_[:], PS[a][:, :], AF.Sigmoid), r=[("ps", a)], w=[("gs", n % 2)])
                        for kc in range(2):
                            p.pe(MM(PS[b][:, :], Wb[:, n * 2 + kc, osl], ob[:, n * 2 + kc, :], kc == 0, kc == 1),
                                 r=["Wb", "ob"], w=[("ps", b)])
                        if n == 0:
                            p.dve(TT(macc[:], PS[b][:, :], g_[:], ALU.mult), r=[("ps", b), ("gs", n % 2)], w=["macc"])
                        else:
                            p.dve(TT(mtmp[:], PS[b][:, :], g_[:], ALU.mult), r=[("ps", b), ("gs", n % 2)], w=["mtmp"])
                            if n < 3:
                                p.dve(TT(macc[:], macc[:], mtmp[:], ALU.add), r=["macc", "mtmp"], w=["macc"])
                            else:
                                p.dve(TT(merged[:, og, :], macc[:], mtmp[:], ALU.add), r=["macc", "mtmp"], w=["merged"])
                for og in range(8):
                    osl = slice(og * 128, (og + 1) * 128)
                    for kc in range(8):
                        p.pe(MM(PS[6][:, :], Wo[:, kc, osl], merged[:, kc, :], kc == 0, kc == 7), r=["Wo", "merged"],
                             w=[("ps", 6)])
                    p.dve(TT(xt[:, og, :], xt[:, og, :], PS[6][:, :], ALU.add), r=["xt", ("ps", 6)], w=["xt"])
                p.dma("pool", "c1o", lambda e, ts_=ts_: e.dma_start(out=fm(dr["x1"])[:, :, ts_], in_=xt[:]), r=["xt"],
                      w=["x1dram"])
        p.barrier()

        with ExitStack() as es2:
            def sb(name, shape, dt):
                return es2.enter_context(nc.sbuf_tensor("s2_" + name, shape, dt))
            W1 = sb("W1", [128, 8, 4096], BF16)
            W2 = sb("W2", [128, 32, 1024], BF16)
            for kc in range(8):
                p.dma("pool", "wg", lambda e, kc=kc: e.dma_start(out=W1[:, kc, :], in_=dr["w_mlp_in"][kc * 128:(kc + 1) * 128, :]),
                      w=["W1"])
            for q4 in range(4):
                p.dma("pool", "wb", lambda e, q4=q4: e.dma_start(
                    out=W2[:, q4 * 8:(q4 + 1) * 8, :], in_=fm(dr["w_mlp_out"][q4 * 1024:(q4 + 1) * 1024, :])), w=["W2"])
            NB = 256
            xt_2 = sb("xt", [128, 8, NB], F32)
            sq_2 = sb("sq", [128, 8, NB], BF16)
            rt_2 = sb("rt", [128, NB], F32)
            rstd_2 = sb("rstd", [128, NB], F32)
            hT_2 = sb("hT", [128, 8, NB], BF16)
            aT = sb("aT", [128, 32, NB], BF16)
            rb = [sb("rb%d" % i, [128, NB], F32) for i in range(2)]
            for t in range(TC // NB):
                ts_ = slice(t * NB, (t + 1) * NB)
                p.dma("sp", "cx", lambda e, ts_=ts_: e.dma_start(out=xt_2[:], in_=fm(dr["x1"])[:, :, ts_]), r=["x1dram"], w=["xt2"])
                norm_tile(xt_2, "xt2", NB, gmlp, sq_2, rt_2, rstd_2, hT_2, "hT2")
                for fg in range(32):
                    a = fg % 3
                    for kc in range(8):
                        p.pe(MM(PS[a][:, 0:NB], W1[:, kc, fg * 128:(fg + 1) * 128], hT_2[:, kc, :], kc == 0, kc == 7),
                             r=["W1", "hT2"], w=[("ps", a)])
                    r_ = rb[fg % 2]
                    p.act(ACTF(r_[:], PS[a][:, 0:NB], AF.Relu), r=[("ps", a)], w=[("rb", fg % 2)])
                    p.dve(TT(aT[:, fg, :], r_[:], r_[:], ALU.mult), r=[("rb", fg % 2)], w=["aT"])
                for og in range(8):
                    b = 3 + og % 2
                    for fc in range(32):
                        p.pe(MM(PS[b][:, 0:NB], W2[:, fc, og * 128:(og + 1) * 128], aT[:, fc, :], fc == 0, fc == 31),
                             r=["W2", "aT"], w=[("ps", b)])
                    p.dve(TT(xt_2[:, og, :], xt_2[:, og, :], PS[b][:, 0:NB], ALU.add), r=["xt2", ("ps", b)], w=["xt2"])
                p.dma("pool", "c2o", lambda e, ts_=ts_: e.dma_start(out=fm(dr["xo"])[:, :, ts_], in_=xt_2[:]), r=["xt2"],
                      w=["xodram"])
        p.barrier()
        p.emit(es)
        print("C program ops:", p.stats)
    return nc


def prep_C_weights(P, l):
    m = {}
    m["gmix"] = np.ascontiguousarray(P["norm_mix_g"][l].reshape(8, 128).T)
    m["gmlp"] = np.ascontiguousarray(P["norm_mlp_g"][l].reshape(8, 128).T)
    m["w_gate"] = np.ascontiguousarray(P["w_in"][l][:, IN_OFF[15]:])
    m["w_branch"] = np.ascontiguousarray(P["w_branch"][l].reshape(1024, 1024))
    m["w_out"] = P["w_out"][l]
    m["w_mlp_in"] = P["w_mlp_in"][l]
    m["w_mlp_out"] = P["w_mlp_out"][l]
    m["c_ones"] = np.ones((128, 128), np.float32)
    return m


SEQ = 16384
BATCH = 2
DEPTH = 4
RG = [[0, 1, 2, 3], [4, 5, 6, 7]]
B_WEIGHTS = (("w_sb", [D_MODEL, 192]), ("w_dil", [D_MODEL, 192]), ("w_gla", [D_MODEL, 208]), ("w_dn", [D_MODEL, 258]),
             ("pvec", [128, 8]), ("wlr2", [32, 32]), ("dnconv", [64, 12]), ("gmix", [128, 8]))
C_WEIGHTS = (("gmlp", [128, 8]), ("w_gate", [1024, 4096]), ("w_branch", [1024, 1024]), ("w_out", [1024, 1024]),
             ("w_mlp_in", [1024, 4096]), ("w_mlp_out", [4096, 1024]))


def fm(ap):
    return ap.rearrange("(c p) s -> p c s", p=128)


def build_fused(S=SEQ, depth=DEPTH, debug=False):
    TC = S // 4
    nc = bass.Bass("TRN2", target_bir_lowering=False, num_devices=8)
    dr = {}
    ext = {}
    ext["xT"] = nc.dram_tensor("xT", [D_MODEL, TC], F32, kind="ExternalInput").ap()
    for name, shp in B_WEIGHTS + C_WEIGHTS:
        ext[name] = nc.dram_tensor(name, [depth] + shp, F32, kind="ExternalInput").ap()
    for name, shp in const_shapes(S).items():
        dr[name] = nc.dram_tensor("c_" + name, shp, F32, kind="ExternalInput").ap()
    xo = nc.dram_tensor("xoT", [D_MODEL, TC], F32, kind="ExternalOutput").ap()
    HCH = max(1, (D_MODEL * TC * 2) // (1 << 20))
    HW = TC // HCH
    OCH = max(1, (64 * S * 2) // (1 << 20))
    CW = TC // OCH
    hsrc = [[nc.dram_tensor("hsrc%d_%d" % (i, k), [D_MODEL, HW], BF16, kind="Internal").ap() for k in range(HCH)]
            for i in range(2)]
    hall = [[nc.dram_tensor("hall%d_%d" % (i, k), [4 * D_MODEL, HW], BF16, kind="Internal").ap() for k in range(HCH)]
            for i in range(2)]
    osrc = [[[nc.dram_tensor("osrc%d_%d_%d" % (i, n, k), [64, 4 * CW], BF16, kind="Internal").ap() for k in range(OCH)]
             for n in range(4)] for i in range(2)]
    oall = [[[nc.dram_tensor("oall%d_%d_%d" % (i, n, k), [256, 4 * CW], BF16, kind="Internal").ap() for k in range(OCH)]
             for n in range(4)] for i in range(2)]
    x1d = nc.dram_tensor("x1_scratch", [D_MODEL, TC], F32, kind="Internal").ap()
    xres = nc.dram_tensor("xres", [D_MODEL, TC], F32, kind="Internal").ap()
    if debug:
        dbg_hall = nc.dram_tensor("dbg_hall", [4 * D_MODEL, HW], BF16, kind="ExternalOutput").ap()
        dbg_oall = [nc.dram_tensor("dbg_oall%d" % n, [256, 4 * CW], BF16, kind="ExternalOutput").ap() for n in range(4)]
        dbg_x1 = nc.dram_tensor("dbg_x1", [D_MODEL, TC], F32, kind="ExternalOutput").ap()

    with ExitStack() as es:
        p = Prog(nc)
        C = Ctx()
        C.nc, C.p, C.S, C.NT, C.dr = nc, p, S, S // 512, dr
        C.fused = True
        C.HW, C.CW = HW, CW
        C.rank = nc.partition_id() % 4

        def sb(name, shape, dt):
            return es.enter_context(nc.sbuf_tensor("s_" + name, shape, dt))

        C.PS = [es.enter_context(nc.psum_tensor("ps%d" % k, [128, 512], F32)) for k in range(8)]
        PS = C.PS
        C.ident = sb("ident", [128, 128], BF16)
        C.ones = sb("ones", [128, 128], BF16)
        C.identf = sb("identf", [128, 128], F32)
        C.onesf = sb("onesf", [128, 128], F32)
        C.tri = sb("tri", [128, 512], F32)
        C.tri_bf = sb("tri_bf", [128, 512], BF16)
        p.dma("pool", "cst", lambda e: e.dma_start(out=C.tri_bf[:], in_=dr["tri"][:, :]), w=["tri_bf"])
        C.gmix = sb("gmix", [128, 8], F32)
        C.gmlp = sb("gmlp", [128, 8], F32)
        C.pvec = sb("pvec", [128, 8], F32)
        p.dma("pool", "cst", lambda e: e.dma_start(out=C.ident[:], in_=dr["ident"][:, :]), w=["ident"])
        p.dma("pool", "cst", lambda e: e.dma_start(out=C.ones[:], in_=dr["ones"][:, :]), w=["ones"])
        p.dma("sp", "cst2", lambda e: e.dma_start(out=C.identf[:], in_=dr["ident"][:, :]), w=["identf"])
        p.dma("sp", "cst2", lambda e: e.dma_start(out=C.onesf[:], in_=dr["ones"][:, :]), w=["onesf"])
        p.dma("sp", "cst2", lambda e: e.dma_start(out=C.tri[:], in_=dr["tri"][:, :]), w=["tri"])

        def norm_tile(xt, xtok, n, gcol, sq, rt, rstd, hT, htok):
            p.act(ACTF(sq[:], xt[:], AF.Square), r=[xtok], w=["sq"])
            for ch in range(8):
                p.pe(MM(PS[7][:, 0:n], C.ones[:, :], sq[:, ch, :], ch == 0, ch == 7), r=["sq", "ones"], w=[("ps", 7)])
            p.act(ACTF(rt[:], PS[7][:, 0:n], AF.Ln, scale=1.0 / D_MODEL, bias=EPS), r=[("ps", 7)], w=["rt"])
            p.act(ACTF(rstd[:], rt[:], AF.Exp, scale=-0.5), r=["rt"], w=["rstd"])
            for ch in range(8):
                if gcol is None:
                    p.dve(TT(hT[:, ch, :], xt[:, ch, :], rstd[:], ALU.mult), r=[xtok, "rstd"], w=[htok])
                else:
                    p.dve(STT(hT[:, ch, :], xt[:, ch, :], gcol[:, ch:ch + 1], rstd[:], ALU.mult, ALU.mult),
                          r=[xtok, "rstd", "gmix", "gmlp"], w=[htok])

        ncoll = [0]

        def cc_chan():
            ncoll[0] += 1
            return "cc%d" % (ncoll[0] % 4)

        def gather_h(par, k):
            p.coll(cc_chan(), lambda e: e.collective_compute("AllGather", ALU.bypass, replica_groups=RG,
                                                        ins=[hsrc[par][k][:, :]], outs=[hall[par][k][:, :]]),
                   r=[("hsrc", k)], w=[("hall", k)])

        with ExitStack() as es2:
            def sb2(name, shape, dt):
                return es2.enter_context(nc.sbuf_tensor("s0_" + name, shape, dt))
            xt0 = sb2("xt", [128, 8, 512], F32)
            sq0 = sb2("sq", [128, 8, 512], BF16)
            rt0 = sb2("rt", [128, 512], F32)
            rstd0 = sb2("rstd", [128, 512], F32)
            h0 = sb2("h", [128, 8, 512], BF16)
            for t in range(TC // 512):
                ts_ = slice(t * 512, (t + 1) * 512)
                p.dma("sp", "cx", lambda e, ts_=ts_: e.dma_start(out=xt0[:], in_=fm(ext["xT"])[:, :, ts_]), w=["xt"])
                norm_tile(xt0, "xt", 512, None, sq0, rt0, rstd0, h0, "h0")
                k_, col = divmod(t * 512, HW)
                p.dma("sp", "ch", lambda e, k_=k_, col=col: e.dma_start(out=fm(hsrc[0][k_])[:, :, col:col + 512], in_=h0[:]),
                      r=["h0"], w=[("hsrc", k_)])
                if (t * 512 + 512) % HW == 0:
                    gather_h(0, k_)
        if debug and depth == 1:
            p.dma("pool", "dbg", lambda e: e.dma_start(out=dbg_hall[:, :], in_=hall[0][0][:, :]), r=[("hall", 0)], w=["dbgh"])

        for l in range(depth):
            par = l % 2
            C.tag = "L%d" % l
            with ExitStack() as esB:
                def sbB(name, shape, dt):
                    return esB.enter_context(nc.sbuf_tensor("sB%d_%s" % (l, name), shape, dt))
                C.hT = [sbB("hT%d" % i, [128, 8, 512], BF16) for i in range(2)]
                C.cr_sq = sbB("cr_sq", [64, 512], F32)
                C.cr_l = sbB("cr_l", [64, 512], F32)
                C.gn_of = sbB("gn_of", [64, 512], F32)
                C.gn_rs = sbB("gn_rs", [64, 512], F32)
                C.gn_er = sbB("gn_er", [64, 512], F32)
                C.gn_sg = sbB("gn_sg", [64, 512], F32)
                C.gn_t1 = sbB("gn_t1", [64, 512], F32)
                C.gn_out = [sbB("gn_out%d" % i, [64, 512], BF16) for i in range(2)]
                C.hall = hall[par]
                for name, _ in B_WEIGHTS:
                    dr[name] = ext[name][l]
                C.osrc = {"o_sb": osrc[par][0], "o_gla": osrc[par][1], "o_dn": osrc[par][2], "o_dil": osrc[par][3]}
                p.dma("sp", "cst2", lambda e, l=l: e.dma_start(out=C.gmix[:], in_=ext["gmix"][l]), w=["gmix"])
                p.dma("sp", "cst2", lambda e, l=l: e.dma_start(out=C.pvec[:], in_=ext["pvec"][l]), w=["pvec"])
                for ph, fn, n, otok in (("sb", phase_sb, 0, "out_sb"), ("gla", phase_gla, 1, "gn_dram"),
                                        ("dn", phase_dn, 2, "gn_dram"), ("dil", phase_dil, 3, "od_dram")):
                    with ExitStack() as es2:
                        fn(C, es2)
                    p.barrier()
                    for k in range(OCH):
                        p.coll(cc_chan(), lambda e, n=n, par=par, k=k: e.collective_compute(
                            "AllGather", ALU.bypass, replica_groups=RG, ins=[osrc[par][n][k][:, :]],
                            outs=[oall[par][n][k][:, :]]), r=[otok], w=[("oall", n, k)])
            if debug and l == depth - 1:
                for n in range(4):
                    p.dma("pool", "dbg", lambda e, n=n, par=par: e.dma_start(out=dbg_oall[n][:, :], in_=oall[par][n][0][:, :]),
                          r=[("oall", n, 0)], w=["dbgo"])
            p.barrier(new_epoch=True)

            xin = ext["xT"] if l == 0 else xres
            xout = xo if l == depth - 1 else xres
            with ExitStack() as es2:
                def sb(name, shape, dt, l=l):
                    return es2.enter_context(nc.sbuf_tensor("sC%d_%s" % (l, name), shape, dt))
                Wg = sb("Wg", [128, 8, 4096], BF16)
                Wb = sb("Wb", [128, 8, 1024], BF16)
                Wo = sb("Wo", [128, 8, 1024], BF16)
                p.dma("sp", "cst2", lambda e, l=l: e.dma_start(out=C.gmlp[:], in_=ext["gmlp"][l]), w=["gmlp"])
                for kc in range(8):
                    p.dma("pool", "wg", lambda e, kc=kc, l=l, Wg=Wg: e.dma_start(
                        out=Wg[:, kc, :], in_=ext["w_gate"][l][kc * 128:(kc + 1) * 128, :]), w=["Wg"])
                p.dma("pool", "wb", lambda e, l=l, Wb=Wb: e.dma_start(out=Wb[:], in_=fm(ext["w_branch"][l])), w=["Wb"])
                p.dma("pool", "wb", lambda e, l=l, Wo=Wo: e.dma_start(out=Wo[:], in_=fm(ext["w_out"][l])), w=["Wo"])
                xt = sb("xt", [128, 8, 512], F32)
                ob = sb("ob", [128, 8, 512], BF16)
                sq = sb("sq", [128, 8, 512], BF16)
                rt = sb("rt", [128, 512], F32)
                rstd = sb("rstd", [128, 512], F32)
                hT = sb("hT", [128, 8, 512], BF16)
                merged = sb("merged", [128, 8, 512], BF16)
                gs = [sb("gs%d" % i, [128, 512], F32) for i in range(2)]
                macc = sb("macc", [128, 512], F32)
                mtmp = sb("mtmp", [128, 512], F32)
                for t in range(TC // 512):
                    ts_ = slice(t * 512, (t + 1) * 512)
                    p.dma("sp", "cx", lambda e, ts_=ts_, xt=xt, xin=xin: e.dma_start(out=xt[:], in_=fm(xin)[:, :, ts_]),
                          r=["xres"], w=["xt"])
                    k_, col = divmod(t * 512, CW)
                    for n in range(4):
                        src = oall[par][n][k_].rearrange("(kc p) s -> p kc s", p=128)
                        p.dma("sp", "co", lambda e, col=col, ob=ob, n=n, src=src: e.dma_start(
                            out=ob[:, 2 * n:2 * n + 2, :],
                            in_=src[:, :, bass.ds(C.rank * CW + col, 512)]), r=[("oall", n, k_)], w=["ob"])
                    norm_tile(xt, "xt", 512, C.gmix, sq, rt, rstd, hT, "hT")
                    k = 0
                    for og in range(8):
                        osl = slice(og * 128, (og + 1) * 128)
                        for n in range(4):
                            a, b = k % 3, 3 + (k % 3)
                            k += 1
                            for kc in range(8):
                                p.pe(MM(PS[a][:, :], Wg[:, kc, n * 1024 + og * 128:n * 1024 + (og + 1) * 128], hT[:, kc, :],
                                        kc == 0, kc == 7), r=["Wg", "hT"], w=[("ps", a)])
                            g_ = gs[n % 2]
                            p.act(ACTF(g_[:], PS[a][:, :], AF.Sigmoid), r=[("ps", a)], w=[("gs", n % 2)])
                            for kc in range(2):
                                p.pe(MM(PS[b][:, :], Wb[:, n * 2 + kc, osl], ob[:, n * 2 + kc, :], kc == 0, kc == 1),
                                     r=["Wb", "ob"], w=[("ps", b)])
                            if n == 0:
                                p.dve(TT(macc[:], PS[b][:, :], g_[:], ALU.mult), r=[("ps", b), ("gs", n % 2)], w=["macc"])
                            else:
                                p.dve(TT(mtmp[:], PS[b][:, :], g_[:], ALU.mult), r=[("ps", b), ("gs", n % 2)], w=["mtmp"])
                                if n < 3:
                                    p.dve(TT(macc[:], macc[:], mtmp[:], ALU.add), r=["macc", "mtmp"], w=["macc"])
                                else:
                                    p.dve(TT(merged[:, og, :], macc[:], mtmp[:], ALU.add), r=["macc", "mtmp"], w=["merged"])
                    for og in range(8):
                        osl = slice(og * 128, (og + 1) * 128)
                        for kc in range(8):
                            p.pe(MM(PS[6][:, :], Wo[:, kc, osl], merged[:, kc, :], kc == 0, kc == 7), r=["Wo", "merged"],
                                 w=[("ps", 6)])
                        p.dve(TT(xt[:, og, :], xt[:, og, :], PS[6][:, :], ALU.add), r=["xt", ("ps", 6)], w=["xt"])
                    p.dma("sp", "c1o", lambda e, ts_=ts_, xt=xt: e.dma_start(out=fm(x1d)[:, :, ts_], in_=xt[:]), r=["xt"],
                          w=["x1dram"])
            p.barrier()
            if debug and l == depth - 1:
                p.dma("pool", "dbg", lambda e: e.dma_start(out=dbg_x1[:, :], in_=x1d[:, :]), r=["x1dram"], w=["dbgx"])

            with ExitStack() as es2:
                def sb(name, shape, dt, l=l):
                    return es2.enter_context(nc.sbuf_tensor("sD%d_%s" % (l, name), shape, dt))
                W1 = sb("W1", [128, 8, 4096], BF16)
                W2 = sb("W2", [128, 32, 1024], BF16)
                for kc in range(8):
                    p.dma("pool", "wg", lambda e, kc=kc, l=l, W1=W1: e.dma_start(
                        out=W1[:, kc, :], in_=ext["w_mlp_in"][l][kc * 128:(kc + 1) * 128, :]), w=["W1"])
                for q4 in range(4):
                    p.dma("pool", "wb", lambda e, q4=q4, l=l, W2=W2: e.dma_start(
                        out=W2[:, q4 * 8:(q4 + 1) * 8, :], in_=fm(ext["w_mlp_out"][l][q4 * 1024:(q4 + 1) * 1024, :])), w=["W2"])
                NB = 256
                xt_2 = sb("xt", [128, 8, NB], F32)
                sq_2 = sb("sq", [128, 8, NB], BF16)
                rt_2 = sb("rt", [128, NB], F32)
                rstd_2 = sb("rstd", [128, NB], F32)
                hT_2 = sb("hT", [128, 8, NB], BF16)
                hn_2 = sb("hn", [128, 8, NB], BF16)
                aT = sb("aT", [128, 32, NB], BF16)
                rb = [sb("rb%d" % i, [128, NB], F32) for i in range(2)]
                for t in range(TC // NB):
                    ts_ = slice(t * NB, (t + 1) * NB)
                    p.dma("sp", "cx", lambda e, ts_=ts_, xt_2=xt_2: e.dma_start(out=xt_2[:], in_=fm(x1d)[:, :, ts_]),
                          r=["x1dram"], w=["xt2"])
                    norm_tile(xt_2, "xt2", NB, C.gmlp, sq_2, rt_2, rstd_2, hT_2, "hT2")
                    for fg in range(32):
                        a = fg % 3
                        for kc in range(8):
                            p.pe(MM(PS[a][:, 0:NB], W1[:, kc, fg * 128:(fg + 1) * 128], hT_2[:, kc, :], kc == 0, kc == 7),
                                 r=["W1", "hT2"], w=[("ps", a)])
                        r_ = rb[fg % 2]
                        p.act(ACTF(r_[:], PS[a][:, 0:NB], AF.Relu), r=[("ps", a)], w=[("rb", fg % 2)])
                        p.dve(TT(aT[:, fg, :], r_[:], r_[:], ALU.mult), r=[("rb", fg % 2)], w=["aT"])
                    for og in range(8):
                        b = 3 + og % 2
                        for fc in range(32):
                            p.pe(MM(PS[b][:, 0:NB], W2[:, fc, og * 128:(og + 1) * 128], aT[:, fc, :], fc == 0, fc == 31),
                                 r=["W2", "aT"], w=[("ps", b)])
                        p.dve(TT(xt_2[:, og, :], xt_2[:, og, :], PS[b][:, 0:NB], ALU.add), r=["xt2", ("ps", b)], w=["xt2"])
                    p.dma("sp", "c2o", lambda e, ts_=ts_, xt_2=xt_2, xout=xout: e.dma_start(out=fm(xout)[:, :, ts_], in_=xt_2[:]),
                          r=["xt2"], w=["xres"])
                    if l < depth - 1:
                        norm_tile(xt_2, "xt2", NB, None, sq_2, rt_2, rstd_2, hn_2, "hn2")
                        k_, col = divmod(t * NB, HW)
                        p.dma("sp", "ch", lambda e, k_=k_, col=col, hn_2=hn_2, par=par: e.dma_start(
                            out=fm(hsrc[1 - par][k_])[:, :, col:col + NB], in_=hn_2[:]), r=["hn2"], w=[("hsrc", k_)])
                        if (t * NB + NB) % HW == 0:
                            gather_h(1 - par, k_)
            p.barrier()
            if debug and l == depth - 2:
                p.dma("pool", "dbg", lambda e, par=par: e.dma_start(out=dbg_hall[:, :], in_=hall[1 - par][0][:, :]),
                      r=[("hall", 0)], w=["dbgh"])
            p.barrier(new_epoch=True)
        p.emit(es)
        print("fused program:", p.stats)
    return nc


_CACHE = {}


def kernel(**inputs):
    P = {k: np.asarray(v, dtype=np.float32) for k, v in inputs.items()}
    x = P["x"]
    S, B = SEQ, BATCH
    TC = S // 4
    if "F" not in _CACHE:
        _CACHE["F"] = build_fused()
    nc = _CACHE["F"]
    consts = {"c_" + k: v for k, v in make_consts(S).items()}
    xT = np.ascontiguousarray(x.reshape(B * S, D_MODEL).T)
    wB = {}
    for h in range(4):
        per_l = [prep_B_weights(P, l, h) for l in range(DEPTH)]
        wB[h] = {name: np.ascontiguousarray(np.stack([m[name] for m in per_l])) for name, _ in B_WEIGHTS}
    per_l = [prep_C_weights(P, l) for l in range(DEPTH)]
    wC = {name: np.ascontiguousarray(np.stack([m[name] for m in per_l])) for name, _ in C_WEIGHTS}
    in_maps = []
    for c in range(8):
        m = dict(wB[c % 4])
        m.update(wC)
        m.update(consts)
        m["xT"] = np.ascontiguousarray(xT[:, c * TC:(c + 1) * TC])
        in_maps.append(m)
    res = run_bass_kernel_spmd(nc, in_maps, core_ids=list(range(8))).results
    xoT = np.concatenate([res[c]["xoT"] for c in range(8)], axis=1)
    return np.ascontiguousarray(xoT.T).reshape(B, S, D_MODEL).astype(np.float32)
```

```python
import numpy as np
from contextlib import ExitStack
import concourse.bass as bass
import concourse.mybir as mybir
from concourse.bass_utils import run_bass_kernel_spmd

F32 = mybir.dt.float32
BF16 = mybir.dt.bfloat16
AF = mybir.ActivationFunctionType
ALU = mybir.AluOpType

D_MODEL = 1024
EPS = 1e-6
NEG = -30000.0


class Prog:
    ENGS = ("pe", "act", "dve", "pool", "sp")

    NSETS = 3

    def __init__(self, nc):
        self.nc = nc
        self.ops = []
        self.last_of = {}
        self.epoch = 0

    def add(self, eng, fn, r=(), w=(), chan=None, extra=(), inc=16, epoch_end=False):
        i = len(self.ops)
        x = tuple(t for t in r if isinstance(t, tuple) and t[0] == "ps")
        r = tuple(t for t in r if not (isinstance(t, tuple) and t[0] == "ps"))
        self.ops.append(dict(eng=eng, fn=fn, r=tuple(r), w=tuple(w), x=x, chan=chan, extra=tuple(extra), inc=inc,
                             epoch_end=epoch_end))
        self.last_of[("c", chan) if chan is not None else ("e", eng)] = i
        return i

    def pe(self, fn, r=(), w=()):
        return self.add("pe", fn, r, w)

    def act(self, fn, r=(), w=()):
        return self.add("act", fn, r, w)

    def dve(self, fn, r=(), w=()):
        return self.add("dve", fn, r, w)

    def pool(self, fn, r=(), w=()):
        return self.add("pool", fn, r, w)

    def dma(self, q, chan, fn, r=(), w=()):
        return self.add(q, fn, r, w, chan=chan)

    def coll(self, chan, fn, r=(), w=()):
        return self.add("pool", fn, r, w, chan=chan, inc=1)

    def barrier(self, new_epoch=False):
        new_epoch = False
        if new_epoch and self.epoch >= self.NSETS - 1:
            self.add("sp", "clear")
        lasts = list(self.last_of.values())
        for k, e in enumerate(self.ENGS):
            self.add(e, None, extra=lasts, epoch_end=(new_epoch and k == len(self.ENGS) - 1))
        if new_epoch:
            self.last_of = {}
            self.epoch += 1

    def emit(self, es):
        nc, ops = self.nc, self.ops
        n = len(ops)
        last_w, readers, chan_last = {}, {}, {}
        deps = [set() for _ in range(n)]
        for i, op in enumerate(ops):
            for t in op["r"]:
                if t in last_w:
                    deps[i].add(last_w[t])
            for t in op["w"] + op["x"]:
                if t in last_w:
                    deps[i].add(last_w[t])
                for rd in readers.get(t, ()):
                    deps[i].add(rd)
            for j in op["extra"]:
                deps[i].add(j)
            if op["chan"] is not None:
                if op["chan"] in chan_last:
                    deps[i].add(chan_last[op["chan"]])
                chan_last[op["chan"]] = i
            for t in op["r"]:
                readers.setdefault(t, []).append(i)
            for t in op["w"] + op["x"]:
                last_w[t] = i
                readers[t] = []
            deps[i].discard(i)

        def needed(i, j):
            a, b = ops[i], ops[j]
            if b["fn"] is None:
                return False
            if a["chan"] is not None or b["chan"] is not None:
                return True
            if a["eng"] != b["eng"]:
                return True
            if j in a["extra"]:
                return True
            if a["eng"] == "pe":
                return False
            return bool((set(a["r"]) | set(a["w"]) | set(a["x"])) & set(b["w"]))

        epoch_of = []
        ep = 0
        for i, op in enumerate(ops):
            epoch_of.append(ep)
            if op["epoch_end"]:
                ep += 1
        fdeps = [[j for j in deps[i] if needed(i, j) and epoch_of[j] == epoch_of[i]] for i in range(n)]
        signaled = set(j for i in range(n) for j in fdeps[i])
        cnt = {}
        ev = {}
        for i, op in enumerate(ops):
            if op["fn"] is None:
                continue
            if op["chan"] is not None:
                key = (epoch_of[i], "c", op["chan"])
                cnt[key] = cnt.get(key, 0) + op["inc"]
                ev[i] = (key, cnt[key])
            elif i in signaled:
                key = (epoch_of[i], "e", op["eng"])
                cnt[key] = cnt.get(key, 0) + 1
                ev[i] = (key, cnt[key])
        sems = {}
        for key in cnt:
            k3 = (key[0] % self.NSETS, key[1], key[2])
            if k3 not in sems:
                sems[k3] = es.enter_context(nc.semaphore("s_%d_%s_%s" % k3))
        self.stats = dict(n_ops=n, n_sems=len(sems), max_cnt=max(cnt.values()) if cnt else 0)

        def sem_of(key):
            return sems[(key[0] % self.NSETS, key[1], key[2])]

        def make_clear(i):
            tgt = (epoch_of[i] + 1) % self.NSETS
            lst = [sm for k3, sm in sems.items() if k3[0] == tgt]

            def fn(e):
                ins = None
                for sm in lst:
                    ins = e.sem_clear(sm)
                return ins
            return fn

        for i, op in enumerate(ops):
            if op["fn"] == "clear":
                op["fn"] = make_clear(i)

        block = es.enter_context(nc.Block())

        def run_engine(ename):
            def body(e):
                known = {}
                for i, op in enumerate(ops):
                    if op["eng"] != ename:
                        continue
                    need = {}
                    for j in fdeps[i]:
                        k, v = ev[j]
                        if need.get(k, 0) < v:
                            need[k] = v
                    for k, v in need.items():
                        if known.get(k, 0) < v:
                            e.wait_ge(sem_of(k), v)
                            known[k] = v
                    if op["fn"] is None:
                        continue
                    ins = op["fn"](e)
                    if i in ev:
                        k, v = ev[i]
                        ins.then_inc(sem_of(k), op["inc"] if k[1] == "c" else 1)
            return body

        block.tensor(run_engine("pe"))
        block.scalar(run_engine("act"))
        block.vector(run_engine("dve"))
        block.gpsimd(run_engine("pool"))
        block.sync(run_engine("sp"))


def make_consts(S):
    c = {}
    k = np.arange(128)
    c["ident"] = np.eye(128, dtype=np.float32)
    c["ones"] = np.ones((128, 128), np.float32)
    c["negge"] = -(k[:, None] >= k[None, :]).astype(np.float32)
    q = np.arange(512)
    mb = np.zeros((128, 4, 512), np.float32)
    for d in range(4):
        mb[:, d, :] = np.where((128 * d + k)[:, None] >= q[None, :], NEG, 0.0)
    c["sbmask"] = mb.reshape(128, 2048)
    le = (k[:, None] <= k[None, :]).astype(np.float32)
    lt = (k[:, None] < k[None, :]).astype(np.float32)
    gt = (k[:, None] > k[None, :]).astype(np.float32)
    ge = (k[:, None] >= k[None, :]).astype(np.float32)
    c["tri"] = np.concatenate([le, lt, gt, ge], axis=1)
    c["dilmask"] = np.concatenate([np.tile(ge, (1, 4)), np.tile(le, (1, 4))], axis=1)
    half = 8
    inv = 500000.0 ** (-np.arange(half, dtype=np.float32) / half)
    ang = np.arange(S, dtype=np.float32)[None, :] * inv[:, None].astype(np.float32)
    c["ropec"] = np.concatenate([np.cos(ang), np.cos(ang)], axis=0).astype(np.float32)
    c["ropes"] = np.concatenate([np.sin(ang), np.sin(ang)], axis=0).astype(np.float32)
    rt = np.zeros((16, 16), np.float32)
    for m in range(8):
        rt[m + 8, m] = -1.0
        rt[m, m + 8] = 1.0
    c["ropert"] = rt
    return c


def const_shapes(S):
    return {"ident": [128, 128], "ones": [128, 128], "negge": [128, 128], "sbmask": [128, 2048],
            "tri": [128, 512], "dilmask": [128, 1024], "ropec": [16, S], "ropes": [16, S], "ropert": [16, 16]}


class Ctx:
    pass


def build_B(S, phases=("sb",)):
    nc = bass.Bass("TRN2", target_bir_lowering=False)
    NT = S // 512
    dr = {}
    dr["xT"] = nc.dram_tensor("xT", [D_MODEL, S], F32, kind="ExternalInput").ap()
    dr["gmix"] = nc.dram_tensor("gmix", [128, 8], F32, kind="ExternalInput").ap()
    dr["w_sb"] = nc.dram_tensor("w_sb", [D_MODEL, 192], F32, kind="ExternalInput").ap()
    dr["w_dil"] = nc.dram_tensor("w_dil", [D_MODEL, 192], F32, kind="ExternalInput").ap()
    dr["w_gla"] = nc.dram_tensor("w_gla", [D_MODEL, 208], F32, kind="ExternalInput").ap()
    dr["w_dn"] = nc.dram_tensor("w_dn", [D_MODEL, 258], F32, kind="ExternalInput").ap()
    dr["pvec"] = nc.dram_tensor("pvec", [128, 8], F32, kind="ExternalInput").ap()
    dr["wlr2"] = nc.dram_tensor("wlr2", [32, 32], F32, kind="ExternalInput").ap()
    dr["dnconv"] = nc.dram_tensor("dnconv", [64, 12], F32, kind="ExternalInput").ap()
    for name, shp in const_shapes(S).items():
        dr[name] = nc.dram_tensor("c_" + name, shp, F32, kind="ExternalInput").ap()
    dr["o_sb"] = nc.dram_tensor("o_sb", [64, S], F32, kind="ExternalOutput").ap()
    dr["o_dil"] = nc.dram_tensor("o_dil", [64, S], F32, kind="ExternalOutput").ap()
    dr["o_gla"] = nc.dram_tensor("o_gla", [64, S], F32, kind="ExternalOutput").ap()
    dr["o_dn"] = nc.dram_tensor("o_dn", [64, S], F32, kind="ExternalOutput").ap()

    with ExitStack() as es:
        p = Prog(nc)
        C = Ctx()
        C.nc, C.p, C.S, C.NT, C.dr = nc, p, S, NT, dr

        def sb(name, shape, dt):
            return es.enter_context(nc.sbuf_tensor("s_" + name, shape, dt))

        C.PS = [es.enter_context(nc.psum_tensor("ps%d" % k, [128, 512], F32)) for k in range(8)]
        C.ident = sb("ident", [128, 128], BF16)
        C.ones = sb("ones", [128, 128], BF16)
        C.gmix = sb("gmix", [128, 8], F32)
        p.dma("pool", "cst", lambda e: e.dma_start(out=C.ident[:], in_=dr["ident"][:, :]), w=["ident"])
        p.dma("pool", "cst", lambda e: e.dma_start(out=C.ones[:], in_=dr["ones"][:, :]), w=["ones"])
        p.dma("sp", "cst2", lambda e: e.dma_start(out=C.gmix[:], in_=dr["gmix"][:, :]), w=["gmix"])
        C.identf = sb("identf", [128, 128], F32)
        C.onesf = sb("onesf", [128, 128], F32)
        C.tri = sb("tri", [128, 512], F32)
        C.tri_bf = sb("tri_bf", [128, 512], BF16)
        p.dma("pool", "cst", lambda e: e.dma_start(out=C.tri_bf[:], in_=dr["tri"][:, :]), w=["tri_bf"])
        C.pvec = sb("pvec", [128, 8], F32)
        p.dma("sp", "cst2", lambda e: e.dma_start(out=C.identf[:], in_=dr["ident"][:, :]), w=["identf"])
        p.dma("sp", "cst2", lambda e: e.dma_start(out=C.onesf[:], in_=dr["ones"][:, :]), w=["onesf"])
        p.dma("sp", "cst2", lambda e: e.dma_start(out=C.tri[:], in_=dr["tri"][:, :]), w=["tri"])
        p.dma("sp", "cst2", lambda e: e.dma_start(out=C.pvec[:], in_=dr["pvec"][:, :]), w=["pvec"])
        C.xt = [sb("xt%d" % i, [128, 8, 512], F32) for i in range(2)]
        C.sq = sb("sq", [128, 8, 512], BF16)
        C.rt = sb("rt", [128, 512], F32)
        C.rstd = sb("rstd", [128, 512], F32)
        C.hT = [sb("hT%d" % i, [128, 8, 512], BF16) for i in range(2)]

        C.cr_sq = sb("cr_sq", [64, 512], F32)
        C.cr_l = sb("cr_l", [64, 512], F32)
        C.gn_of = sb("gn_of", [64, 512], F32)
        C.gn_rs = sb("gn_rs", [64, 512], F32)
        C.gn_er = sb("gn_er", [64, 512], F32)
        C.gn_sg = sb("gn_sg", [64, 512], F32)
        C.gn_t1 = sb("gn_t1", [64, 512], F32)
        C.gn_out = [sb("gn_out%d" % i, [64, 512], F32) for i in range(2)]
        for ph, fn in (("sb", phase_sb), ("dil", phase_dil), ("gla", phase_gla), ("dn", phase_dn)):
            if ph in phases:
                with ExitStack() as es2:
                    fn(C, es2)
                p.barrier()

        p.barrier()
        p.emit(es)
        print("B program ops:", p.stats)
    return nc


def o_slice(C, name, tok, width):
    if getattr(C, "fused", False):
        TC = C.S // 4
        r_, w_ = divmod(tok, TC)
        k_, c_ = divmod(w_, C.CW)
        return C.osrc[name][k_][:, r_ * C.CW + c_:r_ * C.CW + c_ + width]
    return C.dr[name][:, tok:tok + width]


def load_weights(C, es2, name, ncols):
    nc, p = C.nc, C.p
    wst = es2.enter_context(nc.sbuf_tensor(getattr(C, "tag", "") + "s_wst_" + name, [128, 8, ncols], F32))
    wb = es2.enter_context(nc.sbuf_tensor(getattr(C, "tag", "") + "s_wb_" + name, [128, 8, ncols], BF16))
    src = C.dr[name].rearrange("(c p) n -> p c n", p=128)
    p.dma("sp", "w", lambda e: e.dma_start(out=wst[:], in_=src), w=["wst"])
    for ch in range(8):
        p.dve(lambda e, ch=ch: e.tensor_scalar(out=wb[:, ch, :], in0=wst[:, ch, :], scalar1=C.gmix[:, ch:ch + 1],
                                               scalar2=None, op0=ALU.mult),
              r=["wst", "gmix"], w=["wb"])
    return wb


def front_end(C, T):
    p, nc = C.p, C.nc
    b = T % 2
    if getattr(C, "fused", False):
        hT = C.hT[b]
        r_t, w_ = divmod(T * 512, C.S // 4)
        k_, col = divmod(w_, C.HW)
        src = C.hall[k_].rearrange("(r c p) s -> r p c s", r=4, p=128)[r_t][:, :, col:col + 512]
        p.dma("sp", "x%d" % b, lambda e: e.dma_start(out=hT[:], in_=src), r=[("hall", k_)], w=[("hT", b)])
        return hT, ("hT", b)
    xt, hT = C.xt[b], C.hT[b]
    src = C.dr["xT"].rearrange("(c p) s -> p c s", p=128)[:, :, T * 512:(T + 1) * 512]
    p.dma("sp", "x%d" % b, lambda e: e.dma_start(out=xt[:], in_=src), w=[("xt", b)])
    p.act(lambda e: e.activation(out=C.sq[:], in_=xt[:], func=AF.Square), r=[("xt", b)], w=["sq"])
    ss = C.PS[7]
    for ch in range(8):
        p.pe(lambda e, ch=ch: e.matmul(ss[:, :], lhsT=C.ones[:, :], rhs=C.sq[:, ch, :], start=(ch == 0), stop=(ch == 7)),
             r=["sq", "ones"], w=[("ps", 7)])
    p.act(lambda e: e.activation(out=C.rt[:], in_=ss[:, :], func=AF.Ln, scale=1.0 / D_MODEL, bias=EPS),
          r=[("ps", 7)], w=["rt"])
    p.act(lambda e: e.activation(out=C.rstd[:], in_=C.rt[:], func=AF.Exp, scale=-0.5), r=["rt"], w=["rstd"])
    for ch in range(8):
        p.dve(lambda e, ch=ch: e.tensor_tensor(out=hT[:, ch, :], in0=xt[:, ch, :], in1=C.rstd[:], op=ALU.mult),
              r=[("xt", b), "rstd"], w=[("hT", b)])
    return hT, ("hT", b)


def project(C, wb, hT, hT_tok, c0, ncol, bank):
    p = C.p
    ps = C.PS[bank]
    for ch in range(8):
        p.pe(lambda e, ch=ch: e.matmul(ps[0:ncol, :], lhsT=wb[:, ch, c0:c0 + ncol], rhs=hT[:, ch, :],
                                       start=(ch == 0), stop=(ch == 7)),
             r=["wb", hT_tok], w=[("ps", bank)])
    return ps


def phase_sb(C, es2):
    nc, p, S, NT = C.nc, C.p, C.S, C.NT
    dr = C.dr

    def sb(name, shape, dt):
        return es2.enter_context(nc.sbuf_tensor(getattr(C, "tag", "") + "s_" + name, shape, dt))

    wb = load_weights(C, es2, "w_sb", 192)
    Q = sb("sbQ", [64, S], BF16)
    K = sb("sbK", [64, S], BF16)
    V = sb("sbV", [128, S // 128, 64], BF16)
    vT = sb("sbvT", [64, 512], BF16)
    negge = sb("negge", [128, 128], BF16)
    mbias = sb("sbmask", [128, 4, 512], BF16)
    onescol = sb("onescol", [128, 1], BF16)
    negrow = sb("negrow", [1, 128], BF16)
    p.dma("pool", "cst", lambda e: e.dma_start(out=negge[:], in_=dr["negge"][:, :]), w=["negge"])
    p.dma("pool", "cst", lambda e: e.dma_start(out=mbias[:], in_=dr["sbmask"].rearrange("p (d q) -> p d q", d=4)),
          w=["mbias"])
    p.dve(lambda e: e.memset(onescol[:], 1.0), w=["onescol"])
    p.dve(lambda e: e.memset(negrow[:], -1.0), w=["negrow"])

    for T in range(NT):
        hT, htok = front_end(C, T)
        ps = project(C, wb, hT, htok, 0, 64, 0)
        p.act(lambda e, ps=ps, T=T: e.mul(out=Q[:, T * 512:(T + 1) * 512], in_=ps[0:64, :], mul=0.125),
              r=[("ps", 0)], w=[("sbQ", T)])
        ps = project(C, wb, hT, htok, 64, 64, 1)
        p.act(lambda e, ps=ps, T=T: e.copy(out=K[:, T * 512:(T + 1) * 512], in_=ps[0:64, :]),
              r=[("ps", 1)], w=[("sbK", T)])
        ps = project(C, wb, hT, htok, 128, 64, 2)
        p.act(lambda e, ps=ps: e.copy(out=vT[:], in_=ps[0:64, :]), r=[("ps", 2)], w=["sbvT"])
        pt = C.PS[3]
        for blk in range(4):
            p.pe(lambda e, blk=blk: e.matmul(pt[:, blk * 64:(blk + 1) * 64], lhsT=vT[:, blk * 128:(blk + 1) * 128],
                                             rhs=C.ident[0:64, 0:64], start=True, stop=True),
                 r=["sbvT", "ident"], w=[("ps", 3)])
        p.dve(lambda e, T=T: e.tensor_copy(out=V[:, 4 * T:4 * T + 4, :],
                                           in_=pt[:, 0:256].rearrange("p (b c) -> p b c", b=4)),
              r=[("ps", 3)], w=[("sbV", T)])

    ebuf = [sb("sb_e%d" % i, [128, 512], F32) for i in range(2)]
    spb = [sb("sb_sp%d" % i, [128, 512], BF16) for i in range(4)]
    wbuf = [sb("sb_w%d" % i, [128, 512], BF16) for i in range(3)]
    acc_f = [sb("sb_af%d" % i, [64, 512], F32) for i in range(2)]
    acc_hl = [sb("sb_ahl%d" % i, [64, 512], BF16) for i in range(3)]
    onesc64 = sb("onesc64", [128, 64], BF16)
    negsel = sb("negsel", [64, 128], BF16)
    p.dve(lambda e: e.memset(onesc64[:], 1.0), w=["onesc64"])
    p.dve(lambda e: e.memset(negsel[:], 0.0), w=["negsel"])
    p.dve(lambda e: e.memset(negsel[0:1, :], -1.0), r=["negsel"], w=["negsel"])
    p.dve(lambda e: e.memset(negsel[32:33, :], -1.0), r=["negsel"], w=["negsel"])
    fused = getattr(C, "fused", False)
    oq = "pool"
    osb = [sb("sb_o%d" % i, [64, 512], BF16 if fused else F32) for i in range(2)]

    tiles = []
    for T in range(NT):
        for j in range(4 * T + 3, -1, -1):
            tiles.append((T, j))
    nt = len(tiles)

    def stageA(s):
        T, j = tiles[s]
        zb = s % 4
        Z = C.PS[zb]
        diag = j >= 4 * T
        p.pe(lambda e: e.matmul(Z[:, :], lhsT=K[:, j * 128:(j + 1) * 128], rhs=Q[:, T * 512:(T + 1) * 512],
                                start=True, stop=False),
             r=[("sbK", j // 4), ("sbQ", T)], w=[("ps", zb)])
        if diag:
            d = j - 4 * T
            p.pe(lambda e: e.matmul(Z[:, :], lhsT=C.ident[:, :], rhs=mbias[:, d, :], start=False, stop=False),
                 r=["ident", "mbias"], w=[("ps", zb)])
        eb = ebuf[s % 2]
        p.act(lambda e: e.activation(out=eb[:], in_=Z[:, :], func=AF.Exp), r=[("ps", zb)], w=[("sbe", s % 2)])
        p.act(lambda e: e.activation(out=spb[s % 4][:], in_=eb[:], func=AF.Ln, bias=1.0),
              r=[("sbe", s % 2)], w=[("sbsp", s % 4)])

    def stageA2(s):
        T, j = tiles[s]
        first = (j == 4 * T + 3)
        last = (j == 0)
        if last:
            return
        cb = 6 + (s % 2)
        cs = C.PS[cb]
        sp = spb[s % 4]
        p.pe(lambda e: e.matmul(cs[0:64, :], lhsT=onesc64[:, :], rhs=sp[:], start=True, stop=True),
             r=["onesc64", ("sbsp", s % 4)], w=[("ps", cb)])
        af, afp = acc_f[s % 2], acc_f[(s - 1) % 2]
        if first:
            p.dve(CP(af[:], cs[0:64, :]), r=[("ps", cb)], w=[("sbaf", s % 2)])
        else:
            p.dve(TT(af[:], afp[:], cs[0:64, :], ALU.add), r=[("ps", cb), ("sbaf", (s - 1) % 2)], w=[("sbaf", s % 2)])
        hl = acc_hl[s % 3]
        p.dve(CP(hl[:], af[:]), r=[("sbaf", s % 2)], w=[("sbahl", s % 3)])
        p.pool(TT(hl[32:64, :], af[32:64, :], hl[32:64, :], ALU.subtract), r=[("sbaf", s % 2), ("sbahl", s % 3)],
               w=[("sbahl", s % 3)])

    def stageB(s):
        T, j = tiles[s]
        zb = s % 4
        Z = C.PS[zb]
        first = (j == 4 * T + 3)
        sp = spb[s % 4]
        p.pe(lambda e: e.matmul(Z[:, :], lhsT=negge[:, :], rhs=sp[:], start=False, stop=first),
             r=["negge", ("sbsp", s % 4)], w=[("ps", zb)])
        if not first:
            pa = (s - 1) % 3
            p.pe(lambda e: e.matmul(Z[:, :], lhsT=negsel[:, :], rhs=acc_hl[pa][:], start=False, stop=True),
                 r=["negsel", ("sbahl", pa)], w=[("ps", zb)])
        p.act(lambda e: e.activation(out=wbuf[s % 3][:], in_=Z[:, :], func=AF.Exp),
              r=[("ps", zb)], w=[("sbw", s % 3)])

    def stageC(s):
        T, j = tiles[s]
        first = (j == 4 * T + 3)
        last = (j == 0)
        ob = 4 + (T % 2)
        O = C.PS[ob]
        p.pe(lambda e: e.matmul(O[0:64, :], lhsT=V[:, j, :], rhs=wbuf[s % 3][:], start=first, stop=last),
             r=[("sbV", j // 4), ("sbw", s % 3)], w=[("ps", ob)])
        if last:
            o = osb[T % 2]
            dst = o_slice(C, "o_sb", T * 512, 512)
            p.dve(lambda e: e.tensor_copy(out=o[:], in_=O[0:64, :]), r=[("ps", ob)], w=[("sbo", T % 2)])
            p.dma(oq, "o%d" % (T % 2),
                  lambda e: e.dma_start(out=dst, in_=o[:]),
                  r=[("sbo", T % 2)], w=["out_sb"])

    for _ in range(int(getattr(C, "warm", 48))):
        p.pe(lambda e: e.matmul(C.PS[7][:, :], lhsT=negge[:, :], rhs=mbias[:, 0, :], start=True, stop=True),
             r=["negge", "mbias"], w=[("ps", 7)])
    for s in range(nt + 3):
        if s < nt:
            stageA(s)
        if 0 <= s - 1 < nt:
            stageA2(s - 1)
        if 0 <= s - 2 < nt:
            stageB(s - 2)
        if 0 <= s - 3 < nt:
            stageC(s - 3)


def TT(out, in0, in1, op):
    return lambda e: e.tensor_tensor(out=out, in0=in0, in1=in1, op=op)


def TS(out, in0, s1, s2, op0, op1=None):
    if op1 is None:
        return lambda e: e.tensor_scalar(out=out, in0=in0, scalar1=s1, scalar2=None, op0=op0)
    return lambda e: e.tensor_scalar(out=out, in0=in0, scalar1=s1, scalar2=s2, op0=op0, op1=op1)


def STT(out, in0, scalar, in1, op0, op1):
    return lambda e: e.scalar_tensor_tensor(out=out, in0=in0, scalar=scalar, in1=in1, op0=op0, op1=op1)


def ACTF(out, in_, func, **kw):
    return lambda e: e.activation(out=out, in_=in_, func=func, **kw)


def MM(out, lhsT, rhs, start=True, stop=True):
    return lambda e: e.matmul(out, lhsT=lhsT, rhs=rhs, start=start, stop=stop)


def CP(out, in_):
    return lambda e: e.tensor_copy(out=out, in_=in_)


def RCP(out, in_):
    return lambda e: e.reciprocal(out=out, in_=in_)


def pe_warmup(C, bank, n=48):
    for _ in range(n):
        C.p.pe(MM(C.PS[bank][:, :], C.ident[:, :], C.tri_bf[:, :]), r=["ident", "tri_bf"], w=[("ps", bank)])


def col_rstd(C, src, src_toks, nrow, inv_n, bank, out_rs, out_tok):
    p = C.p
    ps = C.PS[bank]
    p.act(ACTF(C.cr_sq[0:nrow, :], src, AF.Square), r=src_toks, w=["cr_sq"])
    p.pe(MM(ps[0:nrow, :], C.onesf[0:nrow, 0:nrow], C.cr_sq[0:nrow, :]), r=["cr_sq", "onesf"], w=[("ps", bank)])
    p.act(ACTF(C.cr_l[0:nrow, :], ps[0:nrow, :], AF.Ln, scale=inv_n, bias=EPS), r=[("ps", bank)], w=["cr_l"])
    p.act(ACTF(out_rs, C.cr_l[0:nrow, :], AF.Exp, scale=-0.5), r=["cr_l"], w=[out_tok])


def gated_norm(C, o_ps, o_tok, rf, rf_tok, gcol, out_name, T, ssbank):
    p = C.p
    p.act(lambda e: e.copy(out=C.gn_of[:], in_=o_ps), r=[o_tok], w=["gn_of"])
    col_rstd(C, C.gn_of[:], ["gn_of"], 64, 1.0 / 64, ssbank, C.gn_rs[:], "gn_rs")
    p.act(ACTF(C.gn_er[:], rf, AF.Exp, scale=-1.0), r=[rf_tok], w=["gn_er"])
    p.dve(TS(C.gn_er[:], C.gn_er[:], 1.0, None, ALU.add), r=["gn_er"], w=["gn_er"])
    p.dve(RCP(C.gn_sg[:], C.gn_er[:]), r=["gn_er"], w=["gn_sg"])
    p.dve(TT(C.gn_t1[:], C.gn_of[:], C.gn_rs[:], ALU.mult), r=["gn_of", "gn_rs"], w=["gn_t1"])
    p.dve(STT(C.gn_t1[:], C.gn_t1[:], gcol, rf, ALU.mult, ALU.mult), r=["gn_t1", rf_tok, "pvec"], w=["gn_t1"])
    ob = C.gn_out[T % 2]
    p.dve(TT(ob[:], C.gn_t1[:], C.gn_sg[:], ALU.mult), r=["gn_t1", "gn_sg"], w=[("gn_out", T % 2)])
    dst = o_slice(C, out_name, T * 512, 512)
    p.dma("pool", "go%d" % (T % 2), lambda e: e.dma_start(out=dst, in_=ob[:]),
          r=[("gn_out", T % 2)], w=["gn_dram"])


def phase_dil(C, es2):
    nc, p, S, NT, dr = C.nc, C.p, C.S, C.NT, C.dr

    def sb(name, shape, dt):
        return es2.enter_context(nc.sbuf_tensor(getattr(C, "tag", "") + "s_" + name, shape, dt))

    wb = load_weights(C, es2, "w_dil", 192)
    QV = sb("dQV", [128, S], BF16)
    Q = QV[0:64, :]
    K = sb("dK", [64, S], BF16)
    dmask = sb("dmask", [128, 1024], BF16)
    rt16 = sb("rt16", [16, 16], F32)
    cosb = sb("dcos", [16, 512], F32)
    sinb = sb("dsin", [16, 512], F32)
    qn = sb("dqn", [64, 512], F32)
    rsb = sb("drs", [64, 512], F32)
    t1 = sb("dt1", [16, 512], F32)
    t2 = sb("dt2", [16, 512], F32)
    p.dma("pool", "cst", lambda e: e.dma_start(out=dmask[:], in_=dr["dilmask"][:, :]), w=["dmask"])
    p.dma("sp", "cst2", lambda e: e.dma_start(out=rt16[:], in_=dr["ropert"][:, :]), w=["rt16"])

    for T in range(NT):
        hT, htok = front_end(C, T)
        ts_ = slice(T * 512, (T + 1) * 512)
        p.dma("sp", "rope", lambda e, ts_=ts_: e.dma_start(out=cosb[:], in_=dr["ropec"][:, ts_]), w=["dcos"])
        p.dma("sp", "rope", lambda e, ts_=ts_: e.dma_start(out=sinb[:], in_=dr["ropes"][:, ts_]), w=["dsin"])
        for which, (c0, gc, dst, scl) in enumerate(((0, 0, QV, 0.125), (64, 1, K, 1.0))):
            ps = project(C, wb, hT, htok, c0, 64, which)
            col_rstd(C, ps[0:64, :], [("ps", which)], 64, 1.0 / 64, 3, rsb[:], "drs")
            p.dve(STT(qn[:], ps[0:64, :], C.pvec[0:64, gc:gc + 1], rsb[:], ALU.mult, ALU.mult),
                  r=[("ps", which), "pvec", "drs"], w=["dqn"])
            p.pe(MM(C.PS[4][0:16, :], rt16[:, :], qn[0:16, :]), r=["rt16", "dqn"], w=[("ps", 4)])
            p.dve(TT(t1[:], qn[0:16, :], cosb[:], ALU.mult), r=["dqn", "dcos"], w=["dt1"])
            p.dve(TT(t2[:], C.PS[4][0:16, :], sinb[:], ALU.mult), r=[("ps", 4), "dsin"], w=["dt2"])
            p.dve(TT(qn[0:16, :], t1[:], t2[:], ALU.add), r=["dt1", "dt2"], w=["dqn"])
            p.act(lambda e, dst=dst, scl=scl, ts_=ts_: e.mul(out=dst[0:64, ts_], in_=qn[:], mul=scl), r=["dqn"], w=["dQK"])
        ps = project(C, wb, hT, htok, 128, 64, 2)
        p.dve(CP(QV[64:128, ts_], ps[0:64, :]), r=[("ps", 2)], w=["dVT"])
    p.barrier()

    nsp = S // 2048
    accn = sb("daccn", [64, 2048], F32)
    accd = sb("daccd", [64, 2048], F32)
    Vl = {(pi, par): sb("dVl%d_%d" % (pi, par), [128, 16, 64], BF16) for pi in range(3) for par in range(2)}
    pA = sb("dpA", [128, 512], BF16)
    pB = sb("dpB", [128, 512], BF16)
    obf = sb("dobf", [64, 2048], BF16)
    pats = (1, 4, 16)

    def lanes_of(d, sidx):
        if d == 1:
            return [(0, 16 * sidx + i) for i in range(16)]
        if d == 4:
            return [(r, 4 * sidx + i) for i in range(4) for r in range(4)]
        return [(r, sidx) for r in range(16)]

    def lane_slice(d, r, nb, off=0):
        t0 = d * 128 * nb + r - off
        return slice(t0, t0 + 127 * d + 1, d)

    def lane_index(d, r, nb):
        if d == 1:
            return nb // 16, nb % 16
        if d == 4:
            return nb // 4, (nb % 4) * 4 + r
        return nb, r

    for sidx in range(nsp):
        par = sidx % 2
        p.dve(lambda e: e.memset(accn[:], 0.0), w=["daccn"])
        p.dve(lambda e: e.memset(accd[:], 0.0), w=["daccd"])
        for pi, d in enumerate(pats):
            lanes = lanes_of(d, sidx)
            vl = Vl[(pi, par)]
            for half in range(2):
                bank = 4 + half
                for l8 in range(8):
                    r_, nb_ = lanes[half * 8 + l8]
                    p.pe(MM(C.PS[bank][:, l8 * 64:(l8 + 1) * 64], QV[64:128, lane_slice(d, r_, nb_)], C.ident[64:128, 64:128]),
                         r=["ident"], w=[("ps", bank)])
                p.dve(CP(vl[:, half * 8:(half + 1) * 8, :], C.PS[bank][:, :].rearrange("p (l c) -> p l c", l=8)),
                      r=[("ps", bank)], w=[("dVl", pi, par)])
            for g in range(4):
                cur = lanes[4 * g:4 * g + 4]
                any_prev = any(nb_ > 0 for (_, nb_) in cur)
                for li, (r_, nb_) in enumerate(cur):
                    qs = Q[:, lane_slice(d, r_, nb_)]
                    cs = slice(li * 128, (li + 1) * 128)
                    if nb_ > 0:
                        p.pe(MM(C.PS[0][:, cs], K[:, lane_slice(d, r_, nb_ - 1)], qs), w=[("ps", 0)])
                    p.pe(MM(C.PS[1][:, cs], K[:, lane_slice(d, r_, nb_)], qs), w=[("ps", 1)])
                if any_prev:
                    p.act(ACTF(pA[:], C.PS[0][:, :], AF.Exp), r=[("ps", 0)], w=["dpA"])
                    p.dve(TT(pA[:], pA[:], dmask[:, 0:512], ALU.mult), r=["dpA", "dmask"], w=["dpA"])
                p.act(ACTF(pB[:], C.PS[1][:, :], AF.Exp), r=[("ps", 1)], w=["dpB"])
                p.dve(TT(pB[:], pB[:], dmask[:, 512:1024], ALU.mult), r=["dpB", "dmask"], w=["dpB"])
                for li, (r_, nb_) in enumerate(cur):
                    cs = slice(li * 128, (li + 1) * 128)
                    sp_c, idx_c = lane_index(d, r_, nb_)
                    vcur = Vl[(pi, sp_c % 2)][:, idx_c, :]
                    rtoks = ["dpA", "dpB", ("dVl", pi, 0), ("dVl", pi, 1), "ones"]
                    if nb_ > 0:
                        sp_p, idx_p = lane_index(d, r_, nb_ - 1)
                        vprev = Vl[(pi, sp_p % 2)][:, idx_p, :]
                        p.pe(MM(C.PS[2][0:64, cs], vprev, pA[:, cs], True, False), r=rtoks, w=[("ps", 2)])
                        p.pe(MM(C.PS[2][0:64, cs], vcur, pB[:, cs], False, True), r=rtoks, w=[("ps", 2)])
                        p.pe(MM(C.PS[3][0:64, cs], C.ones[:, 0:64], pA[:, cs], True, False), r=rtoks, w=[("ps", 3)])
                        p.pe(MM(C.PS[3][0:64, cs], C.ones[:, 0:64], pB[:, cs], False, True), r=rtoks, w=[("ps", 3)])
                    else:
                        p.pe(MM(C.PS[2][0:64, cs], vcur, pB[:, cs], True, True), r=rtoks, w=[("ps", 2)])
                        p.pe(MM(C.PS[3][0:64, cs], C.ones[:, 0:64], pB[:, cs], True, True), r=rtoks, w=[("ps", 3)])
                for li, (r_, nb_) in enumerate(cur):
                    cs = slice(li * 128, (li + 1) * 128)
                    sl = lane_slice(d, r_, nb_, off=2048 * sidx)
                    p.dve(TT(accn[:, sl], accn[:, sl], C.PS[2][0:64, cs], ALU.add), r=[("ps", 2), "daccn"], w=["daccn"])
                    p.dve(TT(accd[:, sl], accd[:, sl], C.PS[3][0:64, cs], ALU.add), r=[("ps", 3), "daccd"], w=["daccd"])
        p.dve(RCP(accd[:], accd[:]), r=["daccd"], w=["daccd"])
        dst = o_slice(C, "o_dil", sidx * 2048, 2048)
        if getattr(C, "fused", False):
            p.dve(TT(obf[:], accn[:], accd[:], ALU.mult), r=["daccn", "daccd"], w=["dobf"])
            p.dma("pool", "od", lambda e, dst=dst: e.dma_start(out=dst, in_=obf[:]), r=["dobf"], w=["od_dram"])
        else:
            p.dve(TT(accn[:], accn[:], accd[:], ALU.mult), r=["daccn", "daccd"], w=["daccn"])
            p.dma("pool", "od", lambda e, dst=dst: e.dma_start(out=dst, in_=accn[:]),
                  r=["daccn"], w=["od_dram"])


def phase_gla(C, es2):
    nc, p, S, NT, dr = C.nc, C.p, C.S, C.NT, C.dr

    def sb(name, shape, dt):
        return es2.enter_context(nc.sbuf_tensor(getattr(C, "tag", "") + "s_" + name, shape, dt))

    wb = load_weights(C, es2, "w_gla", 208)
    wlr2 = sb("gwlr2", [32, 32], F32)
    lrT = sb("glrT", [32, 512], F32)
    qf = sb("gqf", [32, 512], F32)
    kf = sb("gkf", [32, 512], F32)
    vf = sb("gvf", [64, 512], F32)
    rf = sb("grf", [64, 512], F32)
    state = sb("gstate", [32, 64], F32)
    ge1 = sb("ge1", [128, 32], F32)
    spg = sb("gspg", [128, 32], F32)
    E1 = sb("gE1", [32, 128], F32)
    E2 = sb("gE2", [32, 128], F32)
    ED = sb("gED", [128, 32], F32)
    qb = sb("gqb", [32, 128], F32)
    kb = sb("gkb", [32, 128], F32)
    kup = sb("gkup", [128, 32], F32)
    vtok = sb("gvtok", [128, 64], F32)
    scT = sb("gscT", [128, 128], F32)
    tri_le, tri_gt = C.tri[:, 0:128], C.tri[:, 256:384]
    wlr2_dram = dr["wlr2"]
    p.dma("sp", "cst2", lambda e: e.dma_start(out=wlr2[:], in_=wlr2_dram[:, :]), w=["gwlr2"])
    pe_warmup(C, 7)
    p.dve(lambda e: e.memset(lrT[:], 1.0), w=["glrT"])
    p.dve(lambda e: e.memset(state[:], 0.0), w=["gstate"])
    PS = C.PS

    for T in range(NT):
        hT, htok = front_end(C, T)
        for (c0, n, bank, dst, tok) in ((0, 32, 0, qf, "gqf"), (32, 32, 1, kf, "gkf"), (64, 64, 2, vf, "gvf"),
                                       (128, 64, 3, rf, "grf"), (192, 16, 4, lrT, "glrT")):
            ps = project(C, wb, hT, htok, c0, n, bank)
            p.act(lambda e, dst=dst, n=n, ps=ps: e.copy(out=dst[0:n, :], in_=ps[0:n, :]), r=[("ps", bank)], w=[tok])
        for ci in range(4):
            cs = slice(ci * 128, (ci + 1) * 128)
            p.pe(MM(PS[5][:, 0:32], lrT[:, cs], wlr2[:, :]), r=["glrT", "gwlr2"], w=[("ps", 5)])
            p.act(ACTF(ge1[:], PS[5][:, 0:32], AF.Exp, scale=-1.0), r=[("ps", 5)], w=["ge1"])
            p.act(ACTF(spg[:], ge1[:], AF.Ln, bias=1.0), r=["ge1"], w=["gspg"])
            p.pe(MM(PS[6][0:32, 0:128], spg[:, :], tri_le), r=["gspg", "tri"], w=[("ps", 6)])
            p.pe(MM(PS[5][:, 32:64], tri_gt, spg[:, :]), r=["gspg", "tri"], w=[("ps", 5)])
            p.act(ACTF(E1[:], PS[6][0:32, 0:128], AF.Exp, scale=-1.0 / 16), r=[("ps", 6)], w=["gE1"])
            p.act(ACTF(E2[:], PS[6][0:32, 0:128], AF.Exp, scale=1.0 / 16), r=[("ps", 6)], w=["gE2"])
            p.act(ACTF(ED[:], PS[5][:, 32:64], AF.Exp, scale=-1.0 / 16), r=[("ps", 5)], w=["gED"])
            p.dve(STT(qb[:], qf[:, cs], 32.0 ** -0.5, E1[:], ALU.mult, ALU.mult), r=["gqf", "gE1"], w=["gqb"])
            p.dve(TT(kb[:], kf[:, cs], E2[:], ALU.mult), r=["gkf", "gE2"], w=["gkb"])
            p.pe(MM(PS[5][:, 64:96], kf[:, cs], C.identf[0:32, 0:32]), r=["gkf", "identf"], w=[("ps", 5)])
            p.pe(MM(PS[5][:, 128:192], vf[:, cs], C.identf[0:64, 0:64]), r=["gvf", "identf"], w=[("ps", 5)])
            p.dve(TT(kup[:], PS[5][:, 64:96], ED[:], ALU.mult), r=[("ps", 5), "gED"], w=["gkup"])
            p.dve(CP(vtok[:], PS[5][:, 128:192]), r=[("ps", 5)], w=["gvtok"])
            p.pe(MM(PS[6][:, 256:384], kb[:, :], qb[:, :]), r=["gkb", "gqb"], w=[("ps", 6)])
            p.dve(TT(scT[:], PS[6][:, 256:384], tri_le, ALU.mult), r=[("ps", 6), "tri"], w=["gscT"])
            p.pe(MM(PS[4][0:64, cs], vtok[:, :], scT[:, :], True, False), r=["gvtok", "gscT"], w=[("ps", 4)])
            p.pe(MM(PS[4][0:64, cs], state[:, :], qb[:, :], False, True), r=["gstate", "gqb"], w=[("ps", 4)])
            p.pe(MM(PS[6][0:32, 128:192], kup[:, :], vtok[:, :]), r=["gkup", "gvtok"], w=[("ps", 6)])
            p.dve(STT(state[:], state[:], E1[:, 127:128], PS[6][0:32, 128:192], ALU.mult, ALU.add),
                  r=["gstate", "gE1", ("ps", 6)], w=["gstate"])
        gated_norm(C, PS[4][0:64, :], ("ps", 4), rf[:], "grf", C.pvec[0:64, 2:3], "o_gla", T, 0)


IN_SIZES = (256, 256, 256, 128, 128, 256, 16, 256, 768, 4, 4, 256, 256, 256, 256, 4096)
IN_OFF = [0] + list(np.cumsum(IN_SIZES)[:-1])


def _cols(i, h, n):
    return np.arange(IN_OFF[i] + h * n, IN_OFF[i] + (h + 1) * n)


def prep_B_weights(P, l, h):
    w_in = P["w_in"][l]
    m = {}
    m["gmix"] = np.ascontiguousarray(P["norm_mix_g"][l].reshape(8, 128).T)
    m["w_sb"] = np.ascontiguousarray(w_in[:, np.concatenate([_cols(0, h, 64), _cols(1, h, 64), _cols(2, h, 64)])])
    m["w_dil"] = np.ascontiguousarray(w_in[:, np.concatenate([_cols(12, h, 64), _cols(13, h, 64), _cols(14, h, 64)])])
    m["w_gla"] = np.ascontiguousarray(w_in[:, np.concatenate([_cols(3, h, 32), _cols(4, h, 32), _cols(5, h, 64),
                                                              _cols(7, h, 64), _cols(6, 0, 16)])])
    dn_q = IN_OFF[8] + h * 64 + np.arange(64)
    m["w_dn"] = np.ascontiguousarray(w_in[:, np.concatenate([dn_q, dn_q + 256, dn_q + 512, _cols(11, h, 64),
                                                             _cols(9, h, 1), _cols(10, h, 1)])])
    pv = np.zeros((128, 8), np.float32)
    pv[:64, 0] = P["dil_q_norm_g"][l]
    pv[:64, 1] = P["dil_k_norm_g"][l]
    pv[:64, 2] = P["gla_norm_g"][l]
    pv[:64, 3] = P["dn_norm_g"][l]
    pv[:, 4] = P["dn_a_log"][l, h]
    pv[:, 5] = P["dn_dt_bias"][l, h]
    m["pvec"] = pv
    wl = np.zeros((32, 32), np.float32)
    wl[:16] = P["gla_w_lr2"][l][:, h * 32:(h + 1) * 32]
    wl[16] = P["gla_b_lr"][l][h * 32:(h + 1) * 32]
    m["wlr2"] = wl
    cw = P["dn_conv_w"][l]
    m["dnconv"] = np.ascontiguousarray(
        np.concatenate([cw[s * 256 + h * 64:s * 256 + (h + 1) * 64, :] for s in range(3)], axis=1))
    return m


def phase_dn(C, es2):
    nc, p, S, NT, dr = C.nc, C.p, C.S, C.NT, C.dr
    PS = C.PS

    def sb(name, shape, dt):
        return es2.enter_context(nc.sbuf_tensor(getattr(C, "tag", "") + "s_" + name, shape, dt))

    wb = load_weights(C, es2, "w_dn", 258)
    cw = sb("ncw", [64, 12], F32)
    dnconv_dram = dr["dnconv"]
    p.dma("sp", "cst2", lambda e: e.dma_start(out=cw[:], in_=dnconv_dram[:, :]), w=["ncw"])
    xbuf = [sb("nxb%d" % s_, [64, 515], F32) for s_ in range(3)]
    for s_ in range(3):
        p.dve(lambda e, s_=s_: e.memset(xbuf[s_][:], 0.0), w=[("nxb", s_)])
    y = sb("ny", [64, 512], F32)
    ee = sb("nee", [64, 512], F32)
    ys = sb("nys", [64, 512], F32)
    rs = sb("nrs", [64, 512], F32)
    qT = sb("nqT", [64, 512], F32)
    kT = sb("nkT", [64, 512], F32)
    vT = sb("nvT", [64, 512], F32)
    rf = sb("nrf", [64, 512], F32)
    state = sb("nstate", [64, 64], F32)
    p.dve(lambda e: e.memset(state[:], 0.0), w=["nstate"])
    pe_warmup(C, 7)
    nea = sb("nnea", [128, 1], F32)
    p.act(ACTF(nea[:], C.pvec[:, 4:5], AF.Exp), r=["pvec"], w=["nnea"])
    p.dve(TS(nea[:], nea[:], -1.0, None, ALU.mult), r=["nnea"], w=["nnea"])
    NSET = 2
    SETS = []
    for st_ in range(NSET):
        d_ = dict(
            sc=sb("nsc%d" % st_, [128, 16], F32), gtri=sb("ngtri%d" % st_, [128, 128], F32),
            bdiag=sb("nbdiag%d" % st_, [128, 128], F32), exT=sb("nexT%d" % st_, [128, 128], F32),
            exN=sb("nexN%d" % st_, [128, 128], F32), DBT=sb("nDBT%d" % st_, [128, 128], F32),
            decI=sb("ndecI%d" % st_, [128, 128], F32), eGB=sb("neGB%d" % st_, [64, 128], F32),
            scT=sb("nscT%d" % st_, [128, 128], F32),
            Xb=[sb("nX%d_%d" % (st_, i), [128, 128], F32) for i in range(2)],
            Yb=[sb("nY%d_%d" % (st_, i), [128, 128], F32) for i in range(2)],
            Rb=[sb("nR%d_%d" % (st_, i), [128, 128], F32) for i in range(2)],
            kup=sb("nkup%d" % st_, [128, 64], F32), bS=4 + 2 * st_, bG=5 + 2 * st_, c0=128 + 8 * st_, id=st_)
        SETS.append(d_)
    qin = sb("nqin", [64, 128], F32)
    wT = sb("nwT", [64, 128], F32)
    vnew = sb("nvnew", [128, 64], F32)
    tri_le, tri_lt, tri_gt = C.tri[:, 0:128], C.tri[:, 128:256], C.tri[:, 256:384]
    A_, B_, BETA, G_, GC, NGC, EG, ED_, BG, NB_ = range(10)

    for T in range(NT):
        hT, htok = front_end(C, T)
        for s_, dst in ((0, qT), (1, kT), (2, vT)):
            ps = project(C, wb, hT, htok, 64 * s_, 64, s_)
            xb = xbuf[s_]
            p.act(lambda e, xb=xb, ps=ps: e.copy(out=xb[:, 3:515], in_=ps[0:64, :]), r=[("ps", s_)], w=[("nxb", s_)])
            p.dve(TS(y[:], xb[:, 3:515], cw[:, 4 * s_ + 3:4 * s_ + 4], None, ALU.mult), r=[("nxb", s_), "ncw"], w=["ny"])
            for tap in (2, 1, 0):
                p.dve(STT(y[:], xb[:, tap:tap + 512], cw[:, 4 * s_ + tap:4 * s_ + tap + 1], y[:], ALU.mult, ALU.add),
                      r=[("nxb", s_), "ncw", "ny"], w=["ny"])
            p.dve(CP(xb[:, 0:3], xb[:, 512:515]), r=[("nxb", s_)], w=[("nxb", s_)])
            p.act(ACTF(ee[:], y[:], AF.Exp, scale=-1.0), r=["ny"], w=["nee"])
            p.dve(TS(ee[:], ee[:], 1.0, None, ALU.add), r=["nee"], w=["nee"])
            p.dve(RCP(ee[:], ee[:]), r=["nee"], w=["nee"])
            if s_ == 2:
                p.dve(TT(vT[:], y[:], ee[:], ALU.mult), r=["ny", "nee"], w=["nvT"])
            else:
                p.dve(TT(ys[:], y[:], ee[:], ALU.mult), r=["ny", "nee"], w=["nys"])
                col_rstd(C, ys[:], ["nys"], 64, 1.0, s_, rs[:], "nrs")
                p.dve(STT(dst[:], ys[:], 0.125 if s_ == 0 else 1.0, rs[:], ALU.mult, ALU.mult),
                      r=["nys", "nrs"], w=["nqT" if s_ == 0 else "nkT"])
        ps = project(C, wb, hT, htok, 192, 64, 3)
        p.act(lambda e, ps=ps: e.copy(out=rf[:], in_=ps[0:64, :]), r=[("ps", 3)], w=["nrf"])

        def prep_ops(ci, S_):
            cs = slice(ci * 128, (ci + 1) * 128)
            i_ = S_["id"]
            sc = S_["sc"]
            COL = lambda k: sc[:, k:k + 1]
            tS, tG = ("ps", S_["bS"]), ("ps", S_["bG"])
            PSs, PSg = PS[S_["bS"]], PS[S_["bG"]]
            c0 = S_["c0"]
            nsc = ("nsc", i_)
            ops = []
            A = ops.append

            def ab():
                for ch in range(8):
                    p.pe(MM(PS[2][:, c0:c0 + 2], hT[:, ch, cs], wb[:, ch, 256:258], ch == 0, ch == 7), r=[htok, "wb"],
                         w=[("ps", 2)])
            A(ab)
            A(lambda: p.act(ACTF(COL(B_), PS[2][:, c0 + 1:c0 + 2], AF.Exp, scale=-1.0), r=[("ps", 2)], w=[nsc]))
            A(lambda: p.dve(TS(COL(B_), COL(B_), 1.0, None, ALU.add), r=[nsc], w=[nsc]))
            A(lambda: p.dve(RCP(COL(BETA), COL(B_)), r=[nsc], w=[nsc]))
            A(lambda: p.act(ACTF(COL(A_), PS[2][:, c0:c0 + 1], AF.Exp, bias=C.pvec[:, 5:6]), r=[("ps", 2), "pvec"], w=[nsc]))
            A(lambda: p.act(ACTF(COL(A_), COL(A_), AF.Ln, bias=1.0), r=[nsc], w=[nsc]))
            A(lambda: p.dve(TT(COL(G_), COL(A_), nea[:], ALU.mult), r=[nsc, "nnea"], w=[nsc]))
            A(lambda: p.dve(TS(COL(NB_), COL(BETA), -1.0, None, ALU.mult), r=[nsc], w=[nsc]))
            A(lambda: p.pe(MM(PS[2][:, c0 + 2:c0 + 3], tri_le, COL(G_)), r=["tri", nsc], w=[("ps", 2)]))
            A(lambda: p.pe(MM(PS[2][:, c0 + 3:c0 + 4], tri_gt, COL(G_)), r=["tri", nsc], w=[("ps", 2)]))
            A(lambda: p.dve(TS(S_["gtri"][:], tri_le, COL(G_), None, ALU.mult), r=["tri", nsc], w=[("ngtri", i_)]))
            A(lambda: p.dve(TS(S_["bdiag"][:], C.identf[:, :], COL(BETA), None, ALU.mult), r=["identf", nsc],
                            w=[("nbdiag", i_)]))
            A(lambda: p.pe(MM(PSg[:, 0:128], C.onesf[:, :], S_["gtri"][:, :]), r=["onesf", ("ngtri", i_)], w=[tG]))
            A(lambda: p.pe(MM(PSg[:, 128:256], C.onesf[:, :], S_["bdiag"][:, :]), r=["onesf", ("nbdiag", i_)], w=[tG]))
            A(lambda: p.dve(CP(COL(GC), PS[2][:, c0 + 2:c0 + 3]), r=[("ps", 2)], w=[nsc]))
            A(lambda: p.act(ACTF(COL(EG), PS[2][:, c0 + 2:c0 + 3], AF.Exp), r=[("ps", 2)], w=[nsc]))
            A(lambda: p.act(ACTF(COL(ED_), PS[2][:, c0 + 3:c0 + 4], AF.Exp), r=[("ps", 2)], w=[nsc]))
            A(lambda: p.dve(TT(COL(BG), COL(BETA), COL(EG), ALU.mult), r=[nsc], w=[nsc]))
            exT, exN, DBT, decI, eGB, scT = S_["exT"], S_["exN"], S_["DBT"], S_["decI"], S_["eGB"], S_["scT"]
            A(lambda: p.dve(TS(exT[:], PSg[:, 0:128], COL(GC), 0.0, ALU.subtract, ALU.min), r=[tG, nsc], w=[("nexT", i_)]))
            A(lambda: p.dve(TS(exN[:], PSg[:, 0:128], COL(GC), 0.0, ALU.subtract, ALU.max), r=[tG, nsc], w=[("nexN", i_)]))
            A(lambda: p.act(ACTF(exT[:], exT[:], AF.Exp), r=[("nexT", i_)], w=[("nexT", i_)]))
            A(lambda: p.act(ACTF(exN[:], exN[:], AF.Exp, scale=-1.0), r=[("nexN", i_)], w=[("nexN", i_)]))
            A(lambda: p.dve(TT(decI[:], exT[:], tri_le, ALU.mult), r=[("nexT", i_), "tri"], w=[("ndecI", i_)]))
            A(lambda: p.dve(TT(DBT[:], exT[:], tri_lt, ALU.mult), r=[("nexT", i_), "tri"], w=[("nDBT", i_)]))
            A(lambda: p.dve(TT(DBT[:], DBT[:], PSg[:, 128:256], ALU.mult), r=[("nDBT", i_), tG], w=[("nDBT", i_)]))
            A(lambda: p.dve(TT(exN[:], exN[:], tri_gt, ALU.mult), r=[("nexN", i_), "tri"], w=[("nexN", i_)]))
            A(lambda: p.act(ACTF(eGB[:], PSg[0:64, 0:128], AF.Exp), r=[tG], w=[("neGB", i_)]))
            A(lambda: p.pe(MM(PSg[:, 256:384], kT[:, cs], kT[:, cs]), r=["nkT"], w=[tG]))
            A(lambda: p.pe(MM(PSg[:, 384:512], kT[:, cs], qT[:, cs]), r=["nkT", "nqT"], w=[tG]))
            Xb, Yb, Rb, kup = S_["Xb"], S_["Yb"], S_["Rb"], S_["kup"]
            A(lambda: p.dve(STT(Yb[0][:], PSg[:, 256:384], -1.0, DBT[:], ALU.mult, ALU.mult), r=[tG, ("nDBT", i_)],
                            w=[("nY", i_, 0)]))
            A(lambda: p.dve(STT(Xb[0][:], PSg[:, 256:384], COL(NB_), exN[:], ALU.mult, ALU.mult),
                            r=[tG, nsc, ("nexN", i_)], w=[("nX", i_, 0)]))
            A(lambda: p.dve(TT(scT[:], PSg[:, 384:512], decI[:], ALU.mult), r=[tG, ("ndecI", i_)], w=[("nscT", i_)]))
            A(lambda: p.pe(MM(PSs[:, 384:448], kT[:, cs], C.identf[0:64, 0:64]), r=["nkT", "identf"], w=[tS]))
            A(lambda: p.pe(MM(PSs[:, 448:512], vT[:, cs], C.identf[0:64, 0:64]), r=["nvT", "identf"], w=[tS]))
            A(lambda: p.dve(TS(Rb[0][:, 0:64], PSs[:, 448:512], COL(BETA), None, ALU.mult), r=[tS, nsc], w=[("nR", i_, 0)]))
            A(lambda: p.dve(TS(Rb[0][:, 64:128], PSs[:, 384:448], COL(BG), None, ALU.mult), r=[tS, nsc], w=[("nR", i_, 0)]))
            A(lambda: p.dve(TS(kup[:], PSs[:, 384:448], COL(ED_), None, ALU.mult), r=[tS, nsc], w=[("nkup", i_)]))
            for lv in range(7):
                a, b = lv % 2, (lv + 1) % 2
                A(lambda a=a: p.pe(MM(PSs[:, 0:128], Yb[a][:, :], Rb[a][:, :]), r=[("nY", i_, a), ("nR", i_, a)], w=[tS]))
                if lv < 6:
                    A(lambda a=a: p.pe(MM(PSs[:, 128:256], Yb[a][:, :], Xb[a][:, :]), r=[("nY", i_, a), ("nX", i_, a)], w=[tS]))
                    A(lambda a=a: p.pe(MM(PSs[:, 256:384], Xb[a][:, :], Yb[a][:, :]), r=[("nY", i_, a), ("nX", i_, a)], w=[tS]))
                A(lambda a=a, b=b: p.dve(TT(Rb[b][:], Rb[a][:], PSs[:, 0:128], ALU.add), r=[("nR", i_, a), tS], w=[("nR", i_, b)]))
                if lv < 6:
                    A(lambda b=b: p.act(lambda e: e.copy(out=Xb[b][:], in_=PSs[:, 128:256]), r=[tS], w=[("nX", i_, b)]))
                    A(lambda b=b: p.act(lambda e: e.copy(out=Yb[b][:], in_=PSs[:, 256:384]), r=[tS], w=[("nY", i_, b)]))
            return ops

        def scan_ops(ci, S_):
            cs = slice(ci * 128, (ci + 1) * 128)
            i_ = S_["id"]
            Rf, eGB, scT, kup = S_["Rb"][1], S_["eGB"], S_["scT"], S_["kup"]
            p.pe(MM(PS[1][0:64, 128:256], Rf[:, 64:128], C.identf[:, :]), r=[("nR", i_, 1), "identf"], w=[("ps", 1)])
            p.act(lambda e: e.copy(out=wT[:], in_=PS[1][0:64, 128:256]), r=[("ps", 1)], w=["nwT"])
            p.pe(MM(PS[1][:, 64:128], wT[:, :], state[:, :]), r=["nwT", "nstate"], w=[("ps", 1)])
            p.dve(TT(vnew[:], Rf[:, 0:64], PS[1][:, 64:128], ALU.subtract), r=[("nR", i_, 1), ("ps", 1)], w=["nvnew"])
            p.dve(TT(qin[:], qT[:, cs], eGB[:], ALU.mult), r=["nqT", ("neGB", i_)], w=["nqin"])
            p.pe(MM(PS[0][0:64, cs], state[:, :], qin[:, :], True, False), r=["nstate", "nqin"], w=[("ps", 0)])
            p.pe(MM(PS[0][0:64, cs], vnew[:, :], scT[:, :], False, True), r=["nvnew", ("nscT", i_)], w=[("ps", 0)])
            p.pe(MM(PS[1][0:64, 0:64], kup[:, :], vnew[:, :]), r=[("nkup", i_), "nvnew"], w=[("ps", 1)])
            p.dve(STT(state[:], state[:], eGB[:, 127:128], PS[1][0:64, 0:64], ALU.mult, ALU.add),
                  r=["nstate", ("neGB", i_), ("ps", 1)], w=["nstate"])

        for pair in range(2):
            la = prep_ops(2 * pair, SETS[0])
            lb = prep_ops(2 * pair + 1, SETS[1])
            for k in range(max(len(la), len(lb))):
                if k < len(la):
                    la[k]()
                if k < len(lb):
                    lb[k]()
            scan_ops(2 * pair, SETS[0])
            scan_ops(2 * pair + 1, SETS[1])
        gated_norm(C, PS[0][0:64, :], ("ps", 0), rf[:], "nrf", C.pvec[0:64, 3:4], "o_dn", T, 3)


def build_C(TC):
    nc = bass.Bass("TRN2", target_bir_lowering=False)
    dr = {}
    dr["xT"] = nc.dram_tensor("xT", [D_MODEL, TC], F32, kind="ExternalInput").ap()
    dr["oT"] = nc.dram_tensor("oT", [1024, TC], F32, kind="ExternalInput").ap()
    dr["gmix"] = nc.dram_tensor("gmix", [128, 8], F32, kind="ExternalInput").ap()
    dr["gmlp"] = nc.dram_tensor("gmlp", [128, 8], F32, kind="ExternalInput").ap()
    dr["w_gate"] = nc.dram_tensor("w_gate", [1024, 4096], F32, kind="ExternalInput").ap()
    dr["w_branch"] = nc.dram_tensor("w_branch", [1024, 1024], F32, kind="ExternalInput").ap()
    dr["w_out"] = nc.dram_tensor("w_out", [1024, 1024], F32, kind="ExternalInput").ap()
    dr["w_mlp_in"] = nc.dram_tensor("w_mlp_in", [1024, 4096], F32, kind="ExternalInput").ap()
    dr["w_mlp_out"] = nc.dram_tensor("w_mlp_out", [4096, 1024], F32, kind="ExternalInput").ap()
    dr["ones"] = nc.dram_tensor("c_ones", [128, 128], F32, kind="ExternalInput").ap()
    dr["x1"] = nc.dram_tensor("x1_scratch", [D_MODEL, TC], F32, kind="Internal").ap()
    dr["xo"] = nc.dram_tensor("xoT", [D_MODEL, TC], F32, kind="ExternalOutput").ap()

    def fm(ap):
        return ap.rearrange("(c p) s -> p c s", p=128)

    with ExitStack() as es:
        p = Prog(nc)
        PS = [es.enter_context(nc.psum_tensor("ps%d" % k, [128, 512], F32)) for k in range(8)]

        def sb0(name, shape, dt):
            return es.enter_context(nc.sbuf_tensor("s_" + name, shape, dt))

        ones = sb0("ones", [128, 128], BF16)
        gmix = sb0("gmix", [128, 8], F32)
        gmlp = sb0("gmlp", [128, 8], F32)
        p.dma("pool", "cst", lambda e: e.dma_start(out=ones[:], in_=dr["ones"][:, :]), w=["ones"])
        p.dma("sp", "cst2", lambda e: e.dma_start(out=gmix[:], in_=dr["gmix"][:, :]), w=["gmix"])
        p.dma("sp", "cst2", lambda e: e.dma_start(out=gmlp[:], in_=dr["gmlp"][:, :]), w=["gmlp"])

        def norm_tile(xt, xtok, n, gcol, sq, rt, rstd, hT, htok):
            p.act(ACTF(sq[:], xt[:], AF.Square), r=[xtok], w=["sq"])
            for ch in range(8):
                p.pe(MM(PS[7][:, 0:n], ones[:, :], sq[:, ch, :], ch == 0, ch == 7), r=["sq", "ones"], w=[("ps", 7)])
            p.act(ACTF(rt[:], PS[7][:, 0:n], AF.Ln, scale=1.0 / D_MODEL, bias=EPS), r=[("ps", 7)], w=["rt"])
            p.act(ACTF(rstd[:], rt[:], AF.Exp, scale=-0.5), r=["rt"], w=["rstd"])
            for ch in range(8):
                p.dve(STT(hT[:, ch, :], xt[:, ch, :], gcol[:, ch:ch + 1], rstd[:], ALU.mult, ALU.mult),
                      r=[xtok, "rstd", "gmix", "gmlp"], w=[htok])

        with ExitStack() as es2:
            def sb(name, shape, dt):
                return es2.enter_context(nc.sbuf_tensor("s_" + name, shape, dt))
            Wg = sb("Wg", [128, 8, 4096], BF16)
            Wb = sb("Wb", [128, 8, 1024], BF16)
            Wo = sb("Wo", [128, 8, 1024], BF16)
            for kc in range(8):
                p.dma("pool", "wg", lambda e, kc=kc: e.dma_start(out=Wg[:, kc, :], in_=dr["w_gate"][kc * 128:(kc + 1) * 128, :]),
                      w=["Wg"])
            p.dma("pool", "wb", lambda e: e.dma_start(out=Wb[:], in_=fm(dr["w_branch"])), w=["Wb"])
            p.dma("pool", "wb", lambda e: e.dma_start(out=Wo[:], in_=fm(dr["w_out"])), w=["Wo"])
            xt = sb("xt", [128, 8, 512], F32)
            ob = sb("ob", [128, 8, 512], BF16)
            sq = sb("sq", [128, 8, 512], BF16)
            rt = sb("rt", [128, 512], F32)
            rstd = sb("rstd", [128, 512], F32)
            hT = sb("hT", [128, 8, 512], BF16)
            merged = sb("merged", [128, 8, 512], BF16)
            gs = [sb("gs%d" % i, [128, 512], F32) for i in range(2)]
            macc = sb("macc", [128, 512], F32)
            mtmp = sb("mtmp", [128, 512], F32)
            for t in range(TC // 512):
                ts_ = slice(t * 512, (t + 1) * 512)
                p.dma("sp", "cx", lambda e, ts_=ts_: e.dma_start(out=xt[:], in_=fm(dr["xT"])[:, :, ts_]), w=["xt"])
                p.dma("pool", "co", lambda e, ts_=ts_: e.dma_start(out=ob[:], in_=fm(dr["oT"])[:, :, ts_]), w=["ob"])
                norm_tile(xt, "xt", 512, gmix, sq, rt, rstd, hT, "hT")
                k = 0
                for og in range(8):
                    osl = slice(og * 128, (og + 1) * 128)
                    for n in range(4):
                        a, b = k % 3, 3 + (k % 3)
                        k += 1
                        for kc in range(8):
                            p.pe(MM(PS[a][:, :], Wg[:, kc, n * 1024 + og * 128:n * 1024 + (og + 1) * 128], hT[:, kc, :],
                                    kc == 0, kc == 7), r=["Wg", "hT"], w=[("ps", a)])
                        g_ = gs[n % 2]
                        p.act(ACTF(g_[:], PS[a][:, :], AF.Sigmoid), r=[("ps", a)], w=[("gs", n % 2)])
                        for kc in range(2):
                            p.pe(MM(PS[b][:, :], Wb[:, n * 2 + kc, osl], ob[:, n * 2 + kc, :], kc == 0, kc == 1),
                                 r=["Wb", "ob"], w=[("ps", b)])
                        if n == 0:
                            p.dve(TT(macc[:], PS[b][:, :], g_[:], ALU.mult), r=[("ps", b), ("gs", n % 2)], w=["macc"])
                        else:
                            p.dve(TT(mtmp[:], PS[b][:, :], g_[:], ALU.mult), r=[("ps", b), ("gs", n % 2)], w=["mtmp"])
                            if n < 3:
                                p.dve(TT(macc[:], macc[:], mtmp[:], ALU.add), r=["macc", "mtmp"], w=["macc"])
                            else:
                                p.dve(TT(merged[:, og, :], macc[:], mtmp[:], ALU.add), r=["macc", "mtmp"], w=["merged"])
                for og in range(8):
                    osl = slice(og * 128, (og + 1) * 128)
                    for kc in range(8):
                        p.pe(MM(PS[6][:, :], Wo[:, kc, osl], merged[:, kc, :], kc == 0, kc == 7), r=["Wo", "merged"],
                             w=[("ps", 6)])
                    p.dve(TT(xt[:, og, :], xt[:, og, :], PS[6][:, :], ALU.add), r=["xt", ("ps", 6)], w=["xt"])
                p.dma("pool", "c1o", lambda e, ts_=ts_: e.dma_start(out=fm(dr["x1"])[:, :, ts_], in_=xt[:]), r=["xt"],
                      w=["x1dram"])
        p.barrier()

        with ExitStack() as es2:
            def sb(name, shape, dt):
                return es2.enter_context(nc.sbuf_tensor("s2_" + name, shape, dt))
            W1 = sb("W1", [128, 8, 4096], BF16)
            W2 = sb("W2", [128, 32, 1024], BF16)
            for kc in range(8):
                p.dma("pool", "wg", lambda e, kc=kc: e.dma_start(out=W1[:, kc, :], in_=dr["w_mlp_in"][kc * 128:(kc + 1) * 128, :]),
                      w=["W1"])
            for q4 in range(4):
                p.dma("pool", "wb", lambda e, q4=q4: e.dma_start(
                    out=W2[:, q4 * 8:(q4 + 1) * 8, :], in_=fm(dr["w_mlp_out"][q4 * 1024:(q4 + 1) * 1024, :])), w=["W2"])
            NB = 256
            xt_2 = sb("xt", [128, 8, NB], F32)
            sq_2 = sb("sq", [128, 8, NB], BF16)
            rt_2 = sb("rt", [128, NB], F32)
            rstd_2 = sb("rstd", [128, NB], F32)
            hT_2 = sb("hT", [128, 8, NB], BF16)
            aT = sb("aT", [128, 32, NB], BF16)
            rb = [sb("rb%d" % i, [128, NB], F32) for i in range(2)]
            for t in range(TC // NB):
                ts_ = slice(t * NB, (t + 1) * NB)
                p.dma("sp", "cx", lambda e, ts_=ts_: e.dma_start(out=xt_2[:], in_=fm(dr["x1"])[:, :, ts_]), r=["x1dram"], w=["xt2"])
                norm_tile(xt_2, "xt2", NB, gmlp, sq_2, rt_2, rstd_2, hT_2, "hT2")
                for fg in range(32):
                    a = fg % 3
                    for kc in range(8):
                        p.pe(MM(PS[a][:, 0:NB], W1[:, kc, fg * 128:(fg + 1) * 128], hT_2[:, kc, :], kc == 0, kc == 7),
                             r=["W1", "hT2"], w=[("ps", a)])
                    r_ = rb[fg % 2]
                    p.act(ACTF(r_[:], PS[a][:, 0:NB], AF.Relu), r=[("ps", a)], w=[("rb", fg % 2)])
                    p.dve(TT(aT[:, fg, :], r_[:], r_[:], ALU.mult), r=[("rb", fg % 2)], w=["aT"])
                for og in range(8):
                    b = 3 + og % 2
                    for fc in range(32):
                        p.pe(MM(PS[b][:, 0:NB], W2[:, fc, og * 128:(og + 1) * 128], aT[:, fc, :], fc == 0, fc == 31),
                             r=["W2", "aT"], w=[("ps", b)])
                    p.dve(TT(xt_2[:, og, :], xt_2[:, og, :], PS[b][:, 0:NB], ALU.add), r=["xt2", ("ps", b)], w=["xt2"])
                p.dma("pool", "c2o", lambda e, ts_=ts_: e.dma_start(out=fm(dr["xo"])[:, :, ts_], in_=xt_2[:]), r=["xt2"],
                      w=["xodram"])
        p.barrier()
        p.emit(es)
        print("C program ops:", p.stats)
    return nc


def prep_C_weights(P, l):
    m = {}
    m["gmix"] = np.ascontiguousarray(P["norm_mix_g"][l].reshape(8, 128).T)
    m["gmlp"] = np.ascontiguousarray(P["norm_mlp_g"][l].reshape(8, 128).T)
    m["w_gate"] = np.ascontiguousarray(P["w_in"][l][:, IN_OFF[15]:])
    m["w_branch"] = np.ascontiguousarray(P["w_branch"][l].reshape(1024, 1024))
    m["w_out"] = P["w_out"][l]
    m["w_mlp_in"] = P["w_mlp_in"][l]
    m["w_mlp_out"] = P["w_mlp_out"][l]
    m["c_ones"] = np.ones((128, 128), np.float32)
    return m


SEQ = 16384
BATCH = 2
DEPTH = 4
RG = [[0, 1, 2, 3], [4, 5, 6, 7]]
B_WEIGHTS = (("w_sb", [D_MODEL, 192]), ("w_dil", [D_MODEL, 192]), ("w_gla", [D_MODEL, 208]), ("w_dn", [D_MODEL, 258]),
             ("pvec", [128, 8]), ("wlr2", [32, 32]), ("dnconv", [64, 12]), ("gmix", [128, 8]))
C_WEIGHTS = (("gmlp", [128, 8]), ("w_gate", [1024, 4096]), ("w_branch", [1024, 1024]), ("w_out", [1024, 1024]),
             ("w_mlp_in", [1024, 4096]), ("w_mlp_out", [4096, 1024]))


def fm(ap):
    return ap.rearrange("(c p) s -> p c s", p=128)


def build_fused(S=SEQ, depth=DEPTH, debug=False):
    TC = S // 4
    nc = bass.Bass("TRN2", target_bir_lowering=False, num_devices=8)
    dr = {}
    ext = {}
    ext["xT"] = nc.dram_tensor("xT", [D_MODEL, TC], F32, kind="ExternalInput").ap()
    for name, shp in B_WEIGHTS + C_WEIGHTS:
        ext[name] = nc.dram_tensor(name, [depth] + shp, F32, kind="ExternalInput").ap()
    for name, shp in const_shapes(S).items():
        dr[name] = nc.dram_tensor("c_" + name, shp, F32, kind="ExternalInput").ap()
    xo = nc.dram_tensor("xoT", [D_MODEL, TC], F32, kind="ExternalOutput").ap()
    HCH = max(1, (D_MODEL * TC * 2) // (1 << 20))
    HW = TC // HCH
    OCH = max(1, (64 * S * 2) // (1 << 20))
    CW = TC // OCH
    hsrc = [[nc.dram_tensor("hsrc%d_%d" % (i, k), [D_MODEL, HW], BF16, kind="Internal").ap() for k in range(HCH)]
            for i in range(2)]
    hall = [[nc.dram_tensor("hall%d_%d" % (i, k), [4 * D_MODEL, HW], BF16, kind="Internal").ap() for k in range(HCH)]
            for i in range(2)]
    osrc = [[[nc.dram_tensor("osrc%d_%d_%d" % (i, n, k), [64, 4 * CW], BF16, kind="Internal").ap() for k in range(OCH)]
             for n in range(4)] for i in range(2)]
    oall = [[[nc.dram_tensor("oall%d_%d_%d" % (i, n, k), [256, 4 * CW], BF16, kind="Internal").ap() for k in range(OCH)]
             for n in range(4)] for i in range(2)]
    x1d = nc.dram_tensor("x1_scratch", [D_MODEL, TC], F32, kind="Internal").ap()
    xres = nc.dram_tensor("xres", [D_MODEL, TC], F32, kind="Internal").ap()
    if debug:
        dbg_hall = nc.dram_tensor("dbg_hall", [4 * D_MODEL, HW], BF16, kind="ExternalOutput").ap()
        dbg_oall = [nc.dram_tensor("dbg_oall%d" % n, [256, 4 * CW], BF16, kind="ExternalOutput").ap() for n in range(4)]
        dbg_x1 = nc.dram_tensor("dbg_x1", [D_MODEL, TC], F32, kind="ExternalOutput").ap()

    with ExitStack() as es:
        p = Prog(nc)
        C = Ctx()
        C.nc, C.p, C.S, C.NT, C.dr = nc, p, S, S // 512, dr
        C.fused = True
        C.HW, C.CW = HW, CW
        C.rank = nc.partition_id() % 4

        def sb(name, shape, dt):
            return es.enter_context(nc.sbuf_tensor("s_" + name, shape, dt))

        C.PS = [es.enter_context(nc.psum_tensor("ps%d" % k, [128, 512], F32)) for k in range(8)]
        PS = C.PS
        C.ident = sb("ident", [128, 128], BF16)
        C.ones = sb("ones", [128, 128], BF16)
        C.identf = sb("identf", [128, 128], F32)
        C.onesf = sb("onesf", [128, 128], F32)
        C.tri = sb("tri", [128, 512], F32)
        C.tri_bf = sb("tri_bf", [128, 512], BF16)
        p.dma("pool", "cst", lambda e: e.dma_start(out=C.tri_bf[:], in_=dr["tri"][:, :]), w=["tri_bf"])
        C.gmix = sb("gmix", [128, 8], F32)
        C.gmlp = sb("gmlp", [128, 8], F32)
        C.pvec = sb("pvec", [128, 8], F32)
        p.dma("pool", "cst", lambda e: e.dma_start(out=C.ident[:], in_=dr["ident"][:, :]), w=["ident"])
        p.dma("pool", "cst", lambda e: e.dma_start(out=C.ones[:], in_=dr["ones"][:, :]), w=["ones"])
        p.dma("sp", "cst2", lambda e: e.dma_start(out=C.identf[:], in_=dr["ident"][:, :]), w=["identf"])
        p.dma("sp", "cst2", lambda e: e.dma_start(out=C.onesf[:], in_=dr["ones"][:, :]), w=["onesf"])
        p.dma("sp", "cst2", lambda e: e.dma_start(out=C.tri[:], in_=dr["tri"][:, :]), w=["tri"])

        def norm_tile(xt, xtok, n, gcol, sq, rt, rstd, hT, htok):
            p.act(ACTF(sq[:], xt[:], AF.Square), r=[xtok], w=["sq"])
            for ch in range(8):
                p.pe(MM(PS[7][:, 0:n], C.ones[:, :], sq[:, ch, :], ch == 0, ch == 7), r=["sq", "ones"], w=[("ps", 7)])
            p.act(ACTF(rt[:], PS[7][:, 0:n], AF.Ln, scale=1.0 / D_MODEL, bias=EPS), r=[("ps", 7)], w=["rt"])
            p.act(ACTF(rstd[:], rt[:], AF.Exp, scale=-0.5), r=["rt"], w=["rstd"])
            for ch in range(8):
                if gcol is None:
                    p.dve(TT(hT[:, ch, :], xt[:, ch, :], rstd[:], ALU.mult), r=[xtok, "rstd"], w=[htok])
                else:
                    p.dve(STT(hT[:, ch, :], xt[:, ch, :], gcol[:, ch:ch + 1], rstd[:], ALU.mult, ALU.mult),
                          r=[xtok, "rstd", "gmix", "gmlp"], w=[htok])

        ncoll = [0]

        def cc_chan():
            ncoll[0] += 1
            return "cc%d" % (ncoll[0] % 4)

        def gather_h(par, k):
            p.coll(cc_chan(), lambda e: e.collective_compute("AllGather", ALU.bypass, replica_groups=RG,
                                                        ins=[hsrc[par][k][:, :]], outs=[hall[par][k][:, :]]),
                   r=[("hsrc", k)], w=[("hall", k)])

        with ExitStack() as es2:
            def sb2(name, shape, dt):
                return es2.enter_context(nc.sbuf_tensor("s0_" + name, shape, dt))
            xt0 = sb2("xt", [128, 8, 512], F32)
            sq0 = sb2("sq", [128, 8, 512], BF16)
            rt0 = sb2("rt", [128, 512], F32)
            rstd0 = sb2("rstd", [128, 512], F32)
            h0 = sb2("h", [128, 8, 512], BF16)
            for t in range(TC // 512):
                ts_ = slice(t * 512, (t + 1) * 512)
                p.dma("sp", "cx", lambda e, ts_=ts_: e.dma_start(out=xt0[:], in_=fm(ext["xT"])[:, :, ts_]), w=["xt"])
                norm_tile(xt0, "xt", 512, None, sq0, rt0, rstd0, h0, "h0")
                k_, col = divmod(t * 512, HW)
                p.dma("sp", "ch", lambda e, k_=k_, col=col: e.dma_start(out=fm(hsrc[0][k_])[:, :, col:col + 512], in_=h0[:]),
                      r=["h0"], w=[("hsrc", k_)])
                if (t * 512 + 512) % HW == 0:
                    gather_h(0, k_)
        if debug and depth == 1:
            p.dma("pool", "dbg", lambda e: e.dma_start(out=dbg_hall[:, :], in_=hall[0][0][:, :]), r=[("hall", 0)], w=["dbgh"])

        for l in range(depth):
            par = l % 2
            C.tag = "L%d" % l
            with ExitStack() as esB:
                def sbB(name, shape, dt):
                    return esB.enter_context(nc.sbuf_tensor("sB%d_%s" % (l, name), shape, dt))
                C.hT = [sbB("hT%d" % i, [128, 8, 512], BF16) for i in range(2)]
                C.cr_sq = sbB("cr_sq", [64, 512], F32)
                C.cr_l = sbB("cr_l", [64, 512], F32)
                C.gn_of = sbB("gn_of", [64, 512], F32)
                C.gn_rs = sbB("gn_rs", [64, 512], F32)
                C.gn_er = sbB("gn_er", [64, 512], F32)
                C.gn_sg = sbB("gn_sg", [64, 512], F32)
                C.gn_t1 = sbB("gn_t1", [64, 512], F32)
                C.gn_out = [sbB("gn_out%d" % i, [64, 512], BF16) for i in range(2)]
                C.hall = hall[par]
                for name, _ in B_WEIGHTS:
                    dr[name] = ext[name][l]
                C.osrc = {"o_sb": osrc[par][0], "o_gla": osrc[par][1], "o_dn": osrc[par][2], "o_dil": osrc[par][3]}
                p.dma("sp", "cst2", lambda e, l=l: e.dma_start(out=C.gmix[:], in_=ext["gmix"][l]), w=["gmix"])
                p.dma("sp", "cst2", lambda e, l=l: e.dma_start(out=C.pvec[:], in_=ext["pvec"][l]), w=["pvec"])
                for ph, fn, n, otok in (("sb", phase_sb, 0, "out_sb"), ("gla", phase_gla, 1, "gn_dram"),
                                        ("dn", phase_dn, 2, "gn_dram"), ("dil", phase_dil, 3, "od_dram")):
                    with ExitStack() as es2:
                        fn(C, es2)
                    p.barrier()
                    for k in range(OCH):
                        p.coll(cc_chan(), lambda e, n=n, par=par, k=k: e.collective_compute(
                            "AllGather", ALU.bypass, replica_groups=RG, ins=[osrc[par][n][k][:, :]],
                            outs=[oall[par][n][k][:, :]]), r=[otok], w=[("oall", n, k)])
            if debug and l == depth - 1:
                for n in range(4):
                    p.dma("pool", "dbg", lambda e, n=n, par=par: e.dma_start(out=dbg_oall[n][:, :], in_=oall[par][n][0][:, :]),
                          r=[("oall", n, 0)], w=["dbgo"])
            p.barrier(new_epoch=True)

            xin = ext["xT"] if l == 0 else xres
            xout = xo if l == depth - 1 else xres
            with ExitStack() as es2:
                def sb(name, shape, dt, l=l):
                    return es2.enter_context(nc.sbuf_tensor("sC%d_%s" % (l, name), shape, dt))
                Wg = sb("Wg", [128, 8, 4096], BF16)
                Wb = sb("Wb", [128, 8, 1024], BF16)
                Wo = sb("Wo", [128, 8, 1024], BF16)
                p.dma("sp", "cst2", lambda e, l=l: e.dma_start(out=C.gmlp[:], in_=ext["gmlp"][l]), w=["gmlp"])
                for kc in range(8):
                    p.dma("pool", "wg", lambda e, kc=kc, l=l, Wg=Wg: e.dma_start(
                        out=Wg[:, kc, :], in_=ext["w_gate"][l][kc * 128:(kc + 1) * 128, :]), w=["Wg"])
                p.dma("pool", "wb", lambda e, l=l, Wb=Wb: e.dma_start(out=Wb[:], in_=fm(ext["w_branch"][l])), w=["Wb"])
                p.dma("pool", "wb", lambda e, l=l, Wo=Wo: e.dma_start(out=Wo[:], in_=fm(ext["w_out"][l])), w=["Wo"])
                xt = sb("xt", [128, 8, 512], F32)
                ob = sb("ob", [128, 8, 512], BF16)
                sq = sb("sq", [128, 8, 512], BF16)
                rt = sb("rt", [128, 512], F32)
                rstd = sb("rstd", [128, 512], F32)
                hT = sb("hT", [128, 8, 512], BF16)
                merged = sb("merged", [128, 8, 512], BF16)
                gs = [sb("gs%d" % i, [128, 512], F32) for i in range(2)]
                macc = sb("macc", [128, 512], F32)
                mtmp = sb("mtmp", [128, 512], F32)
                for t in range(TC // 512):
                    ts_ = slice(t * 512, (t + 1) * 512)
                    p.dma("sp", "cx", lambda e, ts_=ts_, xt=xt, xin=xin: e.dma_start(out=xt[:], in_=fm(xin)[:, :, ts_]),
                          r=["xres"], w=["xt"])
                    k_, col = divmod(t * 512, CW)
                    for n in range(4):
                        src = oall[par][n][k_].rearrange("(kc p) s -> p kc s", p=128)
                        p.dma("sp", "co", lambda e, col=col, ob=ob, n=n, src=src: e.dma_start(
                            out=ob[:, 2 * n:2 * n + 2, :],
                            in_=src[:, :, bass.ds(C.rank * CW + col, 512)]), r=[("oall", n, k_)], w=["ob"])
                    norm_tile(xt, "xt", 512, C.gmix, sq, rt, rstd, hT, "hT")
                    k = 0
                    for og in range(8):
                        osl = slice(og * 128, (og + 1) * 128)
                        for n in range(4):
                            a, b = k % 3, 3 + (k % 3)
                            k += 1
                            for kc in range(8):
                                p.pe(MM(PS[a][:, :], Wg[:, kc, n * 1024 + og * 128:n * 1024 + (og + 1) * 128], hT[:, kc, :],
                                        kc == 0, kc == 7), r=["Wg", "hT"], w=[("ps", a)])
                            g_ = gs[n % 2]
                            p.act(ACTF(g_[:], PS[a][:, :], AF.Sigmoid), r=[("ps", a)], w=[("gs", n % 2)])
                            for kc in range(2):
                                p.pe(MM(PS[b][:, :], Wb[:, n * 2 + kc, osl], ob[:, n * 2 + kc, :], kc == 0, kc == 1),
                                     r=["Wb", "ob"], w=[("ps", b)])
                            if n == 0:
                                p.dve(TT(macc[:], PS[b][:, :], g_[:], ALU.mult), r=[("ps", b), ("gs", n % 2)], w=["macc"])
                            else:
                                p.dve(TT(mtmp[:], PS[b][:, :], g_[:], ALU.mult), r=[("ps", b), ("gs", n % 2)], w=["mtmp"])
                                if n < 3:
                                    p.dve(TT(macc[:], macc[:], mtmp[:], ALU.add), r=["macc", "mtmp"], w=["macc"])
                                else:
                                    p.dve(TT(merged[:, og, :], macc[:], mtmp[:], ALU.add), r=["macc", "mtmp"], w=["merged"])
                    for og in range(8):
                        osl = slice(og * 128, (og + 1) * 128)
                        for kc in range(8):
                            p.pe(MM(PS[6][:, :], Wo[:, kc, osl], merged[:, kc, :], kc == 0, kc == 7), r=["Wo", "merged"],
                                 w=[("ps", 6)])
                        p.dve(TT(xt[:, og, :], xt[:, og, :], PS[6][:, :], ALU.add), r=["xt", ("ps", 6)], w=["xt"])
                    p.dma("pool", "c1o", lambda e, ts_=ts_, xt=xt: e.dma_start(out=fm(x1d)[:, :, ts_], in_=xt[:]), r=["xt"],
                          w=["x1dram"])
            p.barrier()
            if debug and l == depth - 1:
                p.dma("pool", "dbg", lambda e: e.dma_start(out=dbg_x1[:, :], in_=x1d[:, :]), r=["x1dram"], w=["dbgx"])

            with ExitStack() as es2:
                def sb(name, shape, dt, l=l):
                    return es2.enter_context(nc.sbuf_tensor("sD%d_%s" % (l, name), shape, dt))
                W1 = sb("W1", [128, 8, 4096], BF16)
                W2 = sb("W2", [128, 32, 1024], BF16)
                for kc in range(8):
                    p.dma("pool", "wg", lambda e, kc=kc, l=l, W1=W1: e.dma_start(
                        out=W1[:, kc, :], in_=ext["w_mlp_in"][l][kc * 128:(kc + 1) * 128, :]), w=["W1"])
                for q4 in range(4):
                    p.dma("pool", "wb", lambda e, q4=q4, l=l, W2=W2: e.dma_start(
                        out=W2[:, q4 * 8:(q4 + 1) * 8, :], in_=fm(ext["w_mlp_out"][l][q4 * 1024:(q4 + 1) * 1024, :])), w=["W2"])
                NB = 256
                xt_2 = sb("xt", [128, 8, NB], F32)
                sq_2 = sb("sq", [128, 8, NB], BF16)
                rt_2 = sb("rt", [128, NB], F32)
                rstd_2 = sb("rstd", [128, NB], F32)
                hT_2 = sb("hT", [128, 8, NB], BF16)
                hn_2 = sb("hn", [128, 8, NB], BF16)
                aT = sb("aT", [128, 32, NB], BF16)
                rb = [sb("rb%d" % i, [128, NB], F32) for i in range(2)]
                for t in range(TC // NB):
                    ts_ = slice(t * NB, (t + 1) * NB)
                    p.dma("sp", "cx", lambda e, ts_=ts_, xt_2=xt_2: e.dma_start(out=xt_2[:], in_=fm(x1d)[:, :, ts_]),
                          r=["x1dram"], w=["xt2"])
                    norm_tile(xt_2, "xt2", NB, C.gmlp, sq_2, rt_2, rstd_2, hT_2, "hT2")
                    for fg in range(32):
                        a = fg % 3
                        for kc in range(8):
                            p.pe(MM(PS[a][:, 0:NB], W1[:, kc, fg * 128:(fg + 1) * 128], hT_2[:, kc, :], kc == 0, kc == 7),
                                 r=["W1", "hT2"], w=[("ps", a)])
                        r_ = rb[fg % 2]
                        p.act(ACTF(r_[:], PS[a][:, 0:NB], AF.Relu), r=[("ps", a)], w=[("rb", fg % 2)])
                        p.dve(TT(aT[:, fg, :], r_[:], r_[:], ALU.mult), r=[("rb", fg % 2)], w=["aT"])
                    for og in range(8):
                        b = 3 + og % 2
                        for fc in range(32):
                            p.pe(MM(PS[b][:, 0:NB], W2[:, fc, og * 128:(og + 1) * 128], aT[:, fc, :], fc == 0, fc == 31),
                                 r=["W2", "aT"], w=[("ps", b)])
                        p.dve(TT(xt_2[:, og, :], xt_2[:, og, :], PS[b][:, 0:NB], ALU.add), r=["xt2", ("ps", b)], w=["xt2"])
                    p.dma("pool", "c2o", lambda e, ts_=ts_, xt_2=xt_2, xout=xout: e.dma_start(out=fm(xout)[:, :, ts_], in_=xt_2[:]),
                          r=["xt2"], w=["xres"])
                    if l < depth - 1:
                        norm_tile(xt_2, "xt2", NB, None, sq_2, rt_2, rstd_2, hn_2, "hn2")
                        k_, col = divmod(t * NB, HW)
                        p.dma("sp", "ch", lambda e, k_=k_, col=col, hn_2=hn_2, par=par: e.dma_start(
                            out=fm(hsrc[1 - par][k_])[:, :, col:col + NB], in_=hn_2[:]), r=["hn2"], w=[("hsrc", k_)])
                        if (t * NB + NB) % HW == 0:
                            gather_h(1 - par, k_)
            p.barrier()
            if debug and l == depth - 2:
                p.dma("pool", "dbg", lambda e, par=par: e.dma_start(out=dbg_hall[:, :], in_=hall[1 - par][0][:, :]),
                      r=[("hall", 0)], w=["dbgh"])
            p.barrier(new_epoch=True)
        p.emit(es)
        print("fused program:", p.stats)
    return nc


_CACHE = {}


def kernel(**inputs):
    P = {k: np.asarray(v, dtype=np.float32) for k, v in inputs.items()}
    x = P["x"]
    S, B = SEQ, BATCH
    TC = S // 4
    if "F" not in _CACHE:
        _CACHE["F"] = build_fused()
    nc = _CACHE["F"]
    consts = {"c_" + k: v for k, v in make_consts(S).items()}
    xT = np.ascontiguousarray(x.reshape(B * S, D_MODEL).T)
    wB = {}
    for h in range(4):
        per_l = [prep_B_weights(P, l, h) for l in range(DEPTH)]
        wB[h] = {name: np.ascontiguousarray(np.stack([m[name] for m in per_l])) for name, _ in B_WEIGHTS}
    per_l = [prep_C_weights(P, l) for l in range(DEPTH)]
    wC = {name: np.ascontiguousarray(np.stack([m[name] for m in per_l])) for name, _ in C_WEIGHTS}
    in_maps = []
    for c in range(8):
        m = dict(wB[c % 4])
        m.update(wC)
        m.update(consts)
        m["xT"] = np.ascontiguousarray(xT[:, c * TC:(c + 1) * TC])
        in_maps.append(m)
    res = run_bass_kernel_spmd(nc, in_maps, core_ids=list(range(8))).results
    xoT = np.concatenate([res[c]["xoT"] for c in range(8)], axis=1)
    return np.ascontiguousarray(xoT.T).reshape(B, S, D_MODEL).astype(np.float32)
```
